# Optimizing a Trainium2 kernel written in Bass

```python
import math
import jax
import jax.numpy as jnp
from jax import lax
import numpy as np

D_MODEL = 2048
BATCH = 4
SEQ = 4096
DEPTH = 4

D_MIX = D_MODEL
S5_WIDTH = D_MIX // 4
S5_GROUP = 16
S5_GROUPS = S5_WIDTH // S5_GROUP
S5_STATE = 64
MLA_HEADS = 8
MLA_NOPE = 128
MLA_ROPE = 64
MLA_V = 128
MLA_Q_RANK = D_MODEL // 4
MLA_KV_RANK = D_MODEL // 8
MLA_WIDTH = MLA_HEADS * MLA_V
ROPE_THETA = 10000.0
Q_BLOCK = 128
MASK_VALUE = -1e30
HG_HEADS = 4
HG_DK = 128
HG_DV = (D_MIX - S5_WIDTH - MLA_WIDTH) // HG_HEADS
HG_WIDTH = HG_HEADS * HG_DV
HG_CHUNK = 64
D_FF = 5504
CONV_W = 3
EPS = 1e-6
IN_SIZES = (S5_WIDTH, MLA_Q_RANK, MLA_KV_RANK, MLA_ROPE,
            HG_HEADS * HG_DK, HG_HEADS * HG_DK, HG_WIDTH, HG_WIDTH)
IN_OFFSETS = tuple(int(v) for v in np.cumsum(IN_SIZES)[:-1])
D_IN = int(sum(IN_SIZES))

kernel_name = 'hybrid_s5_mla_hgrn2_block'


def rmsnorm(x, gain):
    xf = x.astype(jnp.float32)
    xf = xf * lax.rsqrt(jnp.mean(xf * xf, axis=-1, keepdims=True) + EPS)
    return (xf * gain.astype(jnp.float32)).astype(x.dtype)


def rope_tables(positions):
    inv_freq = 1.0 / (ROPE_THETA ** (jnp.arange(0, MLA_ROPE, 2, dtype=jnp.float32) / MLA_ROPE))
    ang = positions.astype(jnp.float32)[..., None] * inv_freq
    return jnp.cos(ang), jnp.sin(ang)


def apply_rope(x, cos, sin):
    half = x.shape[-1] // 2
    x1, x2 = x[..., :half], x[..., half:]
    cos = cos.astype(x.dtype)
    sin = sin.astype(x.dtype)
    return jnp.concatenate([x1 * cos - x2 * sin, x2 * cos + x1 * sin], axis=-1)


def s5_mixer(u, lam_re, lam_im, log_dt, b_re, b_im, c_re, c_im, d, w_glu):
    bsz, seq, _ = u.shape
    uf = u.astype(jnp.float32).reshape(bsz, seq, S5_GROUPS, S5_GROUP)
    lam = lax.complex(lam_re.astype(jnp.float32), lam_im.astype(jnp.float32))
    dt = jnp.exp(log_dt.astype(jnp.float32))[:, None]
    lam_bar = jnp.exp(lam * dt)
    b = lax.complex(b_re.astype(jnp.float32), b_im.astype(jnp.float32))
    b_bar = ((lam_bar - 1.0) / lam)[..., None] * b
    bu = jnp.einsum('gpc,bsgc->bsgp', b_bar, uf.astype(jnp.complex64))
    a = jnp.broadcast_to(lam_bar, bu.shape)

    def combine(left, right):
        a_l, b_l = left
        a_r, b_r = right
        return a_r * a_l, a_r * b_l + b_r

    _, h = lax.associative_scan(combine, (a, bu), axis=1)
    cm = lax.complex(c_re.astype(jnp.float32), c_im.astype(jnp.float32))
    y = (jnp.einsum('gcp,bsgp->bsgc', cm, h).real
         + d.astype(jnp.float32).reshape(S5_GROUPS, S5_GROUP) * uf)
    y = jax.nn.gelu(y.reshape(bsz, seq, S5_WIDTH))
    out = y * jax.nn.sigmoid(y @ w_glu.astype(jnp.float32))
    return out.astype(u.dtype)


def causal_attention_blocks(q_nope, q_rope, k_nope, k_rope, v):
    seq = q_nope.shape[1]
    scale = (MLA_NOPE + MLA_ROPE) ** -0.5
    outs = []
    for blk in range(seq // Q_BLOCK):
        q0, q1 = blk * Q_BLOCK, (blk + 1) * Q_BLOCK
        s = (jnp.einsum('bqhd,bkhd->bhqk', q_nope[:, q0:q1], k_nope[:, :q1])
             + jnp.einsum('bqhr,bkr->bhqk', q_rope[:, q0:q1], k_rope[:, :q1])).astype(jnp.float32) * scale
        causal = jnp.arange(q1)[None, :] <= jnp.arange(q0, q1)[:, None]
        p = jax.nn.softmax(jnp.where(causal, s, MASK_VALUE), axis=-1).astype(v.dtype)
        outs.append(jnp.einsum('bhqk,bkhd->bqhd', p, v[:, :q1]))
    return jnp.concatenate(outs, axis=1)


def mla_mixer(c_q, c_kv, k_rope_in, q_norm, w_uq, kv_norm, w_ukv, cos, sin):
    bsz, seq, _ = c_q.shape
    q = (rmsnorm(c_q, q_norm) @ w_uq).reshape(bsz, seq, MLA_HEADS, MLA_NOPE + MLA_ROPE)
    q_nope = q[..., :MLA_NOPE]
    q_rope = apply_rope(q[..., MLA_NOPE:], cos[:, :, None, :], sin[:, :, None, :])
    kv = (rmsnorm(c_kv, kv_norm) @ w_ukv).reshape(bsz, seq, MLA_HEADS, MLA_NOPE + MLA_V)
    k_nope, v = kv[..., :MLA_NOPE], kv[..., MLA_NOPE:]
    k_rope = apply_rope(k_rope_in, cos, sin)
    o = causal_attention_blocks(q_nope, q_rope, k_nope, k_rope, v)
    return o.reshape(bsz, seq, MLA_WIDTH)


def hgrn2_chunkwise(q, k, v, logf):
    bsz, seq, nh, dk = q.shape
    dv = v.shape[-1]
    n_chunks = seq // HG_CHUNK

    def to_chunks(t):
        return t.reshape(bsz, n_chunks, HG_CHUNK, nh, t.shape[-1]).transpose(1, 0, 3, 2, 4)

    mask = jnp.tril(jnp.ones((HG_CHUNK, HG_CHUNK), dtype=bool))[:, :, None]

    def step(state, inp):
        qc, kc, vc, gc = inp
        b = jnp.cumsum(gc, axis=2)
        o_inter = jnp.einsum('bhtk,bhkv->bhtv', qc * jnp.exp(b), state)
        diff = b[:, :, :, None, :] - b[:, :, None, :, :]
        decay = jnp.where(mask, jnp.exp(jnp.where(mask, diff, 0.0)), 0.0)
        attn = jnp.einsum('bhtk,bhsk,bhtsk->bhts', qc, kc, decay)
        o_intra = jnp.einsum('bhts,bhsv->bhtv', attn, vc)
        b_last = b[:, :, -1:, :]
        new_state = (jnp.exp(b_last[:, :, 0, :])[..., None] * state
                     + jnp.einsum('bhsk,bhsv->bhkv', kc * jnp.exp(b_last - b), vc))
        return new_state, o_inter + o_intra

    state0 = jnp.zeros((bsz, nh, dk, dv), jnp.float32)
    _, o = lax.scan(step, state0, (to_chunks(q), to_chunks(k), to_chunks(v), to_chunks(logf)))
    return o.transpose(1, 0, 3, 2, 4).reshape(bsz, seq, nh, dv)


def hgrn2_mixer(q_in, f_in, i_in, g_in, lb, out_norm):
    bsz, seq, _ = q_in.shape
    z = f_in.astype(jnp.float32)
    lb = lb.astype(jnp.float32)
    logf = jnp.log(lb + (1.0 - lb) * jax.nn.sigmoid(z))
    k = (1.0 - lb) * jax.nn.sigmoid(-z)
    q = jax.nn.silu(q_in.astype(jnp.float32))

    def heads(t):
        return t.reshape(bsz, seq, HG_HEADS, t.shape[-1] // HG_HEADS)

    o = hgrn2_chunkwise(heads(q), heads(k), heads(i_in.astype(jnp.float32)), heads(logf))
    o = rmsnorm(o, out_norm) * jax.nn.silu(heads(g_in.astype(jnp.float32)))
    return o.reshape(bsz, seq, HG_WIDTH).astype(q_in.dtype)


def causal_dwconv(u, w, b):
    seq = u.shape[1]
    taps = w.shape[0]
    up = jnp.pad(u, ((0, 0), (taps - 1, 0), (0, 0)))
    out = b
    for j in range(taps):
        out = out + up[:, j:j + seq] * w[j]
    return out


def conv_geglu_ffn(h, w_up, conv_w, conv_b, w_down):
    u = causal_dwconv(h @ w_up, conv_w, conv_b)
    gate, val = jnp.split(u, 2, axis=-1)
    return (jax.nn.gelu(gate, approximate=True) * val) @ w_down


def setup_inputs(seed: int = 0) -> dict:
    key = jax.random.key(seed)
    k = jax.random.split(key, 32)
    L = DEPTH

    def nrm(i, shape, scale=1.0):
        return scale * jax.random.normal(k[i], shape, jnp.float32)

    def gain(i, shape):
        return 1.0 + nrm(i, shape, 0.1)

    x = nrm(0, (BATCH, SEQ, D_MODEL))
    c = nrm(1, (BATCH, D_MODEL))
    offsets = jax.random.randint(k[2], (BATCH, 1), 0, 1024, dtype=jnp.int32)
    positions = offsets + jnp.arange(SEQ, dtype=jnp.int32)[None, :]
    w_in = nrm(3, (L, D_MODEL, D_IN), D_MODEL ** -0.5)
    s5_lambda_re = -0.5 + nrm(4, (L, S5_GROUPS, S5_STATE), 0.01)
    s5_lambda_im = math.pi * jnp.arange(S5_STATE, dtype=jnp.float32) + nrm(5, (L, S5_GROUPS, S5_STATE), 0.01)
    s5_log_dt = jax.random.uniform(k[6], (L, S5_GROUPS), jnp.float32, math.log(1e-3), math.log(1e-1))
    s5_b_re = nrm(7, (L, S5_GROUPS, S5_STATE, S5_GROUP), (2 * S5_GROUP) ** -0.5)
    s5_b_im = nrm(8, (L, S5_GROUPS, S5_STATE, S5_GROUP), (2 * S5_GROUP) ** -0.5)
    s5_c_re = nrm(9, (L, S5_GROUPS, S5_GROUP, S5_STATE), (2 * S5_STATE) ** -0.5)
    s5_c_im = nrm(10, (L, S5_GROUPS, S5_GROUP, S5_STATE), (2 * S5_STATE) ** -0.5)
    s5_d = nrm(11, (L, S5_WIDTH))
    s5_w_glu = nrm(12, (L, S5_WIDTH, S5_WIDTH), S5_WIDTH ** -0.5)
    mla_q_norm = gain(13, (L, MLA_Q_RANK))
    mla_w_uq = nrm(14, (L, MLA_Q_RANK, MLA_HEADS * (MLA_NOPE + MLA_ROPE)), MLA_Q_RANK ** -0.5)
    mla_kv_norm = gain(15, (L, MLA_KV_RANK))
    mla_w_ukv = nrm(16, (L, MLA_KV_RANK, MLA_HEADS * (MLA_NOPE + MLA_V)), MLA_KV_RANK ** -0.5)
    hg_lb_logits = nrm(17, (L, HG_HEADS * HG_DK), 0.5)
    hg_out_norm = gain(18, (L, HG_DV))
    w_out = nrm(19, (L, D_MIX, D_MODEL), D_MIX ** -0.5)
    mix_pre_norm = gain(20, (L, D_MODEL))
    mix_post_norm = gain(21, (L, D_MODEL))
    ffn_pre_norm = gain(22, (L, D_MODEL))
    ffn_post_norm = gain(23, (L, D_MODEL))
    ffn_w_up = nrm(24, (L, D_MODEL, 2 * D_FF), D_MODEL ** -0.5)
    ffn_conv_w = nrm(25, (L, CONV_W, 2 * D_FF), CONV_W ** -0.5)
    ffn_conv_b = nrm(26, (L, 2 * D_FF), 0.02)
    ffn_w_down = nrm(27, (L, D_FF, D_MODEL), D_FF ** -0.5)
    w_ada = nrm(28, (L, D_MODEL, 6 * D_MODEL), 0.5 * D_MODEL ** -0.5)
    b_ada = nrm(29, (L, 6 * D_MODEL), 0.02)
    return {'x': x, 'c': c, 'positions': positions, 'w_in': w_in,
            's5_lambda_re': s5_lambda_re, 's5_lambda_im': s5_lambda_im, 's5_log_dt': s5_log_dt,
            's5_b_re': s5_b_re, 's5_b_im': s5_b_im, 's5_c_re': s5_c_re, 's5_c_im': s5_c_im,
            's5_d': s5_d, 's5_w_glu': s5_w_glu,
            'mla_q_norm': mla_q_norm, 'mla_w_uq': mla_w_uq, 'mla_kv_norm': mla_kv_norm, 'mla_w_ukv': mla_w_ukv,
            'hg_lb_logits': hg_lb_logits, 'hg_out_norm': hg_out_norm, 'w_out': w_out,
            'mix_pre_norm': mix_pre_norm, 'mix_post_norm': mix_post_norm,
            'ffn_pre_norm': ffn_pre_norm, 'ffn_post_norm': ffn_post_norm,
            'ffn_w_up': ffn_w_up, 'ffn_conv_w': ffn_conv_w, 'ffn_conv_b': ffn_conv_b, 'ffn_w_down': ffn_w_down,
            'w_ada': w_ada, 'b_ada': b_ada}


def reference(x, c, positions, w_in, s5_lambda_re, s5_lambda_im, s5_log_dt, s5_b_re, s5_b_im,
              s5_c_re, s5_c_im, s5_d, s5_w_glu, mla_q_norm, mla_w_uq, mla_kv_norm, mla_w_ukv,
              hg_lb_logits, hg_out_norm, w_out, mix_pre_norm, mix_post_norm, ffn_pre_norm, ffn_post_norm,
              ffn_w_up, ffn_conv_w, ffn_conv_b, ffn_w_down, w_ada, b_ada):
    cos, sin = rope_tables(positions)
    probs = jax.nn.softmax(hg_lb_logits.astype(jnp.float32), axis=0)
    lower_bounds = jnp.cumsum(probs, axis=0) - probs[0:1]
    c_act = jax.nn.silu(c)
    for l in range(DEPTH):
        mod = c_act @ w_ada[l] + b_ada[l]
        sh1, sc1, g1, sh2, sc2, g2 = jnp.split(mod[:, None, :], 6, axis=-1)
        h = rmsnorm(x, mix_pre_norm[l]) * (1.0 + sc1) + sh1
        proj = h @ w_in[l]
        u_s5, c_q, c_kv, k_rope, hq, hf, hi, hg = jnp.split(proj, IN_OFFSETS, axis=-1)
        y_s5 = s5_mixer(u_s5, s5_lambda_re[l], s5_lambda_im[l], s5_log_dt[l], s5_b_re[l], s5_b_im[l],
                        s5_c_re[l], s5_c_im[l], s5_d[l], s5_w_glu[l])
        y_mla = mla_mixer(c_q, c_kv, k_rope, mla_q_norm[l], mla_w_uq[l], mla_kv_norm[l], mla_w_ukv[l], cos, sin)
        y_hg = hgrn2_mixer(hq, hf, hi, hg, lower_bounds[l], hg_out_norm[l])
        mixed = jnp.concatenate([y_s5, y_mla, y_hg], axis=-1) @ w_out[l]
        x = x + g1 * rmsnorm(mixed, mix_post_norm[l])
        h = rmsnorm(x, ffn_pre_norm[l]) * (1.0 + sc2) + sh2
        y = conv_geglu_ffn(h, ffn_w_up[l], ffn_conv_w[l], ffn_conv_b[l], ffn_w_down[l])
        x = x + g2 * rmsnorm(y, ffn_post_norm[l])
    return x
```

```python
import math
from contextlib import ExitStack
import numpy as np
import concourse.bass as bass
import concourse.mybir as mybir
from concourse.bass_utils import run_bass_kernel_spmd

F32 = mybir.dt.float32
BF16 = mybir.dt.bfloat16
I32 = mybir.dt.int32
AF = mybir.ActivationFunctionType
ALU = mybir.AluOpType
AX = mybir.AxisListType

D = 2048
DFF = 5504
NCH = 43
WIN = 3584
EPS = 1e-6
TWO_PI = 2.0 * math.pi
C1_2PI = 6.28125
C2_2PI = TWO_PI - 6.28125

COMPUTE = ("pe", "act", "dve", "pool")


class Buf:
    __slots__ = ("w", "r")

    def __init__(self):
        self.w = {}
        self.r = {}


class Tile:
    def __init__(self, t, name):
        self.t = t
        self.b = Buf()
        self.name = name

    def __getitem__(self, k):
        return self.t[k]


class Prog:
    def __init__(self, nc, es):
        self.nc = nc
        self.es = es
        self.eh = {"pe": nc.tensor, "act": nc.scalar, "dve": nc.vector, "pool": nc.gpsimd, "sp": nc.sync}
        self.clock = {e: {} for e in self.eh}
        self.val = {}
        self.sems = {}
        self.n_ins = 0

    def sem(self, key):
        s = self.sems.get(key)
        if s is None:
            s = self.es.enter_context(self.nc.semaphore("s_" + key.replace(":", "_")))
            self.sems[key] = s
        return s

    def op(self, eng, fn, r=(), w=(), dsem=None):
        deps = {}
        war = {}
        for b in r:
            for k, v in b.w.items():
                if deps.get(k, 0) < v:
                    deps[k] = v
        for b in w:
            for k, v in b.w.items():
                if deps.get(k, 0) < v:
                    deps[k] = v
            for k, v in b.r.items():
                if war.get(k, 0) < v:
                    war[k] = v
        if dsem is None and eng == "pe":
            war.pop("pe", None)
            deps.pop("pe", None)
        for k, v in war.items():
            if deps.get(k, 0) < v:
                deps[k] = v
        clk = self.clock[eng]
        e = self.eh[eng]
        for k, v in deps.items():
            if clk.get(k, 0) < v:
                e.wait_ge(self.sem(k), v)
                clk[k] = v
        ins = fn(e)
        self.n_ins += 1
        if dsem is None:
            key = eng
            val = self.val.get(key, 0) + 1
            ins.then_inc(self.sem(key), 1)
        else:
            key = "d:" + dsem
            val = self.val.get(key, 0) + 16
            ins.then_inc(self.sem(key), 16)
        self.val[key] = val
        for b in w:
            b.w = {key: val}
            b.r = {}
        for b in r:
            if b.r.get(key, 0) < val:
                b.r[key] = val
        return (key, val)

    def barrier(self, engines=None):
        for eng in (engines or self.eh):
            clk = self.clock[eng]
            e = self.eh[eng]
            for k, v in self.val.items():
                if clk.get(k, 0) < v:
                    e.wait_ge(self.sem(k), v)
                    clk[k] = v


def build(T, L, dbg=False, stop=None):
    NB = T // 512
    NT = T // 128
    SBT = min(T, 2048)
    NSB = T // SBT
    NCK = T // 64
    nc = bass.Bass("TRN2", target_bir_lowering=False)

    def din(name, shape, dt=F32):
        return nc.dram_tensor(name, list(shape), dt, kind="ExternalInput").ap()

    def dscr(name, shape, dt=F32):
        return nc.dram_tensor(name, list(shape), dt, kind=("ExternalOutput" if dbg else "Internal")).ap()

    x_in = din("x", [T, D])
    pos_in = din("pos", [1, T], I32)
    cT_in = din("cT", [128, 16])
    w_in = din("w_in", [L, D, WIN])
    w_out = din("w_out", [L, D, D])
    w_up = din("w_up", [L, D, 2 * DFF])
    w_down = din("w_down", [L, DFF, D])
    w_ada = din("w_ada", [L, D, 6 * D])
    b_ada = din("b_ada", [L, 6 * D])
    w_uq = din("w_uq", [L, 512, 2048])
    w_ukv = din("w_ukv", [L, 256, 2048])
    w_glu = din("w_glu", [L, 512, 512])
    norms = din("norms", [L, 4, D])
    pvec = din("pvec", [128, L, 16])
    convp = din("convp", [128, L, 86, 4])
    s5lam = din("s5lam", [L, 3, 2048])
    s5lamp = din("s5lamp", [128, L, 3, 32])
    s5b = din("s5b", [L, 2, 128, 2048])
    s5c = din("s5c", [L, 2, 128, 4096])
    lbl = din("lbl", [128, 4, L])
    cst = din("cst", [128, 1024])
    maskA_in = din("maskA", [128, 2048])
    out = nc.dram_tensor("out", [T, D], F32, kind="ExternalOutput").ap()

    modt = dscr("modt", [L, 6, 128, D])
    projT = dscr("projT", [3072, T])
    vtok = dscr("vtok", [T, 512], BF16)
    yT = dscr("yT", [D, T], BF16)
    h2T = dscr("h2T", [D, T], BF16)
    aT = dscr("aT", [T // 128, 128, NCH, 128], BF16)
    ydn = dscr("ydn", [T, D])
    ropeT = dscr("ropeT", [2, 64, T])

    with ExitStack() as es:
        P = Prog(nc, es)

        uid = [0]

        def sb(stack, name, shape, dt):
            uid[0] += 1
            return Tile(stack.enter_context(nc.sbuf_tensor("%s_u%d" % (name, uid[0]), list(shape), dt)), name)

        def ps(stack, name, shape=(128, 512), dt=F32):
            uid[0] += 1
            return Tile(stack.enter_context(nc.psum_tensor("%s_u%d" % (name, uid[0]), list(shape), dt)), name)

        def load(eng, dst, dst_ap, src_ap):
            return P.op(eng, lambda e: e.dma_start(out=dst_ap, in_=src_ap), w=[dst.b], dsem="L" + dst.name)

        def store(eng, src, dst_ap, src_ap):
            return P.op(eng, lambda e: e.dma_start(out=dst_ap, in_=src_ap), r=[src.b], dsem="S" + src.name)

        def mm(o, o_ap, lhsT_ap, rhs_ap, start, stop, r):
            return P.op("pe", lambda e: e.matmul(o_ap, lhsT_ap, rhs_ap, start=start, stop=stop), r=r, w=[o.b])

        def act(o, o_ap, i_ap, func, r, bias=None, scale=None, accum=None, extra_w=()):
            kw = {}
            if bias is not None:
                kw["bias"] = bias
            if scale is not None:
                kw["scale"] = scale
            if accum is not None:
                kw["accum_out"] = accum
            return P.op("act", lambda e: e.activation(out=o_ap, in_=i_ap, func=func, **kw), r=r, w=[o.b] + list(extra_w))

        def tt(o, o_ap, a_ap, b_ap, op, r, eng="dve"):
            return P.op(eng, lambda e: e.tensor_tensor(out=o_ap, in0=a_ap, in1=b_ap, op=op), r=r, w=[o.b])

        def ts(o, o_ap, a_ap, s1, s2, op0, op1, r, eng="dve"):
            if op1 is None:
                return P.op(eng, lambda e: e.tensor_scalar(out=o_ap, in0=a_ap, scalar1=s1, scalar2=None, op0=op0), r=r, w=[o.b])
            return P.op(eng, lambda e: e.tensor_scalar(out=o_ap, in0=a_ap, scalar1=s1, scalar2=s2, op0=op0, op1=op1), r=r, w=[o.b])

        def stt(o, o_ap, a_ap, s, b_ap, op0, op1, r):
            return P.op("dve", lambda e: e.scalar_tensor_tensor(out=o_ap, in0=a_ap, scalar=s, in1=b_ap, op0=op0, op1=op1), r=r, w=[o.b])

        def cp(o, o_ap, i_ap, r, eng="dve"):
            return P.op(eng, lambda e: e.tensor_copy(out=o_ap, in_=i_ap), r=r, w=[o.b])

        def mset(o, o_ap, v, eng="dve"):
            return P.op(eng, lambda e: e.memset(o_ap, v), w=[o.b])

        cstf = sb(es, "cstf", [128, 1024], F32)
        ident = sb(es, "ident", [128, 128], BF16)
        ones = sb(es, "ones", [128, 128], BF16)
        mask64 = sb(es, "mask64", [128, 64], BF16)
        rst = sb(es, "rst", [128, 512], F32)
        lbt = sb(es, "lbt", [128, 4, L], F32)
        omlt = sb(es, "omlt", [128, 4, L], F32)
        nomlt = sb(es, "nomlt", [128, 4, L], F32)
        pv = sb(es, "pv", [128, L, 16], F32)
        load("sp", cstf, cstf[:], cst)
        load("sp", pv, pv[:], pvec)
        cp(ident, ident[:], cstf[:, 0:128], [cstf.b])
        cp(mask64, mask64[:], cstf[:, 256:320], [cstf.b])
        mset(ones, ones[:], 1.0)
        mset(rst, rst[:], 1.0)
        mset(rst, rst[:, 0:512:64], 0.0)
        rst32 = sb(es, "rst32", [128, 512], F32)
        mset(rst32, rst32[:], 1.0)
        mset(rst32, rst32[:, 0:512:32], 0.0)
        permsw = cstf[:, 128:256]
        sgnB = cstf[:, 320:321]
        invf = cstf[0:64, 321:322]
        sgnr = cstf[0:64, 322:323]
        epsc = cstf[:, 323:324]

        def sincos(stack, nm, ang, shape, o_sin, o_sin_ap, o_cos, o_cos_ap, npart=128):
            k_i = sb(stack, nm + "_ki", shape, I32)
            k_f = sb(stack, nm + "_kf", shape, F32)
            r_t = sb(stack, nm + "_r", shape, F32)
            sl = tuple([slice(0, npart)] + [slice(None)] * (len(shape) - 1))
            for which in (0, 1):
                if which == 1:
                    ts(r_t, r_t[sl], ang[sl], math.pi / 2, None, ALU.add, None, [ang.b])
                    src = r_t
                else:
                    src = ang
                ts(k_f, k_f[sl], src[sl], 1.0 / TWO_PI, None, ALU.mult, None, [src.b])
                cp(k_i, k_i[sl], k_f[sl], [k_f.b])
                cp(k_f, k_f[sl], k_i[sl], [k_i.b])
                stt(r_t, r_t[sl], k_f[sl], -C1_2PI, src[sl], ALU.mult, ALU.add, [k_f.b, src.b])
                stt(r_t, r_t[sl], k_f[sl], -C2_2PI, r_t[sl], ALU.mult, ALU.add, [k_f.b, r_t.b])
                ts(r_t, r_t[sl], r_t[sl], -3.1415925, 3.1415925, ALU.max, ALU.min, [r_t.b])
                if which == 0:
                    act(o_sin, o_sin_ap, r_t[sl], AF.Sin, [r_t.b])
                else:
                    act(o_cos, o_cos_ap, r_t[sl], AF.Sin, [r_t.b])

        with ExitStack() as st:
            lg = sb(st, "lg", [128, 4, L], F32)
            ssum = sb(st, "ssum", [128, 4], F32)
            load("sp", lg, lg[:], lbl)
            act(lg, lg[:], lg[:], AF.Exp, [lg.b])
            P.op("dve", lambda e: e.reduce_sum(out=ssum[:], in_=lg[:], axis=AX.X), r=[lg.b], w=[ssum.b])
            P.op("dve", lambda e: e.reciprocal(out=ssum[:], in_=ssum[:]), r=[ssum.b], w=[ssum.b])
            tt(lg, lg[:], lg[:], ssum[:].unsqueeze(2).broadcast_to([128, 4, L]), ALU.mult, [lg.b, ssum.b])
            mset(lbt, lbt[:, :, 0:1], 0.0)
            for l in range(1, L):
                tt(lbt, lbt[:, :, l:l + 1], lbt[:, :, l - 1:l], lg[:, :, l:l + 1], ALU.add, [lbt.b, lg.b])
            ts(omlt, omlt[:], lbt[:], -1.0, 1.0, ALU.mult, ALU.add, [lbt.b])
            ts(nomlt, nomlt[:], omlt[:], -1.0, None, ALU.mult, None, [omlt.b])
            posi = sb(st, "posi", [64, T], I32)
            ang = sb(st, "ang", [64, T], F32)
            sint = sb(st, "sint", [64, T], F32)
            cost = sb(st, "cost", [64, T], F32)
            load("sp", posi, posi[:], pos_in[0, :].partition_broadcast(64))
            cp(ang, ang[:], posi[:], [posi.b])
            ts(ang, ang[:], ang[:], invf, None, ALU.mult, None, [ang.b, cstf.b])
            sincos(st, "rp", ang, [64, T], sint, sint[:], cost, cost[:], npart=64)
            ts(sint, sint[:], sint[:], sgnr, None, ALU.mult, None, [sint.b, cstf.b])
            store("sp", cost, ropeT[0], cost[:])
            store("sp", sint, ropeT[1], sint[:])
            P.barrier()

        cbc = sb(es, "cbc", [128, 16, 128], BF16)
        with ExitStack() as st:
            cTt = sb(st, "cTt", [128, 16], F32)
            onesf = sb(st, "onesf", [128, 128], F32)
            load("sp", cTt, cTt[:], cT_in)
            act(cTt, cTt[:], cTt[:], AF.Silu, [cTt.b])
            mset(onesf, onesf[:], 1.0)
            for kc in range(16):
                ts(cbc, cbc[:, kc, :], onesf[:], cTt[:, kc:kc + 1], None, ALU.mult, None, [onesf.b, cTt.b])
            P.barrier()

        def mod_gen(l, st, pm):
            wa = [sb(st, "wa%d" % i, [128, 16, 512], BF16) for i in range(2)]
            bb = [sb(st, "mbb%d" % i, [128, 512], F32) for i in range(2)]
            nn = [sb(st, "mnn%d" % i, [128, 512], F32) for i in range(2)]
            rr = [sb(st, "mrr%d" % i, [128, 512], F32) for i in range(2)]
            for cg in range(24):
                i = cg % 2
                sec = cg // 4
                c0 = (cg % 4) * 512
                wt = wa[i]
                load("pool", wt, wt[:], w_ada[l, :, cg * 512:(cg + 1) * 512].rearrange("(k p) n -> p k n", p=128))
                load("pool", bb[i], bb[i][:], b_ada[l, cg * 512:(cg + 1) * 512].partition_broadcast(128))
                kind = sec % 3
                if kind != 0:
                    nidx = (0 if kind == 1 else 1) + 2 * (sec // 3)
                    load("pool", nn[i], nn[i][:], norms[l, nidx, c0:c0 + 512].partition_broadcast(128))
                p_ = pm[i]
                for kc in range(16):
                    mm(p_, p_[:], cbc[:, kc, :], wt[:, kc, :], kc == 0, kc == 15, [cbc.b, wt.b])
                tt(rr[i], rr[i][:], p_[:], bb[i][:], ALU.add, [p_.b, bb[i].b])
                if kind == 1:
                    stt(rr[i], rr[i][:], rr[i][:], 1.0, nn[i][:], ALU.add, ALU.mult, [rr[i].b, nn[i].b])
                elif kind == 2:
                    tt(rr[i], rr[i][:], rr[i][:], nn[i][:], ALU.mult, [rr[i].b, nn[i].b])
                store("pool", rr[i], modt[l, sec, :, c0:c0 + 512], rr[i][:])
                yield cg

        with ExitStack() as st:
            pm0 = [ps(st, "pm%d" % i) for i in range(2)]
            for _ in mod_gen(0, st, pm0):
                pass
            P.barrier()
        MOD_B1, MOD_A1, MOD_G1, MOD_B2, MOD_A2, MOD_G2 = 0, 1, 2, 3, 4, 5

        def rms_rstd(stack_tiles, src, src_ap, junk, r):
            ss, rs = stack_tiles
            act(junk, junk[:], src_ap, AF.Square, r, accum=ss[:], extra_w=[ss.b])
            act(rs, rs[:], ss[:], AF.Sqrt, [ss.b, cstf.b], bias=epsc, scale=1.0 / D)
            P.op("dve", lambda e: e.reciprocal(out=rs[:], in_=rs[:]), r=[rs.b], w=[rs.b])
            return rs

        JT = cstf[:, 384:448]

        def phase_s5(l):
            with ExitStack() as st:
                TA = sb(st, "TA", [128, 32, 64], F32)
                TB = sb(st, "TB", [128, 32, 64], F32)
                TC = sb(st, "TC", [128, 32, 64], F32)
                TD = sb(st, "TD", [128, 32, 64], F32)
                Bb1 = sb(st, "Bb1", [128, 32, 128], BF16)
                Bb2 = sb(st, "Bb2", [128, 32, 128], BF16)
                Cp1 = sb(st, "Cp1", [128, 32, 128], BF16)
                Cp2 = sb(st, "Cp2", [128, 32, 128], BF16)
                K1 = sb(st, "K1", [128, 32], F32)
                K2 = sb(st, "K2", [128, 32], F32)
                with ExitStack() as s2:
                    lamf = sb(s2, "lamf", [128, 3, 2048], F32)
                    load("sp", lamf, lamf[:].rearrange("p a n -> p (a n)"), s5lam[l].rearrange("a n -> (a n)").partition_broadcast(128))
                    lr, li, ld = lamf[:, 0, :], lamf[:, 1, :], lamf[:, 2, :]
                    tl = [sb(s2, "tl%d" % i, [128, 2048], F32) for i in range(6)]
                    dtf, af, ef, sf, cf, phf = tl
                    act(dtf, dtf[:], ld, AF.Exp, [lamf.b])
                    tt(af, af[:], lr, dtf[:], ALU.mult, [lamf.b, dtf.b])
                    tt(phf, phf[:], li, dtf[:], ALU.mult, [lamf.b, dtf.b])
                    act(ef, ef[:], af[:], AF.Exp, [af.b])
                    sincos(s2, "sf", phf, [128, 2048], sf, sf[:], cf, cf[:])
                    tt(cf, cf[:], cf[:], ef[:], ALU.mult, [cf.b, ef.b])
                    ts(cf, cf[:], cf[:], -1.0, None, ALU.add, None, [cf.b])
                    tt(sf, sf[:], sf[:], ef[:], ALU.mult, [sf.b, ef.b])
                    tt(dtf, dtf[:], lr, lr, ALU.mult, [lamf.b])
                    tt(af, af[:], li, li, ALU.mult, [lamf.b])
                    tt(dtf, dtf[:], dtf[:], af[:], ALU.add, [dtf.b, af.b])
                    P.op("dve", lambda e: e.reciprocal(out=dtf[:], in_=dtf[:]), r=[dtf.b], w=[dtf.b])
                    tt(ef, ef[:], cf[:], lr, ALU.mult, [cf.b, lamf.b])
                    tt(af, af[:], sf[:], li, ALU.mult, [sf.b, lamf.b])
                    tt(ef, ef[:], ef[:], af[:], ALU.add, [ef.b, af.b])
                    tt(ef, ef[:], ef[:], dtf[:], ALU.mult, [ef.b, dtf.b])
                    tt(phf, phf[:], sf[:], lr, ALU.mult, [sf.b, lamf.b])
                    tt(af, af[:], cf[:], li, ALU.mult, [cf.b, lamf.b])
                    tt(phf, phf[:], phf[:], af[:], ALU.subtract, [phf.b, af.b])
                    tt(phf, phf[:], phf[:], dtf[:], ALU.mult, [phf.b, dtf.b])
                    cr3 = ef[:].rearrange("p (g q) -> p g q", q=64)
                    ci3 = phf[:].rearrange("p (g q) -> p g q", q=64)
                    braw = sb(s2, "braw", [128, 2, 2048], F32)
                    load("sp", braw, braw[:], s5b[l].rearrange("a k n -> k a n"))
                    bre3 = braw[:, 0, :].rearrange("p (g q) -> p g q", q=64)
                    bim3 = braw[:, 1, :].rearrange("p (g q) -> p g q", q=64)
                    t1 = af[:].rearrange("p (g q) -> p g q", q=64)
                    t2 = sf[:].rearrange("p (g q) -> p g q", q=64)
                    tt(af, t1, cr3, bre3, ALU.mult, [ef.b, braw.b])
                    tt(sf, t2, ci3, bim3, ALU.mult, [phf.b, braw.b])
                    tt(Bb1, Bb1[:, :, 0:64], t1, t2, ALU.subtract, [af.b, sf.b])
                    tt(af, t1, cr3, bim3, ALU.mult, [ef.b, braw.b])
                    tt(sf, t2, ci3, bre3, ALU.mult, [phf.b, braw.b])
                    tt(Bb1, Bb1[:, :, 64:128], t1, t2, ALU.add, [af.b, sf.b])
                    cp(Bb2, Bb2[:, :, 0:64], Bb1[:, :, 64:128], [Bb1.b])
                    cp(Bb2, Bb2[:, :, 64:128], Bb1[:, :, 0:64], [Bb1.b])
                    P.barrier()
                with ExitStack() as s2:
                    craw = sb(s2, "craw", [128, 2, 4096], F32)
                    load("sp", craw, craw[:], s5c[l].rearrange("a k n -> k a n"))
                    ts(Cp1, Cp1[:].rearrange("p g m -> p (g m)"), craw[:, 0, :], sgnB, None, ALU.mult, None, [craw.b, cstf.b])
                    ts(Cp2, Cp2[:].rearrange("p g m -> p (g m)"), craw[:, 1, :], -1.0, None, ALU.mult, None, [craw.b])
                    lamp = sb(s2, "lamp", [128, 3, 32], F32)
                    load("sp", lamp, lamp[:], s5lamp[:, l])
                    dtp = sb(s2, "dtp", [128, 32], F32)
                    app = sb(s2, "app", [128, 32], F32)
                    php = sb(s2, "php", [128, 32], F32)
                    act(dtp, dtp[:], lamp[:, 2, :], AF.Exp, [lamp.b])
                    tt(app, app[:], lamp[:, 0, :], dtp[:], ALU.mult, [lamp.b, dtp.b])
                    tt(php, php[:], lamp[:, 1, :], dtp[:], ALU.mult, [lamp.b, dtp.b])
                    ang3 = sb(s2, "ang3", [128, 32, 64], F32)
                    ma3 = sb(s2, "ma3", [128, 32, 64], F32)
                    sn3 = sb(s2, "sn3", [128, 32, 64], F32)
                    cs3 = sb(s2, "cs3", [128, 32, 64], F32)
                    jb = JT.unsqueeze(1).broadcast_to([128, 32, 64])
                    tt(ang3, ang3[:], php[:].unsqueeze(2).broadcast_to([128, 32, 64]), jb, ALU.mult, [php.b, cstf.b])
                    tt(ma3, ma3[:], app[:].unsqueeze(2).broadcast_to([128, 32, 64]), jb, ALU.mult, [app.b, cstf.b])
                    sincos(s2, "s3", ang3, [128, 32, 64], sn3, sn3[:], cs3, cs3[:])
                    act(ang3, ang3[:], ma3[:], AF.Exp, [ma3.b], scale=-1.0)
                    tt(TA, TA[:], ang3[:], cs3[:], ALU.mult, [ang3.b, cs3.b])
                    tt(TB, TB[:], ang3[:], sn3[:], ALU.mult, [ang3.b, sn3.b])
                    ts(TB, TB[:], TB[:], sgnB, None, ALU.mult, None, [TB.b, cstf.b])
                    act(ang3, ang3[:], ma3[:], AF.Exp, [ma3.b])
                    tt(TC, TC[:], ang3[:], cs3[:], ALU.mult, [ang3.b, cs3.b])
                    tt(TD, TD[:], ang3[:], sn3[:], ALU.mult, [ang3.b, sn3.b])
                    a64 = sb(s2, "a64", [128, 32], F32)
                    p64 = sb(s2, "p64", [128, 32], F32)
                    s64 = sb(s2, "s64", [128, 32], F32)
                    c64 = sb(s2, "c64", [128, 32], F32)
                    ts(a64, a64[:], app[:], 64.0, None, ALU.mult, None, [app.b])
                    ts(p64, p64[:], php[:], 64.0, None, ALU.mult, None, [php.b])
                    sincos(s2, "s6", p64, [128, 32], s64, s64[:], c64, c64[:])
                    act(a64, a64[:], a64[:], AF.Exp, [a64.b])
                    tt(K1, K1[:], a64[:], c64[:], ALU.mult, [a64.b, c64.b])
                    tt(K2, K2[:], a64[:], s64[:], ALU.mult, [a64.b, s64.b])
                    ts(K2, K2[:], K2[:], sgnB, -1.0, ALU.mult, ALU.mult, [K2.b, cstf.b])
                    P.barrier()
                sy = ExitStack()
                ygb = sb(sy, "ygb", [128, 4, T], BF16)
                with ExitStack() as s2:
                    uTb = [sb(s2, "uTb%d" % i, [128, 2, 512], BF16) for i in range(2)]
                    Sall2 = [sb(s2, "Sall%d" % i, [128, 16, 512], F32) for i in range(2)]
                    Hs1 = sb(s2, "Hs1", [128, 16, NCK + 1], F32)
                    H2 = [sb(s2, "H2%d" % i, [128, 16], F32) for i in range(2)]
                    S63 = sb(s2, "S63", [128, 16, 8], F32)
                    S63s = sb(s2, "S63s", [128, 16, 8], F32)
                    w1 = [sb(s2, "w1%d" % i, [128, 512], F32) for i in range(3)]
                    w2 = [sb(s2, "w2%d" % i, [128, 512], F32) for i in range(3)]
                    spt = [sb(s2, "spt%d" % i, [128, 512], F32) for i in range(2)]
                    V1 = [sb(s2, "V1%d" % i, [128, 512], BF16) for i in range(2)]
                    V2 = [sb(s2, "V2%d" % i, [128, 512], BF16) for i in range(2)]
                    zt = [sb(s2, "zt%d" % i, [128, 16], F32) for i in range(6)]
                    yv = sb(s2, "yv", [128, 512], F32)
                    pbu = [ps(s2, "pbu%d" % i) for i in range(4)]
                    psw = ps(s2, "psw")
                    py = [ps(s2, "py%d" % i) for i in range(2)]
                    nwl = [0]
                    for half in range(2):
                        g0 = half * 16
                        mset(Hs1, Hs1[:, :, 0:1], 0.0)
                        mset(H2[0], H2[0][:], 0.0)
                        hcl = [0]
                        K1h = K1[:, g0:g0 + 16]
                        K2h = K2[:, g0:g0 + 16]
                        def P1(tb):
                            tsl = slice(tb * 512, (tb + 1) * 512)
                            Sall = Sall2[tb % 2]
                            uT = uTb[tb % 2]
                            for c in range(2):
                                load("pool", uT, uT[:, c, :], projT[(2 * half + c) * 128:(2 * half + c + 1) * 128, tsl])
                            pend = None
                            for gi in range(17):
                                if gi < 16:
                                    g = g0 + gi
                                    c = gi // 8
                                    nw = nwl[0]
                                    p1 = pbu[(2 * nw) % 4]
                                    p2 = pbu[(2 * nw + 1) % 4]
                                    a_, b_ = w1[nw % 3], w2[nw % 3]
                                    nwl[0] += 1
                                    mm(p1, p1[:], Bb1[:, g, :], uT[:, c, :], True, True, [Bb1.b, uT.b])
                                    mm(p2, p2[:], Bb2[:, g, :], uT[:, c, :], True, True, [Bb2.b, uT.b])
                                    tab = TA[:, g, :].unsqueeze(1).broadcast_to([128, 8, 64])
                                    tbb = TB[:, g, :].unsqueeze(1).broadcast_to([128, 8, 64])
                                    tt(a_, a_[:].rearrange("p (n j) -> p n j", j=64), p1[:].rearrange("p (n j) -> p n j", j=64), tab, ALU.mult, [p1.b, TA.b])
                                    tt(b_, b_[:].rearrange("p (n j) -> p n j", j=64), p2[:].rearrange("p (n j) -> p n j", j=64), tbb, ALU.mult, [p2.b, TB.b])
                                    tt(a_, a_[:], a_[:], b_[:], ALU.add, [a_.b, b_.b])
                                if pend is not None:
                                    pgi, pa = pend
                                    P.op("dve", lambda e, pgi=pgi, pa=pa: e.tensor_tensor_scan(out=Sall[:, pgi, :], data0=rst[:], data1=pa[:], initial=0.0, op0=ALU.mult, op1=ALU.add), r=[rst.b, pa.b], w=[Sall.b])
                                pend = (gi, a_) if gi < 16 else None

                        def BD(tb):
                            Sall = Sall2[tb % 2]
                            cp(S63, S63[:], Sall[:, :, 63:512:64], [Sall.b])
                            mm(psw, psw[:, 0:128], permsw, S63[:].rearrange("p g n -> p (g n)"), True, True, [cstf.b, S63.b])
                            cp(S63s, S63s[:].rearrange("p g n -> p (g n)"), psw[:, 0:128], [psw.b])
                            for n8 in range(8):
                                n = tb * 8 + n8
                                z1, z2, t1_, t2_, t3_, t4_ = zt
                                hcur = hcl[0]
                                hc, hn = H2[hcur], H2[1 - hcur]
                                tt(z1, z1[:], S63[:, :, n8], Hs1[:, :, n], ALU.add, [S63.b, Hs1.b])
                                tt(z2, z2[:], S63s[:, :, n8], hc[:], ALU.add, [S63s.b, hc.b])
                                tt(t1_, t1_[:], z1[:], K1h, ALU.mult, [z1.b, K1.b])
                                tt(t2_, t2_[:], z2[:], K2h, ALU.mult, [z2.b, K2.b])
                                tt(Hs1, Hs1[:, :, n + 1], t1_[:], t2_[:], ALU.add, [t1_.b, t2_.b])
                                tt(t3_, t3_[:], z2[:], K1h, ALU.mult, [z2.b, K1.b])
                                tt(t4_, t4_[:], z1[:], K2h, ALU.mult, [z1.b, K2.b])
                                tt(hn, hn[:], t3_[:], t4_[:], ALU.subtract, [t3_.b, t4_.b])
                                hcl[0] = 1 - hcur

                        def P2(tb):
                            tsl = slice(tb * 512, (tb + 1) * 512)
                            Sall = Sall2[tb % 2]
                            uT = uTb[tb % 2]
                            for c in range(2):
                                ch = 2 * half + c
                                py_ = py[(tb * 2 + c) % 2]
                                for g8 in range(8):
                                    gi = c * 8 + g8
                                    g = g0 + gi
                                    s_ = spt[g8 % 2]
                                    v1_, v2_ = V1[g8 % 2], V2[g8 % 2]
                                    hb_ = Hs1[:, gi, tb * 8:tb * 8 + 8].unsqueeze(2).broadcast_to([128, 8, 64])
                                    s3 = s_[:].rearrange("p (n j) -> p n j", j=64)
                                    for n8 in range(8):
                                        act(s_, s_[:, n8 * 64:(n8 + 1) * 64], Sall[:, gi, n8 * 64:(n8 + 1) * 64], AF.Identity, [Sall.b, Hs1.b], bias=Hs1[:, gi, tb * 8 + n8:tb * 8 + n8 + 1])
                                    tt(v1_, v1_[:].rearrange("p (n j) -> p n j", j=64), s3, TC[:, g, :].unsqueeze(1).broadcast_to([128, 8, 64]), ALU.mult, [s_.b, TC.b], eng="pool")
                                    tt(v2_, v2_[:].rearrange("p (n j) -> p n j", j=64), s3, TD[:, g, :].unsqueeze(1).broadcast_to([128, 8, 64]), ALU.mult, [s_.b, TD.b], eng="pool")
                                    mm(py_, py_[:], Cp1[:, g, :], v1_[:], g8 == 0, False, [Cp1.b, v1_.b])
                                    mm(py_, py_[:], Cp2[:, g, :], v2_[:], False, g8 == 7, [Cp2.b, v2_.b])
                                stt(yv, yv[:], uT[:, c, :], pv[:, l, 7 + ch:8 + ch], py_[:], ALU.mult, ALU.add, [uT.b, pv.b, py_.b])
                                act(ygb, ygb[:, ch, tsl], yv[:], AF.Gelu_apprx_tanh, [yv.b])

                        P1(0)
                        for tb in range(NB):
                            BD(tb)
                            if tb + 1 < NB:
                                P1(tb + 1)
                            P2(tb)
                    P.barrier()
                with ExitStack() as s2:
                    wg = sb(s2, "wg", [128, 4, 512], BF16)
                    sg = [sb(s2, "sg%d" % i, [128, 512], F32) for i in range(2)]
                    yo = [sb(s2, "yo%d" % i, [128, 512], BF16) for i in range(2)]
                    pg = [ps(s2, "pg%d" % i) for i in range(2)]
                    load("pool", wg, wg[:], w_glu[l].rearrange("(k p) n -> p k n", p=128))
                    i = 0
                    for tb in range(NB):
                        tsl = slice(tb * 512, (tb + 1) * 512)
                        for mch in range(4):
                            p_ = pg[i % 2]
                            for kc in range(4):
                                mm(p_, p_[:], wg[:, kc, mch * 128:(mch + 1) * 128], ygb[:, kc, tsl], kc == 0, kc == 3, [wg.b, ygb.b])
                            act(sg[i % 2], sg[i % 2][:], p_[:], AF.Sigmoid, [p_.b])
                            tt(yo[i % 2], yo[i % 2][:], ygb[:, mch, tsl], sg[i % 2][:], ALU.mult, [ygb.b, sg[i % 2].b])
                            store("sp", yo[i % 2], yT[mch * 128:(mch + 1) * 128, tsl], yo[i % 2][:])
                            i += 1
                    P.barrier()
                sy.close()

        def phase_mla(l):
            SCALE = 192.0 ** -0.5
            with ExitStack() as st:
                cqn = sb(st, "cqn", [128, 4, T], BF16)
                ckn = sb(st, "ckn", [128, 2, T], BF16)
                krr = sb(st, "krr", [64, T], BF16)
                wq = sb(st, "wq", [128, 4, 2048], BF16)
                wkv = sb(st, "wkv", [128, 2, 2048], BF16)
                maskA = sb(st, "maskA", [128, 2048], BF16)
                load("pool", maskA, maskA[:], maskA_in)
                cosT = sb(st, "cosT", [64, T], F32)
                sinT = sb(st, "sinT", [64, T], F32)
                load("pool", wq, wq[:], w_uq[l].rearrange("(k p) n -> p k n", p=128))
                load("pool", wkv, wkv[:], w_ukv[l].rearrange("(k p) n -> p k n", p=128))
                load("sp", cosT, cosT[:], ropeT[0])
                load("sp", sinT, sinT[:], ropeT[1])
                with ExitStack() as s2:
                    raw = [sb(s2, "raw%d" % i, [128, 6, 512], F32) for i in range(2)]
                    sq = [sb(s2, "sq%d" % i, [128, 6, 512], BF16) for i in range(2)]
                    rq = [sb(s2, "rq%d" % i, [128, 512], F32) for i in range(2)]
                    rk = [sb(s2, "rk%d" % i, [128, 512], F32) for i in range(2)]
                    kra = [sb(s2, "kra%d" % i, [64, 512], F32) for i in range(2)]
                    krb = [sb(s2, "krb%d" % i, [64, 512], F32) for i in range(2)]
                    pq_ = [ps(s2, "pssq%d" % i) for i in range(2)]
                    pk_ = [ps(s2, "pssk%d" % i) for i in range(2)]
                    for tb in range(NB):
                        tsl = slice(tb * 512, (tb + 1) * 512)
                        i = tb % 2
                        load("sp", raw[i], raw[i][:], projT[512:1280, tsl].rearrange("(k p) t -> p k t", p=128))
                        load("sp", kra[i], kra[i][:], projT[1280:1344, tsl])
                        load("sp", krb[i], krb[i][:], projT[1344:1408, tsl])
                        act(sq[i], sq[i][:], raw[i][:], AF.Square, [raw[i].b])
                        for kc in range(4):
                            mm(pq_[i], pq_[i][:], ones[:], sq[i][:, kc, :], kc == 0, kc == 3, [ones.b, sq[i].b])
                        for kc in range(2):
                            mm(pk_[i], pk_[i][:], ones[:], sq[i][:, 4 + kc, :], kc == 0, kc == 1, [ones.b, sq[i].b])
                        act(rq[i], rq[i][:], pq_[i][:], AF.Sqrt, [pq_[i].b, cstf.b], bias=epsc, scale=1.0 / 512)
                        P.op("dve", lambda e, i=i: e.reciprocal(out=rq[i][:], in_=rq[i][:]), r=[rq[i].b], w=[rq[i].b])
                        act(rk[i], rk[i][:], pk_[i][:], AF.Sqrt, [pk_[i].b, cstf.b], bias=epsc, scale=1.0 / 256)
                        P.op("dve", lambda e, i=i: e.reciprocal(out=rk[i][:], in_=rk[i][:]), r=[rk[i].b], w=[rk[i].b])
                        for kc in range(4):
                            stt(cqn, cqn[:, kc, tsl], raw[i][:, kc, :], pv[:, l, kc:kc + 1], rq[i][:], ALU.mult, ALU.mult, [raw[i].b, pv.b, rq[i].b])
                        for kc in range(2):
                            stt(ckn, ckn[:, kc, tsl], raw[i][:, 4 + kc, :], pv[:, l, 4 + kc:5 + kc], rk[i][:], ALU.mult, ALU.mult, [raw[i].b, pv.b, rk[i].b])
                        tt(kra[i], kra[i][:], kra[i][:], cosT[:, tsl], ALU.mult, [kra[i].b, cosT.b])
                        tt(krb[i], krb[i][:], krb[i][:], sinT[:, tsl], ALU.mult, [krb[i].b, sinT.b])
                        tt(krr, krr[:, tsl], kra[i][:], krb[i][:], ALU.add, [kra[i].b, krb[i].b])
                    P.barrier()
                with ExitStack() as s2:
                    qn_ = sb(s2, "qn_", [128, T], BF16)
                    qr_ = sb(s2, "qr_", [64, T], BF16)
                    kn_ = sb(s2, "kn_", [128, T], BF16)
                    v_ = sb(s2, "v_", [128, NT, 128], BF16)
                    r1 = [sb(s2, "r1%d" % i, [64, 512], F32) for i in range(2)]
                    r2 = [sb(s2, "r2%d" % i, [64, 512], F32) for i in range(2)]
                    pT = [sb(s2, "pT%d" % i, [128, 512], BF16) for i in range(3)]
                    rsm = [sb(s2, "rsm%d" % i, [128, 512], F32) for i in range(2)]
                    yh = [sb(s2, "yh%d" % i, [128, 512], BF16) for i in range(2)]
                    pgen = [ps(s2, "pgen%d" % i) for i in range(2)]
                    pS = [ps(s2, "pS%d" % i) for i in range(2)]
                    po = [ps(s2, "po%d" % i) for i in range(2)]
                    psm = [ps(s2, "psm%d" % i) for i in range(2)]
                    ng = 0
                    npt = 0
                    nq = 0
                    for h in range(8):
                        c0 = h * 256
                        for tb in range(NB):
                            tsl = slice(tb * 512, (tb + 1) * 512)
                            p_ = pgen[ng % 2]; ng += 1
                            for kc in range(4):
                                mm(p_, p_[:], wq[:, kc, c0:c0 + 128], cqn[:, kc, tsl], kc == 0, kc == 3, [wq.b, cqn.b])
                            act(qn_, qn_[:, tsl], p_[:], AF.Copy, [p_.b])
                            pr = pgen[ng % 2]; ng += 1
                            for kc in range(4):
                                mm(pr, pr[0:64, :], wq[:, kc, c0 + 128:c0 + 192], cqn[:, kc, tsl], kc == 0, kc == 3, [wq.b, cqn.b])
                            a_ = r1[tb % 2]
                            tt(a_, a_[:], pr[0:64, :], cosT[:, tsl], ALU.mult, [pr.b, cosT.b])
                            pw = pgen[ng % 2]; ng += 1
                            for kc in range(4):
                                mm(pw, pw[0:64, :], wq[:, kc, c0 + 192:c0 + 256], cqn[:, kc, tsl], kc == 0, kc == 3, [wq.b, cqn.b])
                            b_ = r2[tb % 2]
                            tt(b_, b_[:], pw[0:64, :], sinT[:, tsl], ALU.mult, [pw.b, sinT.b])
                            tt(qr_, qr_[:, tsl], a_[:], b_[:], ALU.add, [a_.b, b_.b])
                            pk = pgen[ng % 2]; ng += 1
                            for kc in range(2):
                                mm(pk, pk[:], wkv[:, kc, c0:c0 + 128], ckn[:, kc, tsl], kc == 0, kc == 1, [wkv.b, ckn.b])
                            act(kn_, kn_[:, tsl], pk[:], AF.Copy, [pk.b])
                            pvv = pgen[ng % 2]; ng += 1
                            for ti in range(4):
                                t0 = tb * 512 + ti * 128
                                for kc in range(2):
                                    mm(pvv, pvv[:, ti * 128:(ti + 1) * 128], ckn[:, kc, t0:t0 + 128], wkv[:, kc, c0 + 128:c0 + 256], kc == 0, kc == 1, [wkv.b, ckn.b])
                            cp(v_, v_[:, tb * 4:tb * 4 + 4, :], pvv[:].rearrange("p (a d) -> p a d", d=128), [pvv.b])
                        for qc in range(NB):
                            qsl = slice(qc * 512, (qc + 1) * 512)
                            po_ = po[nq % 2]
                            pm_ = psm[nq % 2]
                            nkb = 4 * (qc + 1)
                            def s_stage(kb, idx):
                                ksl = slice(kb * 128, (kb + 1) * 128)
                                s_ = pS[idx % 2]
                                mm(s_, s_[:], kn_[:, ksl], qn_[:, qsl], True, False, [kn_.b, qn_.b])
                                mm(s_, s_[:], krr[0:64, ksl], qr_[0:64, qsl], False, True, [krr.b, qr_.b])
                                return s_
                            nxt = s_stage(0, npt)
                            for kb in range(nkb):
                                s_ = nxt
                                t_ = pT[npt % 3]
                                npt += 1
                                if kb + 1 < nkb:
                                    nxt = s_stage(kb + 1, npt)
                                act(t_, t_[:], s_[:], AF.Exp, [s_.b], scale=SCALE)
                                if kb >= 4 * qc:
                                    j = kb - 4 * qc
                                    tt(t_, t_[:], t_[:], maskA[:, j * 512:(j + 1) * 512], ALU.mult, [t_.b, maskA.b])
                                mm(po_, po_[:], v_[:, kb, :], t_[:], kb == 0, kb == nkb - 1, [v_.b, t_.b])
                                mm(pm_, pm_[:], ones[:], t_[:], kb == 0, kb == nkb - 1, [ones.b, t_.b])
                            rs_ = rsm[nq % 2]
                            y_ = yh[nq % 2]
                            nq += 1
                            P.op("dve", lambda e, rs_=rs_, pm_=pm_: e.reciprocal(out=rs_[:], in_=pm_[:]), r=[pm_.b], w=[rs_.b])
                            tt(y_, y_[:], po_[:], rs_[:], ALU.mult, [po_.b, rs_.b])
                            store("sp", y_, yT[512 + h * 128:512 + (h + 1) * 128, qsl], y_[:])
                    P.barrier()

        def phase_hg(l):
            HB = min(T, 1024)
            NHB = T // HB
            with ExitStack() as st:
                NC2 = T // 32
                nck = HB // 32
                qt_ = sb(st, "qt_", [128, T], BF16)
                kt_ = sb(st, "kt_", [128, T], BF16)
                qh_ = sb(st, "qh_", [128, T], F32)
                khT = sb(st, "khT", [32, NC2, 128], BF16)
                vt = sb(st, "vt", [32, NC2, 128], BF16)
                ebl = sb(st, "ebl", [128, NC2], F32)
                oT = sb(st, "oT", [128, T], F32)
                qtq = [Tile(qt_.t, "qt_") for _ in range(NHB)]
                ktq = [Tile(kt_.t, "kt_") for _ in range(NHB)]
                qhq = [Tile(qh_.t, "qh_") for _ in range(NHB)]
                khq = [Tile(khT.t, "khT") for _ in range(NHB)]
                ebq = [Tile(ebl.t, "ebl") for _ in range(NHB)]
                S32 = [sb(st, "S32%d" % i, [128, 128], F32) for i in range(4)]
                Zt = sb(st, "hZ", [128, HB], F32)
                Qt = sb(st, "hQ", [128, HB], F32)
                SGt = sb(st, "hSG", [128, HB], F32)
                Bt = sb(st, "hB", [128, HB], F32)
                Et = sb(st, "hE", [128, HB], F32)
                khb = sb(st, "khb", [128, HB], BF16)
                aTt = [sb(st, "aTt%d" % i, [32, 32], BF16) for i in range(3)]
                yo_ = sb(st, "hyo", [128, HB], BF16)
                ptk = ps(st, "ptk", (128, 2048), BF16)
                pa_ = [ps(st, "hpa%d" % i) for i in range(2)]
                pob = [ps(st, "hpo%d" % i) for i in range(2)]
                pst = [ps(st, "hpst%d" % i) for i in range(2)]

                def prep_gen(h, hb):
                    lb_ap = lbt[:, h, l:l + 1]
                    oml_ap = omlt[:, h, l:l + 1]
                    noml_ap = nomlt[:, h, l:l + 1]
                    hsl = slice(hb * HB, (hb + 1) * HB)
                    load("sp", Zt, Zt[:], projT[2048 + h * 128:2048 + (h + 1) * 128, hsl])
                    load("sp", Qt, Qt[:], projT[1536 + h * 128:1536 + (h + 1) * 128, hsl])
                    act(SGt, SGt[:], Zt[:], AF.Sigmoid, [Zt.b])
                    yield
                    ts(Zt, Zt[:], SGt[:], oml_ap, lb_ap, ALU.mult, ALU.add, [SGt.b, omlt.b, lbt.b])
                    yield
                    ts(SGt, SGt[:], SGt[:], noml_ap, oml_ap, ALU.mult, ALU.add, [SGt.b, omlt.b, nomlt.b])
                    act(Zt, Zt[:], Zt[:], AF.Ln, [Zt.b])
                    yield
                    for tb in range(HB // 512):
                        bsl = slice(tb * 512, (tb + 1) * 512)
                        P.op("dve", lambda e, bsl=bsl: e.tensor_tensor_scan(out=Bt[:, bsl], data0=rst32[:], data1=Zt[:, bsl], initial=0.0, op0=ALU.mult, op1=ALU.add), r=[rst32.b, Zt.b], w=[Bt.b])
                        yield
                    act(Qt, Qt[:], Qt[:], AF.Silu, [Qt.b])
                    b3 = Bt[:].rearrange("p (n j) -> p n j", j=32)
                    c3 = Zt[:].rearrange("p (n j) -> p n j", j=32)
                    tt(Zt, c3, b3, b3[:, :, 15:16].broadcast_to([128, nck, 32]), ALU.subtract, [Bt.b], eng="pool")
                    act(Et, Et[:], Zt[:], AF.Exp, [Zt.b])
                    yield
                    tt(qtq[hb], qt_[:, hsl], Qt[:], Et[:], ALU.mult, [Qt.b, Et.b])
                    act(Et, Et[:], Zt[:], AF.Exp, [Zt.b], scale=-1.0)
                    yield
                    tt(ktq[hb], kt_[:, hsl], SGt[:], Et[:], ALU.mult, [SGt.b, Et.b])
                    act(Et, Et[:], Bt[:], AF.Exp, [Bt.b])
                    yield
                    tt(qhq[hb], qh_[:, hsl], Qt[:], Et[:], ALU.mult, [Qt.b, Et.b])
                    tt(Zt, c3, b3[:, :, 31:32].broadcast_to([128, nck, 32]), b3, ALU.subtract, [Bt.b], eng="pool")
                    act(Et, Et[:], Zt[:], AF.Exp, [Zt.b])
                    yield
                    tt(khb, khb[:], SGt[:], Et[:], ALU.mult, [SGt.b, Et.b])
                    act(ebq[hb], ebl[:, hb * nck:(hb + 1) * nck], Bt[:, 31:HB:32], AF.Exp, [Bt.b])
                    yield
                    for n16 in range(nck // 16):
                        for n8 in range(16):
                            n = n16 * 16 + n8
                            P.op("pe", lambda e, n8=n8, n=n: e.transpose(ptk[0:32, n8 * 128:(n8 + 1) * 128], khb[:, n * 32:(n + 1) * 32], ident[:]), r=[khb.b, ident.b], w=[ptk.b])
                        n0 = hb * nck + n16 * 16
                        cp(khq[hb], khT[:, n0:n0 + 16, :], ptk[0:32, :].rearrange("p (n d) -> p n d", d=128), [ptk.b])
                        yield

                for h in range(4):
                    for v0 in range(0, NC2, 32):
                        v1 = min(NC2, v0 + 32)
                        P.op("sp", lambda e, v0=v0, v1=v1, h=h: e.dma_start(out=vt[:, v0:v1, :], in_=vtok[v0 * 32:v1 * 32, h * 128:(h + 1) * 128].rearrange("(n s) d -> s n d", s=32)), w=[vt.b], dsem="Lvt")
                    for _ in prep_gen(h, 0):
                        pass
                    mset(S32[0], S32[0][:], 0.0)
                    R = len(S32)

                    def o_stage(n):
                        q = n // nck
                        nsl = slice(n * 32, (n + 1) * 32)
                        pb = pob[(n // 16) % 2]
                        osl = slice((n % 16) * 32, (n % 16 + 1) * 32)
                        at = aTt[n % 3]
                        mm(pb, pb[:, osl], vt[:, n, :], at[:], True, False, [vt.b, at.b])
                        mm(pb, pb[:, osl], S32[n % R][:], qh_[:, nsl], False, True, [S32[n % R].b, qhq[q].b])
                        if n % 16 == 15:
                            act(oT, oT[:, (n - 15) * 32:(n + 1) * 32], pb[:], AF.Copy, [pb.b])

                    for q in range(NHB):
                        pg = prep_gen(h, q + 1) if q + 1 < NHB else None
                        for n in range(q * nck, (q + 1) * nck):
                            nsl = slice(n * 32, (n + 1) * 32)
                            s_ = pst[n % 2]
                            mm(s_, s_[:, 0:128], khT[:, n, :], vt[:, n, :], True, True, [khq[q].b, vt.b])
                            stt(S32[(n + 1) % R], S32[(n + 1) % R][:], S32[n % R][:], ebl[:, n:n + 1], s_[:, 0:128], ALU.mult, ALU.add, [S32[n % R].b, ebq[q].b, s_.b])
                            a_ = pa_[n % 2]
                            at = aTt[n % 3]
                            mm(a_, a_[0:32, 0:32], kt_[:, nsl], qt_[:, nsl], True, True, [ktq[q].b, qtq[q].b])
                            tt(at, at[:], a_[0:32, 0:32], mask64[0:32, 0:32], ALU.mult, [a_.b, mask64.b])
                            if n >= 1:
                                o_stage(n - 1)
                            if pg is not None and n % 2 == 1:
                                next(pg, None)
                        if pg is not None:
                            for _ in pg:
                                pass
                    o_stage(NC2 - 1)
                    for hb in range(NHB):
                        hsl = slice(hb * HB, (hb + 1) * HB)
                        load("sp", Zt, Zt[:], projT[2560 + h * 128:2560 + (h + 1) * 128, hsl])
                        act(khb, khb[:], oT[:, hsl], AF.Square, [oT.b])
                        for tb in range(HB // 512):
                            bsl = slice(tb * 512, (tb + 1) * 512)
                            pn_ = pa_[tb % 2]
                            mm(pn_, pn_[:], ones[:], khb[:, bsl], True, True, [ones.b, khb.b])
                            act(Et, Et[:, bsl], pn_[:], AF.Sqrt, [pn_.b, cstf.b], bias=epsc, scale=1.0 / 128)
                        P.op("dve", lambda e: e.reciprocal(out=Et[:], in_=Et[:]), r=[Et.b], w=[Et.b])
                        tt(Et, Et[:], Et[:], oT[:, hsl], ALU.mult, [Et.b, oT.b])
                        act(Zt, Zt[:], Zt[:], AF.Silu, [Zt.b])
                        stt(yo_, yo_[:], Et[:], pv[:, l, 6:7], Zt[:], ALU.mult, ALU.mult, [Et.b, pv.b, Zt.b])
                        store("sp", yo_, yT[1536 + h * 128:1536 + (h + 1) * 128, hsl], yo_[:])
                P.barrier()

        def phase_c(l):
            with ExitStack() as st:
                wo = sb(st, "wo", [128, 16, 2048], BF16)
                G1 = sb(st, "G1", [128, D], F32)
                A2 = sb(st, "A2", [128, D], F32)
                B2 = sb(st, "B2", [128, D], F32)
                yTt = [sb(st, "yTt%d" % i, [128, 16, 128], BF16) for i in range(2)]
                xt = [sb(st, "cxt%d" % i, [128, D], F32) for i in range(2)]
                tmp = sb(st, "ctmp", [128, D], F32)
                tmp2 = sb(st, "ctmp2", [128, D], F32)
                hb = [sb(st, "chb%d" % i, [128, D], BF16) for i in range(2)]
                junk = sb(st, "cjunk", [128, D], BF16)
                h2s = [sb(st, "h2s%d" % i, [128, 16, 128], BF16) for i in range(2)]
                sst = [(sb(st, "css%d" % i, [128, 1], F32), sb(st, "crs%d" % i, [128, 1], F32)) for i in range(4)]
                pmx = ps(st, "pmx", (128, 2048), F32)
                ptr = [ps(st, "cptr%d" % i, (128, 2048), BF16) for i in range(2)]
                for cg in range(4):
                    P.op("pool", lambda e, cg=cg: e.dma_start(out=wo[:, :, cg * 512:(cg + 1) * 512], in_=w_out[l, :, cg * 512:(cg + 1) * 512].rearrange("(k p) n -> p k n", p=128)), w=[wo.b], dsem="Lwo")
                load("sp", G1, G1[:], modt[l, MOD_G1])
                load("sp", A2, A2[:], modt[l, MOD_A2])
                load("sp", B2, B2[:], modt[l, MOD_B2])
                def OP(gt):
                    i = gt % 2
                    rows = slice(gt * 128, (gt + 1) * 128)
                    load("sp", yTt[i], yTt[i][:], yT[:, rows].rearrange("(k p) t -> p k t", p=128))
                    load("sp", xt[i], xt[i][:], xsrc_of[0][rows, :])
                    for ng_ in range(4):
                        for kc in range(16):
                            mm(pmx, pmx[:, ng_ * 512:(ng_ + 1) * 512], yTt[i][:, kc, :], wo[:, kc, ng_ * 512:(ng_ + 1) * 512], kc == 0, kc == 15, [yTt[i].b, wo.b])

                def CH1(gt):
                    i = gt % 2
                    rs = rms_rstd(sst[2 * i], pmx, pmx[:], junk, [pmx.b])
                    stt(tmp, tmp[:], pmx[:], rs[:], G1[:], ALU.mult, ALU.mult, [pmx.b, rs.b, G1.b])

                def CH2(gt):
                    i = gt % 2
                    rows = slice(gt * 128, (gt + 1) * 128)
                    tt(xt[i], xt[i][:], xt[i][:], tmp[:], ALU.add, [xt[i].b, tmp.b])
                    store("sp", xt[i], out[rows, :], xt[i][:])
                    rs2 = rms_rstd(sst[2 * i + 1], xt[i], xt[i][:], junk, [xt[i].b])
                    stt(tmp2, tmp2[:], xt[i][:], rs2[:], A2[:], ALU.mult, ALU.mult, [xt[i].b, rs2.b, A2.b])
                    tt(hb[i], hb[i][:], tmp2[:], B2[:], ALU.add, [tmp2.b, B2.b])
                    p_ = ptr[i]
                    for kc in range(16):
                        P.op("pe", lambda e, kc=kc, p_=p_, i=i: e.transpose(p_[:, kc * 128:(kc + 1) * 128], hb[i][:, kc * 128:(kc + 1) * 128], ident[:]), r=[hb[i].b, ident.b], w=[p_.b])
                    act(h2s[i], h2s[i][:], p_[:].rearrange("p (k t) -> p k t", k=16), AF.Copy, [p_.b])
                    store("sp", h2s[i], h2T[:, rows].rearrange("(k p) t -> p k t", p=128), h2s[i][:])

                OP(0)
                for gt in range(NT):
                    CH1(gt)
                    if gt + 1 < NT:
                        OP(gt + 1)
                    CH2(gt)
                P.barrier()
            if stop == "C1":
                return
            SUB = min(1024, SBT)
            with ExitStack() as st:
                h2 = sb(st, "h2", [128, 16, SBT], BF16)
                wg_ = [sb(st, "wg_%d" % i, [128, 16, 256], BF16) for i in range(2)]
                wv_ = [sb(st, "wv_%d" % i, [128, 16, 256], BF16) for i in range(2)]
                rawg = [sb(st, "rawg%d" % i, [128, 2 + SBT], F32) for i in range(2)]
                rawv = [sb(st, "rawv%d" % i, [128, 2 + SBT], F32) for i in range(2)]
                cg_ = [sb(st, "cg_%d" % i, [128, SUB], F32) for i in range(2)]
                cv_ = [sb(st, "cv_%d" % i, [128, SUB], F32) for i in range(2)]
                ao = [sb(st, "ao%d" % i, [128, SUB], BF16) for i in range(2)]
                halo = sb(st, "halo", [128, 86, 2], F32)
                cvp = sb(st, "cvp", [128, 86, 4], F32)
                pg_ = [ps(st, "fpg%d" % i, (128, SUB), F32) for i in range(2)]
                pv_ = [ps(st, "fpv%d" % i, (128, SUB), F32) for i in range(2)]
                load("sp", cvp, cvp[:], convp[:, l])
                it = 0
                for sbi in range(NSB):
                    load("sp", h2, h2[:], h2T[:, sbi * SBT:(sbi + 1) * SBT].rearrange("(k p) t -> p k t", p=128))
                    for jg in range(22):
                        ncol = 256 if jg < 21 else 128
                        wgt, wvt = wg_[jg % 2], wv_[jg % 2]
                        load("pool", wgt, wgt[:, :, 0:ncol], w_up[l, :, jg * 256:jg * 256 + ncol].rearrange("(k p) n -> p k n", p=128))
                        load("pool", wvt, wvt[:, :, 0:ncol], w_up[l, :, DFF + jg * 256:DFF + jg * 256 + ncol].rearrange("(k p) n -> p k n", p=128))
                        for jj in range(ncol // 128):
                            j = jg * 2 + jj
                            rg, rv = rawg[j % 2], rawv[j % 2]
                            if sbi == 0:
                                mset(rg, rg[:, 0:2], 0.0)
                                mset(rv, rv[:, 0:2], 0.0)
                            else:
                                cp(rg, rg[:, 0:2], halo[:, j, :], [halo.b])
                                cp(rv, rv[:, 0:2], halo[:, 43 + j, :], [halo.b])
                            for sub in range(SBT // SUB):
                                pg, pvl = pg_[it % 2], pv_[it % 2]
                                cg, cv, a_ = cg_[it % 2], cv_[it % 2], ao[it % 2]
                                it += 1
                                off = sub * SUB
                                for t5 in range(SUB // 512):
                                    for kc in range(16):
                                        mm(pg, pg[:, t5 * 512:(t5 + 1) * 512], wgt[:, kc, jj * 128:(jj + 1) * 128], h2[:, kc, off + t5 * 512:off + (t5 + 1) * 512], kc == 0, kc == 15, [wgt.b, h2.b])
                                for t5 in range(SUB // 512):
                                    for kc in range(16):
                                        mm(pvl, pvl[:, t5 * 512:(t5 + 1) * 512], wvt[:, kc, jj * 128:(jj + 1) * 128], h2[:, kc, off + t5 * 512:off + (t5 + 1) * 512], kc == 0, kc == 15, [wvt.b, h2.b])
                                act(rg, rg[:, 2 + off:2 + off + SUB], pg[:], AF.Copy, [pg.b])
                                act(rv, rv[:, 2 + off:2 + off + SUB], pvl[:], AF.Copy, [pvl.b])
                                act(cg, cg[:], pg[:], AF.Identity, [pg.b, cvp.b], bias=cvp[:, j, 3:4], scale=cvp[:, j, 2:3])
                                act(cv, cv[:], pvl[:], AF.Identity, [pvl.b, cvp.b], bias=cvp[:, 43 + j, 3:4], scale=cvp[:, 43 + j, 2:3])
                                stt(cg, cg[:], rg[:, 1 + off:1 + off + SUB], cvp[:, j, 1:2], cg[:], ALU.mult, ALU.add, [rg.b, cvp.b, cg.b])
                                stt(cg, cg[:], rg[:, off:off + SUB], cvp[:, j, 0:1], cg[:], ALU.mult, ALU.add, [rg.b, cvp.b, cg.b])
                                stt(cv, cv[:], rv[:, 1 + off:1 + off + SUB], cvp[:, 43 + j, 1:2], cv[:], ALU.mult, ALU.add, [rv.b, cvp.b, cv.b])
                                stt(cv, cv[:], rv[:, off:off + SUB], cvp[:, 43 + j, 0:1], cv[:], ALU.mult, ALU.add, [rv.b, cvp.b, cv.b])
                                act(cg, cg[:], cg[:], AF.Gelu_apprx_tanh, [cg.b])
                                tt(a_, a_[:], cg[:], cv[:], ALU.mult, [cg.b, cv.b])
                                c0 = sbi * SBT + off
                                store("sp", a_, aT[c0 // 128:(c0 + SUB) // 128, :, j, :].rearrange("n p t -> p n t"), a_[:].rearrange("p (n t) -> p n t", t=128))
                            if sbi < NSB - 1:
                                cp(halo, halo[:, j, :], rg[:, SBT:SBT + 2], [rg.b])
                                cp(halo, halo[:, 43 + j, :], rv[:, SBT:SBT + 2], [rv.b])
                P.barrier()
            if stop == "C2":
                return
            st0 = ExitStack()
            ssq = sb(st0, "ssq", [128, NT, 4], F32)
            with ExitStack() as st:
                wd = [sb(st, "wd%d" % i, [128, NCH, 512], BF16) for i in range(2)]
                at_ = [sb(st, "at_%d" % i, [128, NCH, 128], BF16) for i in range(3)]
                ys = [sb(st, "ys%d" % i, [128, 512], F32) for i in range(2)]
                junk = sb(st, "djunk", [128, 512], BF16)
                pd = [ps(st, "pd%d" % i) for i in range(2)]
                mg = None
                if l + 1 < L:
                    pmm = [ps(st, "pmm%d" % i) for i in range(2)]
                    mg = mod_gen(l + 1, st, pmm)

                def load_wd(ng_):
                    for k0 in range(0, NCH, 22):
                        k1 = min(NCH, k0 + 22)
                        P.op("pool", lambda e, k0=k0, k1=k1, ng_=ng_: e.dma_start(out=wd[ng_ % 2][:, k0:k1, :], in_=w_down[l, k0 * 128:k1 * 128, ng_ * 512:(ng_ + 1) * 512].rearrange("(k p) n -> p k n", p=128)), w=[wd[ng_ % 2].b], dsem="Lwd%d" % (ng_ % 2))

                load_wd(0)
                it = 0
                NIT = 4 * NT
                for j in range(min(2, NIT)):
                    load("sp", at_[j % 3], at_[j % 3][:], aT[j % NT])
                for ng_ in range(4):
                    if ng_ + 1 < 4:
                        load_wd(ng_ + 1)
                    w_ = wd[ng_ % 2]
                    for gt in range(NT):
                        i = it % 2
                        a3 = at_[it % 3]
                        if it + 2 < NIT:
                            load("sp", at_[(it + 2) % 3], at_[(it + 2) % 3][:], aT[(it + 2) % NT])
                        it += 1
                        rows = slice(gt * 128, (gt + 1) * 128)
                        p_ = pd[i]
                        for kc in range(NCH):
                            mm(p_, p_[:], a3[:, kc, :], w_[:, kc, :], kc == 0, kc == NCH - 1, [a3.b, w_.b])
                        act(ys[i], ys[i][:], p_[:], AF.Copy, [p_.b])
                        act(junk, junk[:], p_[:], AF.Square, [p_.b], accum=ssq[:, gt, ng_:ng_ + 1], extra_w=[ssq.b])
                        store("sp", ys[i], ydn[rows, ng_ * 512:(ng_ + 1) * 512], ys[i][:])
                        if mg is not None and ng_ >= 1:
                            next(mg, None)
                if mg is not None:
                    for _ in mg:
                        pass
                P.barrier()
            if stop == "C3a":
                st0.close()
                return
            with ExitStack() as st:
                G2 = sb(st, "G2", [128, D], F32)
                yt = [sb(st, "dyt%d" % i, [128, D], F32) for i in range(3)]
                xt = [sb(st, "dxt%d" % i, [128, D], F32) for i in range(3)]
                rs = [sb(st, "drs%d" % i, [128, 1], F32) for i in range(3)]
                load("sp", G2, G2[:], modt[l, MOD_G2])

                def ld(gt):
                    i = gt % 3
                    rows = slice(gt * 128, (gt + 1) * 128)
                    load("sp", yt[i], yt[i][:], ydn[rows, :])
                    load("sp", xt[i], xt[i][:], out[rows, :])

                ld(0)
                if NT > 1:
                    ld(1)
                for gt in range(NT):
                    i = gt % 3
                    rows = slice(gt * 128, (gt + 1) * 128)
                    if gt + 2 < NT:
                        ld(gt + 2)
                    P.op("dve", lambda e, i=i, gt=gt: e.reduce_sum(out=rs[i][:], in_=ssq[:, gt, :], axis=AX.X), r=[ssq.b], w=[rs[i].b])
                    act(rs[i], rs[i][:], rs[i][:], AF.Sqrt, [rs[i].b, cstf.b], bias=epsc, scale=1.0 / D)
                    P.op("dve", lambda e, i=i: e.reciprocal(out=rs[i][:], in_=rs[i][:]), r=[rs[i].b], w=[rs[i].b])
                    stt(yt[i], yt[i][:], yt[i][:], rs[i][:], G2[:], ALU.mult, ALU.mult, [yt[i].b, rs[i].b, G2.b])
                    tt(xt[i], xt[i][:], xt[i][:], yt[i][:], ALU.add, [xt[i].b, yt[i].b])
                    store("sp", xt[i], out[rows, :], xt[i][:])
                P.barrier()
            st0.close()

        xsrc_of = [x_in]
        for l in range(L):
            xsrc = x_in if l == 0 else out
            xsrc_of[0] = xsrc
            with ExitStack() as st:
                A1 = sb(st, "A1", [128, D], F32)
                B1 = sb(st, "B1", [128, D], F32)
                hT = sb(st, "hT", [128, 16, SBT], BF16)
                xt = [sb(st, "xt%d" % i, [128, D], F32) for i in range(2)]
                junk = sb(st, "junk", [128, D], BF16)
                hb = [sb(st, "hb%d" % i, [128, D], BF16) for i in range(2)]
                sst = [(sb(st, "ss%d" % i, [128, 1], F32), sb(st, "rs%d" % i, [128, 1], F32)) for i in range(2)]
                wb = [sb(st, "wb%d" % i, [128, 16, 512], BF16) for i in range(2)]
                stg = [sb(st, "stg%d" % i, [128, 512], F32) for i in range(3)]
                vst = [sb(st, "vst%d" % i, [128, 512], BF16) for i in range(2)]
                ptr = [ps(st, "ptr%d" % i, (128, 2048), BF16) for i in range(2)]
                pa = [ps(st, "pa%d" % i) for i in range(4)]
                load("sp", A1, A1[:], modt[l, MOD_A1])
                load("sp", B1, B1[:], modt[l, MOD_B1])
                nst = 0
                npa = 0
                nv = 0
                for sbi in range(NSB):
                    for ti in range(SBT // 128):
                        gt = sbi * (SBT // 128) + ti
                        x_ = xt[gt % 2]
                        load("sp", x_, x_[:], xsrc[gt * 128:(gt + 1) * 128, :])
                        rs = rms_rstd(sst[gt % 2], x_, x_[:], junk, [x_.b])
                        h_ = hb[gt % 2]
                        stt(x_, x_[:], x_[:], rs[:], A1[:], ALU.mult, ALU.mult, [x_.b, rs.b, A1.b])
                        tt(h_, h_[:], x_[:], B1[:], ALU.add, [x_.b, B1.b])
                        p_ = ptr[gt % 2]
                        for kc in range(16):
                            P.op("pe", lambda e, kc=kc: e.transpose(p_[:, kc * 128:(kc + 1) * 128], h_[:, kc * 128:(kc + 1) * 128], ident[:]), r=[h_.b, ident.b], w=[p_.b])
                        act(hT, hT[:, :, ti * 128:(ti + 1) * 128], p_[:].rearrange("p (k t) -> p k t", k=16), AF.Copy, [p_.b])
                    for mg in range(7):
                        wt = wb[mg % 2]
                        load("pool", wt, wt[:], w_in[l, :, mg * 512:(mg + 1) * 512].rearrange("(k p) n -> p k n", p=128))
                        if mg < 6:
                            nm = 3 if mg == 2 else 4
                            for tb in range(SBT // 512):
                                for m in range(nm):
                                    p_ = pa[npa % 4]
                                    npa += 1
                                    for kc in range(16):
                                        mm(p_, p_[:], wt[:, kc, m * 128:(m + 1) * 128], hT[:, kc, tb * 512:(tb + 1) * 512], kc == 0, kc == 15, [wt.b, hT.b])
                                    s_ = stg[nst % 3]
                                    nst += 1
                                    if nst % 2:
                                        act(s_, s_[:], p_[:], AF.Copy, [p_.b])
                                    else:
                                        cp(s_, s_[:], p_[:], [p_.b])
                                    row = mg * 512 + m * 128
                                    c0 = sbi * SBT + tb * 512
                                    store("sp", s_, projT[row:row + 128, c0:c0 + 512], s_[:])
                        else:
                            for ti in range(SBT // 128):
                                p_ = pa[npa % 4]
                                npa += 1
                                for kc in range(16):
                                    mm(p_, p_[:], hT[:, kc, ti * 128:(ti + 1) * 128], wt[:, kc, :], kc == 0, kc == 15, [wt.b, hT.b])
                                v_ = vst[nv % 2]
                                nv += 1
                                act(v_, v_[:], p_[:], AF.Copy, [p_.b])
                                t0 = sbi * SBT + ti * 128
                                store("sp", v_, vtok[t0:t0 + 128, :], v_[:])
                P.barrier()
            if stop == "A":
                continue
            phase_s5(l)
            if stop == "S5":
                continue
            phase_mla(l)
            if stop == "MLA":
                continue
            phase_hg(l)
            if stop == "B":
                continue
            phase_c(l)
        P.barrier(["sp"])
    return nc


def _consts():
    c = np.zeros((128, 1024), np.float32)
    c[:, 0:128] = np.eye(128, dtype=np.float32)
    perm = np.zeros((128, 128), np.float32)
    for i in range(64):
        perm[i, 64 + i] = 1.0
        perm[64 + i, i] = 1.0
    c[:, 128:256] = perm
    s = np.arange(64)
    c[0:64, 256:320] = (s[:, None] <= s[None, :]).astype(np.float32)
    c[0:64, 320] = 1.0
    c[64:128, 320] = -1.0
    invf = 1.0 / (10000.0 ** (np.arange(0, 64, 2, dtype=np.float32) / 64.0))
    c[0:32, 321] = invf
    c[32:64, 321] = invf
    c[0:32, 322] = -1.0
    c[32:64, 322] = 1.0
    c[:, 323] = EPS
    c[:, 384:448] = np.arange(64, dtype=np.float32)[None, :]
    k = np.arange(128)[:, None]
    q = np.arange(512)[None, :]
    mA = np.concatenate([(q >= j * 128 + k).astype(np.float32) for j in range(4)], axis=1)
    return c, mA


def prep_shared(inp, L):
    f = lambda a: np.ascontiguousarray(np.asarray(a, dtype=np.float32))
    w_in = f(inp["w_in"])
    wa = np.zeros((L, D, WIN), np.float32)
    wa[:, :, 0:1280] = w_in[:, :, 0:1280]
    wa[:, :, 1280:1344] = w_in[:, :, 1280:1344]
    wa[:, :, 1344:1376] = w_in[:, :, 1312:1344]
    wa[:, :, 1376:1408] = w_in[:, :, 1280:1312]
    wa[:, :, 1536:2048] = w_in[:, :, 1344:1856]
    wa[:, :, 2048:2560] = w_in[:, :, 1856:2368]
    wa[:, :, 2560:3072] = w_in[:, :, 2880:3392]
    wa[:, :, 3072:3584] = w_in[:, :, 2368:2880]
    wuq = f(inp["mla_w_uq"]).reshape(L, 512, 8, 192)
    wq = np.zeros((L, 512, 8, 256), np.float32)
    wq[..., 0:192] = wuq
    wq[..., 192:224] = wuq[..., 160:192]
    wq[..., 224:256] = wuq[..., 128:160]
    norms = np.stack([f(inp["mix_pre_norm"]), f(inp["mix_post_norm"]), f(inp["ffn_pre_norm"]), f(inp["ffn_post_norm"])], axis=1)
    pvec = np.zeros((128, L, 16), np.float32)
    pvec[:, :, 0:4] = f(inp["mla_q_norm"]).reshape(L, 4, 128).transpose(2, 0, 1)
    pvec[:, :, 4:6] = f(inp["mla_kv_norm"]).reshape(L, 2, 128).transpose(2, 0, 1)
    pvec[:, :, 6] = f(inp["hg_out_norm"]).T
    pvec[:, :, 7:11] = f(inp["s5_d"]).reshape(L, 4, 128).transpose(2, 0, 1)
    convp = np.zeros((128, L, 86, 4), np.float32)
    convp[:, :, :, 0:3] = f(inp["ffn_conv_w"]).reshape(L, 3, 86, 128).transpose(3, 0, 2, 1)
    convp[:, :, :, 3] = f(inp["ffn_conv_b"]).reshape(L, 86, 128).transpose(2, 0, 1)
    lre, lim, ldt = f(inp["s5_lambda_re"]), f(inp["s5_lambda_im"]), f(inp["s5_log_dt"])
    ldt_e = np.broadcast_to(ldt[:, :, None], (L, 32, 64))
    s5lam = np.stack([lre.reshape(L, 2048), lim.reshape(L, 2048), np.ascontiguousarray(ldt_e).reshape(L, 2048)], axis=1)
    s5lamp = np.zeros((128, L, 3, 32), np.float32)
    for i, a in enumerate((lre, lim, ldt_e)):
        t = np.asarray(a).transpose(2, 0, 1)
        s5lamp[0:64, :, i, :] = t
        s5lamp[64:128, :, i, :] = t
    s5b = np.zeros((L, 2, 128, 32, 64), np.float32)
    s5c = np.zeros((L, 2, 128, 32, 128), np.float32)
    bre, bim = f(inp["s5_b_re"]), f(inp["s5_b_im"])
    cre, cim = f(inp["s5_c_re"]), f(inp["s5_c_im"])
    for g in range(32):
        g8 = g % 8
        s5b[:, 0, g8 * 16:(g8 + 1) * 16, g, :] = bre[:, g].transpose(0, 2, 1)
        s5b[:, 1, g8 * 16:(g8 + 1) * 16, g, :] = bim[:, g].transpose(0, 2, 1)
        s5c[:, 0, 0:64, g, g8 * 16:(g8 + 1) * 16] = cre[:, g].transpose(0, 2, 1)
        s5c[:, 0, 64:128, g, g8 * 16:(g8 + 1) * 16] = cim[:, g].transpose(0, 2, 1)
        s5c[:, 1, 0:64, g, g8 * 16:(g8 + 1) * 16] = cim[:, g].transpose(0, 2, 1)
        s5c[:, 1, 64:128, g, g8 * 16:(g8 + 1) * 16] = cre[:, g].transpose(0, 2, 1)
    lbl = f(inp["hg_lb_logits"]).reshape(L, 4, 128).transpose(2, 1, 0)
    cst, mA = _consts()
    return {
        "w_in": wa, "w_out": f(inp["w_out"]), "w_up": f(inp["ffn_w_up"]), "w_down": f(inp["ffn_w_down"]),
        "w_ada": f(inp["w_ada"]), "b_ada": f(inp["b_ada"]), "w_uq": np.ascontiguousarray(wq.reshape(L, 512, 2048)),
        "w_ukv": f(inp["mla_w_ukv"]), "w_glu": f(inp["s5_w_glu"]), "norms": np.ascontiguousarray(norms),
        "pvec": pvec, "convp": convp, "s5lam": np.ascontiguousarray(s5lam), "s5lamp": s5lamp,
        "s5b": np.ascontiguousarray(s5b.reshape(L, 2, 128, 2048)), "s5c": np.ascontiguousarray(s5c.reshape(L, 2, 128, 4096)),
        "lbl": np.ascontiguousarray(lbl), "cst": cst, "maskA": mA,
    }


def prep_core(inp, b):
    x = np.ascontiguousarray(np.asarray(inp["x"][b], dtype=np.float32))
    pos = np.ascontiguousarray(np.asarray(inp["positions"][b], dtype=np.int32)).reshape(1, -1)
    cT = np.ascontiguousarray(np.asarray(inp["c"][b], dtype=np.float32).reshape(16, 128).T)
    return {"x": x, "pos": pos, "cT": cT}


def kernel(**inputs):
    B, T, _ = inputs["x"].shape
    L = inputs["w_in"].shape[0]
    nc = build(T, L)
    shared = prep_shared(inputs, L)
    in_maps = []
    for b in range(B):
        m = dict(shared)
        m.update(prep_core(inputs, b))
        in_maps.append(m)
    res = run_bass_kernel_spmd(nc, in_maps, core_ids=list(range(B)))
    return np.stack([np.asarray(r["out"], dtype=np.float32) for r in res.results], axis=0)
```

```python
import math
from contextlib import ExitStack
import numpy as np
import concourse.bass as bass
import concourse.mybir as mybir
from concourse.bass_utils import run_bass_kernel_spmd

F32 = mybir.dt.float32
BF16 = mybir.dt.bfloat16
I32 = mybir.dt.int32
AF = mybir.ActivationFunctionType
ALU = mybir.AluOpType
AX = mybir.AxisListType

D = 2048
DFF = 5504
NCH = 43
WIN = 3584
EPS = 1e-6
TWO_PI = 2.0 * math.pi
C1_2PI = 6.28125
C2_2PI = TWO_PI - 6.28125

COMPUTE = ("pe", "act", "dve", "pool")


class Buf:
    __slots__ = ("w", "r")

    def __init__(self):
        self.w = {}
        self.r = {}


class Tile:
    def __init__(self, t, name):
        self.t = t
        self.b = Buf()
        self.name = name

    def __getitem__(self, k):
        return self.t[k]


class Prog:
    def __init__(self, nc, es):
        self.nc = nc
        self.es = es
        self.eh = {"pe": nc.tensor, "act": nc.scalar, "dve": nc.vector, "pool": nc.gpsimd, "sp": nc.sync}
        self.clock = {e: {} for e in self.eh}
        self.val = {}
        self.sems = {}
        self.n_ins = 0

    def sem(self, key):
        s = self.sems.get(key)
        if s is None:
            s = self.es.enter_context(self.nc.semaphore("s_" + key.replace(":", "_")))
            self.sems[key] = s
        return s

    def op(self, eng, fn, r=(), w=(), dsem=None):
        deps = {}
        war = {}
        for b in r:
            for k, v in b.w.items():
                if deps.get(k, 0) < v:
                    deps[k] = v
        for b in w:
            for k, v in b.w.items():
                if deps.get(k, 0) < v:
                    deps[k] = v
            for k, v in b.r.items():
                if war.get(k, 0) < v:
                    war[k] = v
        if dsem is None and eng == "pe":
            war.pop("pe", None)
            deps.pop("pe", None)
        for k, v in war.items():
            if deps.get(k, 0) < v:
                deps[k] = v
        clk = self.clock[eng]
        e = self.eh[eng]
        for k, v in deps.items():
            if clk.get(k, 0) < v:
                e.wait_ge(self.sem(k), v)
                clk[k] = v
        ins = fn(e)
        self.n_ins += 1
        if dsem is None:
            key = eng
            val = self.val.get(key, 0) + 1
            ins.then_inc(self.sem(key), 1)
        else:
            key = "d:" + dsem
            val = self.val.get(key, 0) + 16
            ins.then_inc(self.sem(key), 16)
        self.val[key] = val
        for b in w:
            b.w = {key: val}
            b.r = {}
        for b in r:
            if b.r.get(key, 0) < val:
                b.r[key] = val
        return (key, val)

    def barrier(self, engines=None):
        for eng in (engines or self.eh):
            clk = self.clock[eng]
            e = self.eh[eng]
            for k, v in self.val.items():
                if clk.get(k, 0) < v:
                    e.wait_ge(self.sem(k), v)
                    clk[k] = v


def build(T, L, dbg=False, stop=None):
    NB = T // 512
    NT = T // 128
    SBT = min(T, 2048)
    NSB = T // SBT
    NCK = T // 64
    nc = bass.Bass("TRN2", target_bir_lowering=False)

    def din(name, shape, dt=F32):
        return nc.dram_tensor(name, list(shape), dt, kind="ExternalInput").ap()

    def dscr(name, shape, dt=F32):
        return nc.dram_tensor(name, list(shape), dt, kind=("ExternalOutput" if dbg else "Internal")).ap()

    x_in = din("x", [T, D])
    pos_in = din("pos", [1, T], I32)
    cT_in = din("cT", [128, 16])
    w_in = din("w_in", [L, D, WIN])
    w_out = din("w_out", [L, D, D])
    w_up = din("w_up", [L, D, 2 * DFF])
    w_down = din("w_down", [L, DFF, D])
    w_ada = din("w_ada", [L, D, 6 * D])
    b_ada = din("b_ada", [L, 6 * D])
    w_uq = din("w_uq", [L, 512, 2048])
    w_ukv = din("w_ukv", [L, 256, 2048])
    w_glu = din("w_glu", [L, 512, 512])
    norms = din("norms", [L, 4, D])
    pvec = din("pvec", [128, L, 16])
    convp = din("convp", [128, L, 86, 4])
    s5lam = din("s5lam", [L, 3, 2048])
    s5lamp = din("s5lamp", [128, L, 3, 32])
    s5b = din("s5b", [L, 2, 128, 2048])
    s5c = din("s5c", [L, 2, 128, 4096])
    lbl = din("lbl", [128, 4, L])
    cst = din("cst", [128, 1024])
    maskA_in = din("maskA", [128, 2048])
    out = nc.dram_tensor("out", [T, D], F32, kind="ExternalOutput").ap()

    modt = dscr("modt", [L, 6, 128, D])
    projT = dscr("projT", [3072, T])
    vtok = dscr("vtok", [T, 512], BF16)
    yT = dscr("yT", [D, T], BF16)
    h2T = dscr("h2T", [D, T], BF16)
    aT = dscr("aT", [T // 128, 128, NCH, 128], BF16)
    ydn = dscr("ydn", [T, D])
    ropeT = dscr("ropeT", [2, 64, T])

    with ExitStack() as es:
        P = Prog(nc, es)

        uid = [0]

        def sb(stack, name, shape, dt):
            uid[0] += 1
            return Tile(stack.enter_context(nc.sbuf_tensor("%s_u%d" % (name, uid[0]), list(shape), dt)), name)

        def ps(stack, name, shape=(128, 512), dt=F32):
            uid[0] += 1
            return Tile(stack.enter_context(nc.psum_tensor("%s_u%d" % (name, uid[0]), list(shape), dt)), name)

        def load(eng, dst, dst_ap, src_ap):
            return P.op(eng, lambda e: e.dma_start(out=dst_ap, in_=src_ap), w=[dst.b], dsem="L" + dst.name)

        def store(eng, src, dst_ap, src_ap):
            return P.op(eng, lambda e: e.dma_start(out=dst_ap, in_=src_ap), r=[src.b], dsem="S" + src.name)

        def mm(o, o_ap, lhsT_ap, rhs_ap, start, stop, r):
            return P.op("pe", lambda e: e.matmul(o_ap, lhsT_ap, rhs_ap, start=start, stop=stop), r=r, w=[o.b])

        def act(o, o_ap, i_ap, func, r, bias=None, scale=None, accum=None, extra_w=()):
            kw = {}
            if bias is not None:
                kw["bias"] = bias
            if scale is not None:
                kw["scale"] = scale
            if accum is not None:
                kw["accum_out"] = accum
            return P.op("act", lambda e: e.activation(out=o_ap, in_=i_ap, func=func, **kw), r=r, w=[o.b] + list(extra_w))

        def tt(o, o_ap, a_ap, b_ap, op, r, eng="dve"):
            return P.op(eng, lambda e: e.tensor_tensor(out=o_ap, in0=a_ap, in1=b_ap, op=op), r=r, w=[o.b])

        def ts(o, o_ap, a_ap, s1, s2, op0, op1, r, eng="dve"):
            if op1 is None:
                return P.op(eng, lambda e: e.tensor_scalar(out=o_ap, in0=a_ap, scalar1=s1, scalar2=None, op0=op0), r=r, w=[o.b])
            return P.op(eng, lambda e: e.tensor_scalar(out=o_ap, in0=a_ap, scalar1=s1, scalar2=s2, op0=op0, op1=op1), r=r, w=[o.b])

        def stt(o, o_ap, a_ap, s, b_ap, op0, op1, r):
            return P.op("dve", lambda e: e.scalar_tensor_tensor(out=o_ap, in0=a_ap, scalar=s, in1=b_ap, op0=op0, op1=op1), r=r, w=[o.b])

        def cp(o, o_ap, i_ap, r, eng="dve"):
            return P.op(eng, lambda e: e.tensor_copy(out=o_ap, in_=i_ap), r=r, w=[o.b])

        def mset(o, o_ap, v, eng="dve"):
            return P.op(eng, lambda e: e.memset(o_ap, v), w=[o.b])

        cstf = sb(es, "cstf", [128, 1024], F32)
        ident = sb(es, "ident", [128, 128], BF16)
        ones = sb(es, "ones", [128, 128], BF16)
        mask64 = sb(es, "mask64", [128, 64], BF16)
        rst = sb(es, "rst", [128, 512], F32)
        lbt = sb(es, "lbt", [128, 4, L], F32)
        omlt = sb(es, "omlt", [128, 4, L], F32)
        nomlt = sb(es, "nomlt", [128, 4, L], F32)
        pv = sb(es, "pv", [128, L, 16], F32)
        load("sp", cstf, cstf[:], cst)
        load("sp", pv, pv[:], pvec)
        cp(ident, ident[:], cstf[:, 0:128], [cstf.b])
        cp(mask64, mask64[:], cstf[:, 256:320], [cstf.b])
        mset(ones, ones[:], 1.0)
        mset(rst, rst[:], 1.0)
        mset(rst, rst[:, 0:512:64], 0.0)
        rst32 = sb(es, "rst32", [128, 512], F32)
        mset(rst32, rst32[:], 1.0)
        mset(rst32, rst32[:, 0:512:32], 0.0)
        permsw = cstf[:, 128:256]
        sgnB = cstf[:, 320:321]
        invf = cstf[0:64, 321:322]
        sgnr = cstf[0:64, 322:323]
        epsc = cstf[:, 323:324]

        def sincos(stack, nm, ang, shape, o_sin, o_sin_ap, o_cos, o_cos_ap, npart=128):
            k_i = sb(stack, nm + "_ki", shape, I32)
            k_f = sb(stack, nm + "_kf", shape, F32)
            r_t = sb(stack, nm + "_r", shape, F32)
            sl = tuple([slice(0, npart)] + [slice(None)] * (len(shape) - 1))
            for which in (0, 1):
                if which == 1:
                    ts(r_t, r_t[sl], ang[sl], math.pi / 2, None, ALU.add, None, [ang.b])
                    src = r_t
                else:
                    src = ang
                ts(k_f, k_f[sl], src[sl], 1.0 / TWO_PI, None, ALU.mult, None, [src.b])
                cp(k_i, k_i[sl], k_f[sl], [k_f.b])
                cp(k_f, k_f[sl], k_i[sl], [k_i.b])
                stt(r_t, r_t[sl], k_f[sl], -C1_2PI, src[sl], ALU.mult, ALU.add, [k_f.b, src.b])
                stt(r_t, r_t[sl], k_f[sl], -C2_2PI, r_t[sl], ALU.mult, ALU.add, [k_f.b, r_t.b])
                ts(r_t, r_t[sl], r_t[sl], -3.1415925, 3.1415925, ALU.max, ALU.min, [r_t.b])
                if which == 0:
                    act(o_sin, o_sin_ap, r_t[sl], AF.Sin, [r_t.b])
                else:
                    act(o_cos, o_cos_ap, r_t[sl], AF.Sin, [r_t.b])

        with ExitStack() as st:
            lg = sb(st, "lg", [128, 4, L], F32)
            ssum = sb(st, "ssum", [128, 4], F32)
            load("sp", lg, lg[:], lbl)
            act(lg, lg[:], lg[:], AF.Exp, [lg.b])
            P.op("dve", lambda e: e.reduce_sum(out=ssum[:], in_=lg[:], axis=AX.X), r=[lg.b], w=[ssum.b])
            P.op("dve", lambda e: e.reciprocal(out=ssum[:], in_=ssum[:]), r=[ssum.b], w=[ssum.b])
            tt(lg, lg[:], lg[:], ssum[:].unsqueeze(2).broadcast_to([128, 4, L]), ALU.mult, [lg.b, ssum.b])
            mset(lbt, lbt[:, :, 0:1], 0.0)
            for l in range(1, L):
                tt(lbt, lbt[:, :, l:l + 1], lbt[:, :, l - 1:l], lg[:, :, l:l + 1], ALU.add, [lbt.b, lg.b])
            ts(omlt, omlt[:], lbt[:], -1.0, 1.0, ALU.mult, ALU.add, [lbt.b])
            ts(nomlt, nomlt[:], omlt[:], -1.0, None, ALU.mult, None, [omlt.b])
            posi = sb(st, "posi", [64, T], I32)
            ang = sb(st, "ang", [64, T], F32)
            sint = sb(st, "sint", [64, T], F32)
            cost = sb(st, "cost", [64, T], F32)
            load("sp", posi, posi[:], pos_in[0, :].partition_broadcast(64))
            cp(ang, ang[:], posi[:], [posi.b])
            ts(ang, ang[:], ang[:], invf, None, ALU.mult, None, [ang.b, cstf.b])
            sincos(st, "rp", ang, [64, T], sint, sint[:], cost, cost[:], npart=64)
            ts(sint, sint[:], sint[:], sgnr, None, ALU.mult, None, [sint.b, cstf.b])
            store("sp", cost, ropeT[0], cost[:])
            store("sp", sint, ropeT[1], sint[:])
            P.barrier()

        cbc = sb(es, "cbc", [128, 16, 128], BF16)
        with ExitStack() as st:
            cTt = sb(st, "cTt", [128, 16], F32)
            onesf = sb(st, "onesf", [128, 128], F32)
            load("sp", cTt, cTt[:], cT_in)
            act(cTt, cTt[:], cTt[:], AF.Silu, [cTt.b])
            mset(onesf, onesf[:], 1.0)
            for kc in range(16):
                ts(cbc, cbc[:, kc, :], onesf[:], cTt[:, kc:kc + 1], None, ALU.mult, None, [onesf.b, cTt.b])
            P.barrier()

        def mod_gen(l, st, pm):
            wa = [sb(st, "wa%d" % i, [128, 16, 512], BF16) for i in range(2)]
            bb = [sb(st, "mbb%d" % i, [128, 512], F32) for i in range(2)]
            nn = [sb(st, "mnn%d" % i, [128, 512], F32) for i in range(2)]
            rr = [sb(st, "mrr%d" % i, [128, 512], F32) for i in range(2)]
            for cg in range(24):
                i = cg % 2
                sec = cg // 4
                c0 = (cg % 4) * 512
                wt = wa[i]
                load("pool", wt, wt[:], w_ada[l, :, cg * 512:(cg + 1) * 512].rearrange("(k p) n -> p k n", p=128))
                load("pool", bb[i], bb[i][:], b_ada[l, cg * 512:(cg + 1) * 512].partition_broadcast(128))
                kind = sec % 3
                if kind != 0:
                    nidx = (0 if kind == 1 else 1) + 2 * (sec // 3)
                    load("pool", nn[i], nn[i][:], norms[l, nidx, c0:c0 + 512].partition_broadcast(128))
                p_ = pm[i]
                for kc in range(16):
                    mm(p_, p_[:], cbc[:, kc, :], wt[:, kc, :], kc == 0, kc == 15, [cbc.b, wt.b])
                tt(rr[i], rr[i][:], p_[:], bb[i][:], ALU.add, [p_.b, bb[i].b])
                if kind == 1:
                    stt(rr[i], rr[i][:], rr[i][:], 1.0, nn[i][:], ALU.add, ALU.mult, [rr[i].b, nn[i].b])
                elif kind == 2:
                    tt(rr[i], rr[i][:], rr[i][:], nn[i][:], ALU.mult, [rr[i].b, nn[i].b])
                store("pool", rr[i], modt[l, sec, :, c0:c0 + 512], rr[i][:])
                yield cg

        with ExitStack() as st:
            pm0 = [ps(st, "pm%d" % i) for i in range(2)]
            for _ in mod_gen(0, st, pm0):
                pass
            P.barrier()
        MOD_B1, MOD_A1, MOD_G1, MOD_B2, MOD_A2, MOD_G2 = 0, 1, 2, 3, 4, 5

        def rms_rstd(stack_tiles, src, src_ap, junk, r):
            ss, rs = stack_tiles
            act(junk, junk[:], src_ap, AF.Square, r, accum=ss[:], extra_w=[ss.b])
            act(rs, rs[:], ss[:], AF.Sqrt, [ss.b, cstf.b], bias=epsc, scale=1.0 / D)
            P.op("dve", lambda e: e.reciprocal(out=rs[:], in_=rs[:]), r=[rs.b], w=[rs.b])
            return rs

        JT = cstf[:, 384:448]

        def phase_s5(l):
            with ExitStack() as st:
                TA = sb(st, "TA", [128, 32, 64], F32)
                TB = sb(st, "TB", [128, 32, 64], F32)
                TC = sb(st, "TC", [128, 32, 64], F32)
                TD = sb(st, "TD", [128, 32, 64], F32)
                Bb1 = sb(st, "Bb1", [128, 32, 128], BF16)
                Bb2 = sb(st, "Bb2", [128, 32, 128], BF16)
                Cp1 = sb(st, "Cp1", [128, 32, 128], BF16)
                Cp2 = sb(st, "Cp2", [128, 32, 128], BF16)
                K1 = sb(st, "K1", [128, 32], F32)
                K2 = sb(st, "K2", [128, 32], F32)
                with ExitStack() as s2:
                    lamf = sb(s2, "lamf", [128, 3, 2048], F32)
                    load("sp", lamf, lamf[:].rearrange("p a n -> p (a n)"), s5lam[l].rearrange("a n -> (a n)").partition_broadcast(128))
                    lr, li, ld = lamf[:, 0, :], lamf[:, 1, :], lamf[:, 2, :]
                    tl = [sb(s2, "tl%d" % i, [128, 2048], F32) for i in range(6)]
                    dtf, af, ef, sf, cf, phf = tl
                    act(dtf, dtf[:], ld, AF.Exp, [lamf.b])
                    tt(af, af[:], lr, dtf[:], ALU.mult, [lamf.b, dtf.b])
                    tt(phf, phf[:], li, dtf[:], ALU.mult, [lamf.b, dtf.b])
                    act(ef, ef[:], af[:], AF.Exp, [af.b])
                    sincos(s2, "sf", phf, [128, 2048], sf, sf[:], cf, cf[:])
                    tt(cf, cf[:], cf[:], ef[:], ALU.mult, [cf.b, ef.b])
                    ts(cf, cf[:], cf[:], -1.0, None, ALU.add, None, [cf.b])
                    tt(sf, sf[:], sf[:], ef[:], ALU.mult, [sf.b, ef.b])
                    tt(dtf, dtf[:], lr, lr, ALU.mult, [lamf.b])
                    tt(af, af[:], li, li, ALU.mult, [lamf.b])
                    tt(dtf, dtf[:], dtf[:], af[:], ALU.add, [dtf.b, af.b])
                    P.op("dve", lambda e: e.reciprocal(out=dtf[:], in_=dtf[:]), r=[dtf.b], w=[dtf.b])
                    tt(ef, ef[:], cf[:], lr, ALU.mult, [cf.b, lamf.b])
                    tt(af, af[:], sf[:], li, ALU.mult, [sf.b, lamf.b])
                    tt(ef, ef[:], ef[:], af[:], ALU.add, [ef.b, af.b])
                    tt(ef, ef[:], ef[:], dtf[:], ALU.mult, [ef.b, dtf.b])
                    tt(phf, phf[:], sf[:], lr, ALU.mult, [sf.b, lamf.b])
                    tt(af, af[:], cf[:], li, ALU.mult, [cf.b, lamf.b])
                    tt(phf, phf[:], phf[:], af[:], ALU.subtract, [phf.b, af.b])
                    tt(phf, phf[:], phf[:], dtf[:], ALU.mult, [phf.b, dtf.b])
                    cr3 = ef[:].rearrange("p (g q) -> p g q", q=64)
                    ci3 = phf[:].rearrange("p (g q) -> p g q", q=64)
                    braw = sb(s2, "braw", [128, 2, 2048], F32)
                    load("sp", braw, braw[:], s5b[l].rearrange("a k n -> k a n"))
                    bre3 = braw[:, 0, :].rearrange("p (g q) -> p g q", q=64)
                    bim3 = braw[:, 1, :].rearrange("p (g q) -> p g q", q=64)
                    t1 = af[:].rearrange("p (g q) -> p g q", q=64)
                    t2 = sf[:].rearrange("p (g q) -> p g q", q=64)
                    tt(af, t1, cr3, bre3, ALU.mult, [ef.b, braw.b])
                    tt(sf, t2, ci3, bim3, ALU.mult, [phf.b, braw.b])
                    tt(Bb1, Bb1[:, :, 0:64], t1, t2, ALU.subtract, [af.b, sf.b])
                    tt(af, t1, cr3, bim3, ALU.mult, [ef.b, braw.b])
                    tt(sf, t2, ci3, bre3, ALU.mult, [phf.b, braw.b])
                    tt(Bb1, Bb1[:, :, 64:128], t1, t2, ALU.add, [af.b, sf.b])
                    cp(Bb2, Bb2[:, :, 0:64], Bb1[:, :, 64:128], [Bb1.b])
                    cp(Bb2, Bb2[:, :, 64:128], Bb1[:, :, 0:64], [Bb1.b])
                    P.barrier()
                with ExitStack() as s2:
                    craw = sb(s2, "craw", [128, 2, 4096], F32)
                    load("sp", craw, craw[:], s5c[l].rearrange("a k n -> k a n"))
                    ts(Cp1, Cp1[:].rearrange("p g m -> p (g m)"), craw[:, 0, :], sgnB, None, ALU.mult, None, [craw.b, cstf.b])
                    ts(Cp2, Cp2[:].rearrange("p g m -> p (g m)"), craw[:, 1, :], -1.0, None, ALU.mult, None, [craw.b])
                    lamp = sb(s2, "lamp", [128, 3, 32], F32)
                    load("sp", lamp, lamp[:], s5lamp[:, l])
                    dtp = sb(s2, "dtp", [128, 32], F32)
                    app = sb(s2, "app", [128, 32], F32)
                    php = sb(s2, "php", [128, 32], F32)
                    act(dtp, dtp[:], lamp[:, 2, :], AF.Exp, [lamp.b])
                    tt(app, app[:], lamp[:, 0, :], dtp[:], ALU.mult, [lamp.b, dtp.b])
                    tt(php, php[:], lamp[:, 1, :], dtp[:], ALU.mult, [lamp.b, dtp.b])
                    ang3 = sb(s2, "ang3", [128, 32, 64], F32)
                    ma3 = sb(s2, "ma3", [128, 32, 64], F32)
                    sn3 = sb(s2, "sn3", [128, 32, 64], F32)
                    cs3 = sb(s2, "cs3", [128, 32, 64], F32)
                    jb = JT.unsqueeze(1).broadcast_to([128, 32, 64])
                    tt(ang3, ang3[:], php[:].unsqueeze(2).broadcast_to([128, 32, 64]), jb, ALU.mult, [php.b, cstf.b])
                    tt(ma3, ma3[:], app[:].unsqueeze(2).broadcast_to([128, 32, 64]), jb, ALU.mult, [app.b, cstf.b])
                    sincos(s2, "s3", ang3, [128, 32, 64], sn3, sn3[:], cs3, cs3[:])
                    act(ang3, ang3[:], ma3[:], AF.Exp, [ma3.b], scale=-1.0)
                    tt(TA, TA[:], ang3[:], cs3[:], ALU.mult, [ang3.b, cs3.b])
                    tt(TB, TB[:], ang3[:], sn3[:], ALU.mult, [ang3.b, sn3.b])
                    ts(TB, TB[:], TB[:], sgnB, None, ALU.mult, None, [TB.b, cstf.b])
                    act(ang3, ang3[:], ma3[:], AF.Exp, [ma3.b])
                    tt(TC, TC[:], ang3[:], cs3[:], ALU.mult, [ang3.b, cs3.b])
                    tt(TD, TD[:], ang3[:], sn3[:], ALU.mult, [ang3.b, sn3.b])
                    a64 = sb(s2, "a64", [128, 32], F32)
                    p64 = sb(s2, "p64", [128, 32], F32)
                    s64 = sb(s2, "s64", [128, 32], F32)
                    c64 = sb(s2, "c64", [128, 32], F32)
                    ts(a64, a64[:], app[:], 64.0, None, ALU.mult, None, [app.b])
                    ts(p64, p64[:], php[:], 64.0, None, ALU.mult, None, [php.b])
                    sincos(s2, "s6", p64, [128, 32], s64, s64[:], c64, c64[:])
                    act(a64, a64[:], a64[:], AF.Exp, [a64.b])
                    tt(K1, K1[:], a64[:], c64[:], ALU.mult, [a64.b, c64.b])
                    tt(K2, K2[:], a64[:], s64[:], ALU.mult, [a64.b, s64.b])
                    ts(K2, K2[:], K2[:], sgnB, -1.0, ALU.mult, ALU.mult, [K2.b, cstf.b])
                    P.barrier()
                sy = ExitStack()
                ygb = sb(sy, "ygb", [128, 4, T], BF16)
                with ExitStack() as s2:
                    uTb = [sb(s2, "uTb%d" % i, [128, 2, 512], BF16) for i in range(2)]
                    Sall2 = [sb(s2, "Sall%d" % i, [128, 16, 512], F32) for i in range(2)]
                    Hs1 = sb(s2, "Hs1", [128, 16, NCK + 1], F32)
                    H2 = [sb(s2, "H2%d" % i, [128, 16], F32) for i in range(2)]
                    S63 = sb(s2, "S63", [128, 16, 8], F32)
                    S63s = sb(s2, "S63s", [128, 16, 8], F32)
                    w1 = [sb(s2, "w1%d" % i, [128, 512], F32) for i in range(3)]
                    w2 = [sb(s2, "w2%d" % i, [128, 512], F32) for i in range(3)]
                    spt = [sb(s2, "spt%d" % i, [128, 512], F32) for i in range(2)]
                    V1 = [sb(s2, "V1%d" % i, [128, 512], BF16) for i in range(2)]
                    V2 = [sb(s2, "V2%d" % i, [128, 512], BF16) for i in range(2)]
                    zt = [sb(s2, "zt%d" % i, [128, 16], F32) for i in range(6)]
                    yv = sb(s2, "yv", [128, 512], F32)
                    pbu = [ps(s2, "pbu%d" % i) for i in range(4)]
                    psw = ps(s2, "psw")
                    py = [ps(s2, "py%d" % i) for i in range(2)]
                    nwl = [0]
                    for half in range(2):
                        g0 = half * 16
                        mset(Hs1, Hs1[:, :, 0:1], 0.0)
                        mset(H2[0], H2[0][:], 0.0)
                        hcl = [0]
                        K1h = K1[:, g0:g0 + 16]
                        K2h = K2[:, g0:g0 + 16]
                        def P1(tb):
                            tsl = slice(tb * 512, (tb + 1) * 512)
                            Sall = Sall2[tb % 2]
                            uT = uTb[tb % 2]
                            for c in range(2):
                                load("pool", uT, uT[:, c, :], projT[(2 * half + c) * 128:(2 * half + c + 1) * 128, tsl])
                            pend = None
                            for gi in range(17):
                                if gi < 16:
                                    g = g0 + gi
                                    c = gi // 8
                                    nw = nwl[0]
                                    p1 = pbu[(2 * nw) % 4]
                                    p2 = pbu[(2 * nw + 1) % 4]
                                    a_, b_ = w1[nw % 3], w2[nw % 3]
                                    nwl[0] += 1
                                    mm(p1, p1[:], Bb1[:, g, :], uT[:, c, :], True, True, [Bb1.b, uT.b])
                                    mm(p2, p2[:], Bb2[:, g, :], uT[:, c, :], True, True, [Bb2.b, uT.b])
                                    tab = TA[:, g, :].unsqueeze(1).broadcast_to([128, 8, 64])
                                    tbb = TB[:, g, :].unsqueeze(1).broadcast_to([128, 8, 64])
                                    tt(a_, a_[:].rearrange("p (n j) -> p n j", j=64), p1[:].rearrange("p (n j) -> p n j", j=64), tab, ALU.mult, [p1.b, TA.b])
                                    tt(b_, b_[:].rearrange("p (n j) -> p n j", j=64), p2[:].rearrange("p (n j) -> p n j", j=64), tbb, ALU.mult, [p2.b, TB.b])
                                    tt(a_, a_[:], a_[:], b_[:], ALU.add, [a_.b, b_.b])
                                if pend is not None:
                                    pgi, pa = pend
                                    P.op("dve", lambda e, pgi=pgi, pa=pa: e.tensor_tensor_scan(out=Sall[:, pgi, :], data0=rst[:], data1=pa[:], initial=0.0, op0=ALU.mult, op1=ALU.add), r=[rst.b, pa.b], w=[Sall.b])
                                pend = (gi, a_) if gi < 16 else None

                        def BD(tb):
                            Sall = Sall2[tb % 2]
                            cp(S63, S63[:], Sall[:, :, 63:512:64], [Sall.b])
                            mm(psw, psw[:, 0:128], permsw, S63[:].rearrange("p g n -> p (g n)"), True, True, [cstf.b, S63.b])
                            cp(S63s, S63s[:].rearrange("p g n -> p (g n)"), psw[:, 0:128], [psw.b])
                            for n8 in range(8):
                                n = tb * 8 + n8
                                z1, z2, t1_, t2_, t3_, t4_ = zt
                                hcur = hcl[0]
                                hc, hn = H2[hcur], H2[1 - hcur]
                                tt(z1, z1[:], S63[:, :, n8], Hs1[:, :, n], ALU.add, [S63.b, Hs1.b])
                                tt(z2, z2[:], S63s[:, :, n8], hc[:], ALU.add, [S63s.b, hc.b])
                                tt(t1_, t1_[:], z1[:], K1h, ALU.mult, [z1.b, K1.b])
                                tt(t2_, t2_[:], z2[:], K2h, ALU.mult, [z2.b, K2.b])
                                tt(Hs1, Hs1[:, :, n + 1], t1_[:], t2_[:], ALU.add, [t1_.b, t2_.b])
                                tt(t3_, t3_[:], z2[:], K1h, ALU.mult, [z2.b, K1.b])
                                tt(t4_, t4_[:], z1[:], K2h, ALU.mult, [z1.b, K2.b])
                                tt(hn, hn[:], t3_[:], t4_[:], ALU.subtract, [t3_.b, t4_.b])
                                hcl[0] = 1 - hcur

                        def P2(tb):
                            tsl = slice(tb * 512, (tb + 1) * 512)
                            Sall = Sall2[tb % 2]
                            uT = uTb[tb % 2]
                            for c in range(2):
                                ch = 2 * half + c
                                py_ = py[(tb * 2 + c) % 2]
                                for g8 in range(8):
                                    gi = c * 8 + g8
                                    g = g0 + gi
                                    s_ = spt[g8 % 2]
                                    v1_, v2_ = V1[g8 % 2], V2[g8 % 2]
                                    hb_ = Hs1[:, gi, tb * 8:tb * 8 + 8].unsqueeze(2).broadcast_to([128, 8, 64])
                                    s3 = s_[:].rearrange("p (n j) -> p n j", j=64)
                                    for n8 in range(8):
                                        act(s_, s_[:, n8 * 64:(n8 + 1) * 64], Sall[:, gi, n8 * 64:(n8 + 1) * 64], AF.Identity, [Sall.b, Hs1.b], bias=Hs1[:, gi, tb * 8 + n8:tb * 8 + n8 + 1])
                                    tt(v1_, v1_[:].rearrange("p (n j) -> p n j", j=64), s3, TC[:, g, :].unsqueeze(1).broadcast_to([128, 8, 64]), ALU.mult, [s_.b, TC.b], eng="pool")
                                    tt(v2_, v2_[:].rearrange("p (n j) -> p n j", j=64), s3, TD[:, g, :].unsqueeze(1).broadcast_to([128, 8, 64]), ALU.mult, [s_.b, TD.b], eng="pool")
                                    mm(py_, py_[:], Cp1[:, g, :], v1_[:], g8 == 0, False, [Cp1.b, v1_.b])
                                    mm(py_, py_[:], Cp2[:, g, :], v2_[:], False, g8 == 7, [Cp2.b, v2_.b])
                                stt(yv, yv[:], uT[:, c, :], pv[:, l, 7 + ch:8 + ch], py_[:], ALU.mult, ALU.add, [uT.b, pv.b, py_.b])
                                act(ygb, ygb[:, ch, tsl], yv[:], AF.Gelu_apprx_tanh, [yv.b])

                        P1(0)
                        for tb in range(NB):
                            BD(tb)
                            if tb + 1 < NB:
                                P1(tb + 1)
                            P2(tb)
                    P.barrier()
                with ExitStack() as s2:
                    wg = sb(s2, "wg", [128, 4, 512], BF16)
                    sg = [sb(s2, "sg%d" % i, [128, 512], F32) for i in range(2)]
                    yo = [sb(s2, "yo%d" % i, [128, 512], BF16) for i in range(2)]
                    pg = [ps(s2, "pg%d" % i) for i in range(2)]
                    load("pool", wg, wg[:], w_glu[l].rearrange("(k p) n -> p k n", p=128))
                    i = 0
                    for tb in range(NB):
                        tsl = slice(tb * 512, (tb + 1) * 512)
                        for mch in range(4):
                            p_ = pg[i % 2]
                            for kc in range(4):
                                mm(p_, p_[:], wg[:, kc, mch * 128:(mch + 1) * 128], ygb[:, kc, tsl], kc == 0, kc == 3, [wg.b, ygb.b])
                            act(sg[i % 2], sg[i % 2][:], p_[:], AF.Sigmoid, [p_.b])
                            tt(yo[i % 2], yo[i % 2][:], ygb[:, mch, tsl], sg[i % 2][:], ALU.mult, [ygb.b, sg[i % 2].b])
                            store("sp", yo[i % 2], yT[mch * 128:(mch + 1) * 128, tsl], yo[i % 2][:])
                            i += 1
                    P.barrier()
                sy.close()

        def phase_mla(l):
            SCALE = 192.0 ** -0.5
            with ExitStack() as st:
                cqn = sb(st, "cqn", [128, 4, T], BF16)
                ckn = sb(st, "ckn", [128, 2, T], BF16)
                krr = sb(st, "krr", [64, T], BF16)
                wq = sb(st, "wq", [128, 4, 2048], BF16)
                wkv = sb(st, "wkv", [128, 2, 2048], BF16)
                maskA = sb(st, "maskA", [128, 2048], BF16)
                load("pool", maskA, maskA[:], maskA_in)
                cosT = sb(st, "cosT", [64, T], F32)
                sinT = sb(st, "sinT", [64, T], F32)
                load("pool", wq, wq[:], w_uq[l].rearrange("(k p) n -> p k n", p=128))
                load("pool", wkv, wkv[:], w_ukv[l].rearrange("(k p) n -> p k n", p=128))
                load("sp", cosT, cosT[:], ropeT[0])
                load("sp", sinT, sinT[:], ropeT[1])
                with ExitStack() as s2:
                    raw = [sb(s2, "raw%d" % i, [128, 6, 512], F32) for i in range(2)]
                    sq = [sb(s2, "sq%d" % i, [128, 6, 512], BF16) for i in range(2)]
                    rq = [sb(s2, "rq%d" % i, [128, 512], F32) for i in range(2)]
                    rk = [sb(s2, "rk%d" % i, [128, 512], F32) for i in range(2)]
                    kra = [sb(s2, "kra%d" % i, [64, 512], F32) for i in range(2)]
                    krb = [sb(s2, "krb%d" % i, [64, 512], F32) for i in range(2)]
                    pq_ = [ps(s2, "pssq%d" % i) for i in range(2)]
                    pk_ = [ps(s2, "pssk%d" % i) for i in range(2)]
                    for tb in range(NB):
                        tsl = slice(tb * 512, (tb + 1) * 512)
                        i = tb % 2
                        load("sp", raw[i], raw[i][:], projT[512:1280, tsl].rearrange("(k p) t -> p k t", p=128))
                        load("sp", kra[i], kra[i][:], projT[1280:1344, tsl])
                        load("sp", krb[i], krb[i][:], projT[1344:1408, tsl])
                        act(sq[i], sq[i][:], raw[i][:], AF.Square, [raw[i].b])
                        for kc in range(4):
                            mm(pq_[i], pq_[i][:], ones[:], sq[i][:, kc, :], kc == 0, kc == 3, [ones.b, sq[i].b])
                        for kc in range(2):
                            mm(pk_[i], pk_[i][:], ones[:], sq[i][:, 4 + kc, :], kc == 0, kc == 1, [ones.b, sq[i].b])
                        act(rq[i], rq[i][:], pq_[i][:], AF.Sqrt, [pq_[i].b, cstf.b], bias=epsc, scale=1.0 / 512)
                        P.op("dve", lambda e, i=i: e.reciprocal(out=rq[i][:], in_=rq[i][:]), r=[rq[i].b], w=[rq[i].b])
                        act(rk[i], rk[i][:], pk_[i][:], AF.Sqrt, [pk_[i].b, cstf.b], bias=epsc, scale=1.0 / 256)
                        P.op("dve", lambda e, i=i: e.reciprocal(out=rk[i][:], in_=rk[i][:]), r=[rk[i].b], w=[rk[i].b])
                        for kc in range(4):
                            stt(cqn, cqn[:, kc, tsl], raw[i][:, kc, :], pv[:, l, kc:kc + 1], rq[i][:], ALU.mult, ALU.mult, [raw[i].b, pv.b, rq[i].b])
                        for kc in range(2):
                            stt(ckn, ckn[:, kc, tsl], raw[i][:, 4 + kc, :], pv[:, l, 4 + kc:5 + kc], rk[i][:], ALU.mult, ALU.mult, [raw[i].b, pv.b, rk[i].b])
                        tt(kra[i], kra[i][:], kra[i][:], cosT[:, tsl], ALU.mult, [kra[i].b, cosT.b])
                        tt(krb[i], krb[i][:], krb[i][:], sinT[:, tsl], ALU.mult, [krb[i].b, sinT.b])
                        tt(krr, krr[:, tsl], kra[i][:], krb[i][:], ALU.add, [kra[i].b, krb[i].b])
                    P.barrier()
                with ExitStack() as s2:
                    qn_ = sb(s2, "qn_", [128, T], BF16)
                    qr_ = sb(s2, "qr_", [64, T], BF16)
                    kn_ = sb(s2, "kn_", [128, T], BF16)
                    v_ = sb(s2, "v_", [128, NT, 128], BF16)
                    r1 = [sb(s2, "r1%d" % i, [64, 512], F32) for i in range(2)]
                    r2 = [sb(s2, "r2%d" % i, [64, 512], F32) for i in range(2)]
                    pT = [sb(s2, "pT%d" % i, [128, 512], BF16) for i in range(3)]
                    rsm = [sb(s2, "rsm%d" % i, [128, 512], F32) for i in range(2)]
                    yh = [sb(s2, "yh%d" % i, [128, 512], BF16) for i in range(2)]
                    pgen = [ps(s2, "pgen%d" % i) for i in range(2)]
                    pS = [ps(s2, "pS%d" % i) for i in range(2)]
                    po = [ps(s2, "po%d" % i) for i in range(2)]
                    psm = [ps(s2, "psm%d" % i) for i in range(2)]
                    ng = 0
                    npt = 0
                    nq = 0
                    for h in range(8):
                        c0 = h * 256
                        for tb in range(NB):
                            tsl = slice(tb * 512, (tb + 1) * 512)
                            p_ = pgen[ng % 2]; ng += 1
                            for kc in range(4):
                                mm(p_, p_[:], wq[:, kc, c0:c0 + 128], cqn[:, kc, tsl], kc == 0, kc == 3, [wq.b, cqn.b])
                            act(qn_, qn_[:, tsl], p_[:], AF.Copy, [p_.b])
                            pr = pgen[ng % 2]; ng += 1
                            for kc in range(4):
                                mm(pr, pr[0:64, :], wq[:, kc, c0 + 128:c0 + 192], cqn[:, kc, tsl], kc == 0, kc == 3, [wq.b, cqn.b])
                            a_ = r1[tb % 2]
                            tt(a_, a_[:], pr[0:64, :], cosT[:, tsl], ALU.mult, [pr.b, cosT.b])
                            pw = pgen[ng % 2]; ng += 1
                            for kc in range(4):
                                mm(pw, pw[0:64, :], wq[:, kc, c0 + 192:c0 + 256], cqn[:, kc, tsl], kc == 0, kc == 3, [wq.b, cqn.b])
                            b_ = r2[tb % 2]
                            tt(b_, b_[:], pw[0:64, :], sinT[:, tsl], ALU.mult, [pw.b, sinT.b])
                            tt(qr_, qr_[:, tsl], a_[:], b_[:], ALU.add, [a_.b, b_.b])
                            pk = pgen[ng % 2]; ng += 1
                            for kc in range(2):
                                mm(pk, pk[:], wkv[:, kc, c0:c0 + 128], ckn[:, kc, tsl], kc == 0, kc == 1, [wkv.b, ckn.b])
                            act(kn_, kn_[:, tsl], pk[:], AF.Copy, [pk.b])
                            pvv = pgen[ng % 2]; ng += 1
                            for ti in range(4):
                                t0 = tb * 512 + ti * 128
                                for kc in range(2):
                                    mm(pvv, pvv[:, ti * 128:(ti + 1) * 128], ckn[:, kc, t0:t0 + 128], wkv[:, kc, c0 + 128:c0 + 256], kc == 0, kc == 1, [wkv.b, ckn.b])
                            cp(v_, v_[:, tb * 4:tb * 4 + 4, :], pvv[:].rearrange("p (a d) -> p a d", d=128), [pvv.b])
                        for qc in range(NB):
                            qsl = slice(qc * 512, (qc + 1) * 512)
                            po_ = po[nq % 2]
                            pm_ = psm[nq % 2]
                            nkb = 4 * (qc + 1)
                            def s_stage(kb, idx):
                                ksl = slice(kb * 128, (kb + 1) * 128)
                                s_ = pS[idx % 2]
                                mm(s_, s_[:], kn_[:, ksl], qn_[:, qsl], True, False, [kn_.b, qn_.b])
                                mm(s_, s_[:], krr[0:64, ksl], qr_[0:64, qsl], False, True, [krr.b, qr_.b])
                                return s_
                            nxt = s_stage(0, npt)
                            for kb in range(nkb):
                                s_ = nxt
                                t_ = pT[npt % 3]
                                npt += 1
                                if kb + 1 < nkb:
                                    nxt = s_stage(kb + 1, npt)
                                act(t_, t_[:], s_[:], AF.Exp, [s_.b], scale=SCALE)
                                if kb >= 4 * qc:
                                    j = kb - 4 * qc
                                    tt(t_, t_[:], t_[:], maskA[:, j * 512:(j + 1) * 512], ALU.mult, [t_.b, maskA.b])
                                mm(po_, po_[:], v_[:, kb, :], t_[:], kb == 0, kb == nkb - 1, [v_.b, t_.b])
                                mm(pm_, pm_[:], ones[:], t_[:], kb == 0, kb == nkb - 1, [ones.b, t_.b])
                            rs_ = rsm[nq % 2]
                            y_ = yh[nq % 2]
                            nq += 1
                            P.op("dve", lambda e, rs_=rs_, pm_=pm_: e.reciprocal(out=rs_[:], in_=pm_[:]), r=[pm_.b], w=[rs_.b])
                            tt(y_, y_[:], po_[:], rs_[:], ALU.mult, [po_.b, rs_.b])
                            store("sp", y_, yT[512 + h * 128:512 + (h + 1) * 128, qsl], y_[:])
                    P.barrier()

        def phase_hg(l):
            HB = min(T, 1024)
            NHB = T // HB
            with ExitStack() as st:
                NC2 = T // 32
                nck = HB // 32
                qt_ = sb(st, "qt_", [128, T], BF16)
                kt_ = sb(st, "kt_", [128, T], BF16)
                qh_ = sb(st, "qh_", [128, T], F32)
                khT = sb(st, "khT", [32, NC2, 128], BF16)
                vt = sb(st, "vt", [32, NC2, 128], BF16)
                ebl = sb(st, "ebl", [128, NC2], F32)
                oT = sb(st, "oT", [128, T], F32)
                qtq = [Tile(qt_.t, "qt_") for _ in range(NHB)]
                ktq = [Tile(kt_.t, "kt_") for _ in range(NHB)]
                qhq = [Tile(qh_.t, "qh_") for _ in range(NHB)]
                khq = [Tile(khT.t, "khT") for _ in range(NHB)]
                ebq = [Tile(ebl.t, "ebl") for _ in range(NHB)]
                vtq = [Tile(vt.t, "vt") for _ in range(NHB)]
                oT2 = [oT, sb(st, "oTb", [128, T], F32)]
                nsq = sb(st, "nsq", [128, HB], BF16)
                nE = sb(st, "nE", [128, HB], F32)
                nG = sb(st, "nG", [128, HB], F32)
                S32 = [sb(st, "S32%d" % i, [128, 128], F32) for i in range(4)]
                Zt = sb(st, "hZ", [128, HB], F32)
                Qt = sb(st, "hQ", [128, HB], F32)
                SGt = sb(st, "hSG", [128, HB], F32)
                Bt = sb(st, "hB", [128, HB], F32)
                Et = sb(st, "hE", [128, HB], F32)
                khb = sb(st, "khb", [128, HB], BF16)
                aTt = [sb(st, "aTt%d" % i, [32, 32], BF16) for i in range(3)]
                yo_ = sb(st, "hyo", [128, HB], BF16)
                ptk = ps(st, "ptk", (128, 1024), BF16)
                pnn = ps(st, "hpnn")
                pa_ = [ps(st, "hpa%d" % i) for i in range(2)]
                pob = [ps(st, "hpo%d" % i) for i in range(2)]
                pst = [ps(st, "hpst%d" % i) for i in range(2)]

                def prep_gen(h, hb):
                    lb_ap = lbt[:, h, l:l + 1]
                    oml_ap = omlt[:, h, l:l + 1]
                    noml_ap = nomlt[:, h, l:l + 1]
                    hsl = slice(hb * HB, (hb + 1) * HB)
                    P.op("sp", lambda e: e.dma_start(out=vt[:, hb * nck:(hb + 1) * nck, :], in_=vtok[hb * HB:(hb + 1) * HB, h * 128:(h + 1) * 128].rearrange("(n s) d -> s n d", s=32)), w=[vtq[hb].b], dsem="Lvt%d" % (hb % 2))
                    load("sp", Zt, Zt[:], projT[2048 + h * 128:2048 + (h + 1) * 128, hsl])
                    load("sp", Qt, Qt[:], projT[1536 + h * 128:1536 + (h + 1) * 128, hsl])
                    act(SGt, SGt[:], Zt[:], AF.Sigmoid, [Zt.b])
                    yield
                    ts(Zt, Zt[:], SGt[:], oml_ap, lb_ap, ALU.mult, ALU.add, [SGt.b, omlt.b, lbt.b])
                    yield
                    ts(SGt, SGt[:], SGt[:], noml_ap, oml_ap, ALU.mult, ALU.add, [SGt.b, omlt.b, nomlt.b])
                    act(Zt, Zt[:], Zt[:], AF.Ln, [Zt.b])
                    yield
                    for tb in range(HB // 512):
                        bsl = slice(tb * 512, (tb + 1) * 512)
                        P.op("dve", lambda e, bsl=bsl: e.tensor_tensor_scan(out=Bt[:, bsl], data0=rst32[:], data1=Zt[:, bsl], initial=0.0, op0=ALU.mult, op1=ALU.add), r=[rst32.b, Zt.b], w=[Bt.b])
                        yield
                    act(Qt, Qt[:], Qt[:], AF.Silu, [Qt.b])
                    b3 = Bt[:].rearrange("p (n j) -> p n j", j=32)
                    c3 = Zt[:].rearrange("p (n j) -> p n j", j=32)
                    tt(Zt, c3, b3, b3[:, :, 15:16].broadcast_to([128, nck, 32]), ALU.subtract, [Bt.b], eng="pool")
                    act(Et, Et[:], Zt[:], AF.Exp, [Zt.b])
                    yield
                    tt(qtq[hb], qt_[:, hsl], Qt[:], Et[:], ALU.mult, [Qt.b, Et.b])
                    act(Et, Et[:], Zt[:], AF.Exp, [Zt.b], scale=-1.0)
                    yield
                    tt(ktq[hb], kt_[:, hsl], SGt[:], Et[:], ALU.mult, [SGt.b, Et.b])
                    act(Et, Et[:], Bt[:], AF.Exp, [Bt.b])
                    yield
                    tt(qhq[hb], qh_[:, hsl], Qt[:], Et[:], ALU.mult, [Qt.b, Et.b])
                    tt(Zt, c3, b3[:, :, 31:32].broadcast_to([128, nck, 32]), b3, ALU.subtract, [Bt.b], eng="pool")
                    act(Et, Et[:], Zt[:], AF.Exp, [Zt.b])
                    yield
                    tt(khb, khb[:], SGt[:], Et[:], ALU.mult, [SGt.b, Et.b])
                    act(ebq[hb], ebl[:, hb * nck:(hb + 1) * nck], Bt[:, 31:HB:32], AF.Exp, [Bt.b])
                    yield
                    for n16 in range(nck // 8):
                        for n8 in range(8):
                            n = n16 * 8 + n8
                            P.op("pe", lambda e, n8=n8, n=n: e.transpose(ptk[0:32, n8 * 128:(n8 + 1) * 128], khb[:, n * 32:(n + 1) * 32], ident[:]), r=[khb.b, ident.b], w=[ptk.b])
                        n0 = hb * nck + n16 * 8
                        cp(khq[hb], khT[:, n0:n0 + 8, :], ptk[0:32, :].rearrange("p (n d) -> p n d", d=128), [ptk.b])
                        yield

                def norm_gen(h):
                    oTh = oT2[h % 2]
                    for hb in range(NHB):
                        hsl = slice(hb * HB, (hb + 1) * HB)
                        load("sp", nG, nG[:], projT[2560 + h * 128:2560 + (h + 1) * 128, hsl])
                        act(nsq, nsq[:], oTh[:, hsl], AF.Square, [oTh.b])
                        yield
                        for tb in range(HB // 512):
                            bsl = slice(tb * 512, (tb + 1) * 512)
                            mm(pnn, pnn[:], ones[:], nsq[:, bsl], True, True, [ones.b, nsq.b])
                            act(nE, nE[:, bsl], pnn[:], AF.Sqrt, [pnn.b, cstf.b], bias=epsc, scale=1.0 / 128)
                            yield
                        P.op("dve", lambda e: e.reciprocal(out=nE[:], in_=nE[:]), r=[nE.b], w=[nE.b])
                        yield
                        tt(nE, nE[:], nE[:], oTh[:, hsl], ALU.mult, [nE.b, oTh.b])
                        act(nG, nG[:], nG[:], AF.Silu, [nG.b])
                        yield
                        stt(yo_, yo_[:], nE[:], pv[:, l, 6:7], nG[:], ALU.mult, ALU.mult, [nE.b, pv.b, nG.b])
                        store("sp", yo_, yT[1536 + h * 128:1536 + (h + 1) * 128, hsl], yo_[:])
                        yield

                chain = NHB > 1
                ngen = None
                for h in range(4):
                    oT = oT2[h % 2]
                    if h == 0 or not chain:
                        for _ in prep_gen(h, 0):
                            pass
                    mset(S32[0], S32[0][:], 0.0)
                    R = len(S32)

                    def o_stage(n):
                        q = n // nck
                        nsl = slice(n * 32, (n + 1) * 32)
                        pb = pob[(n // 16) % 2]
                        osl = slice((n % 16) * 32, (n % 16 + 1) * 32)
                        at = aTt[n % 3]
                        mm(pb, pb[:, osl], vt[:, n, :], at[:], True, False, [vtq[q].b, at.b])
                        mm(pb, pb[:, osl], S32[n % R][:], qh_[:, nsl], False, True, [S32[n % R].b, qhq[q].b])
                        if n % 16 == 15:
                            act(oT, oT[:, (n - 15) * 32:(n + 1) * 32], pb[:], AF.Copy, [pb.b])

                    for q in range(NHB):
                        if q + 1 < NHB:
                            pg = prep_gen(h, q + 1)
                        elif chain and h + 1 < 4:
                            pg = prep_gen(h + 1, 0)
                        else:
                            pg = None
                        for n in range(q * nck, (q + 1) * nck):
                            nsl = slice(n * 32, (n + 1) * 32)
                            s_ = pst[n % 2]
                            mm(s_, s_[:, 0:128], khT[:, n, :], vt[:, n, :], True, True, [khq[q].b, vtq[q].b])
                            stt(S32[(n + 1) % R], S32[(n + 1) % R][:], S32[n % R][:], ebl[:, n:n + 1], s_[:, 0:128], ALU.mult, ALU.add, [S32[n % R].b, ebq[q].b, s_.b])
                            a_ = pa_[n % 2]
                            at = aTt[n % 3]
                            mm(a_, a_[0:32, 0:32], kt_[:, nsl], qt_[:, nsl], True, True, [ktq[q].b, qtq[q].b])
                            tt(at, at[:], a_[0:32, 0:32], mask64[0:32, 0:32], ALU.mult, [a_.b, mask64.b])
                            if n >= 1:
                                o_stage(n - 1)
                            if pg is not None and n % 2 == 1:
                                next(pg, None)
                            if ngen is not None and n % 2 == 0:
                                next(ngen, None)
                        if pg is not None:
                            for _ in pg:
                                pass
                    o_stage(NC2 - 1)
                    if ngen is not None:
                        for _ in ngen:
                            pass
                    ngen = norm_gen(h)
                    if not chain or h == 3:
                        for _ in ngen:
                            pass
                        ngen = None
                P.barrier()

        def phase_c(l):
            with ExitStack() as st:
                wo = sb(st, "wo", [128, 16, 2048], BF16)
                G1 = sb(st, "G1", [128, D], F32)
                A2 = sb(st, "A2", [128, D], F32)
                B2 = sb(st, "B2", [128, D], F32)
                yTt = [sb(st, "yTt%d" % i, [128, 16, 128], BF16) for i in range(2)]
                xt = [sb(st, "cxt%d" % i, [128, D], F32) for i in range(2)]
                tmp = sb(st, "ctmp", [128, D], F32)
                tmp2 = sb(st, "ctmp2", [128, D], F32)
                hb = [sb(st, "chb%d" % i, [128, D], BF16) for i in range(2)]
                junk = sb(st, "cjunk", [128, D], BF16)
                h2s = [sb(st, "h2s%d" % i, [128, 16, 128], BF16) for i in range(2)]
                sst = [(sb(st, "css%d" % i, [128, 1], F32), sb(st, "crs%d" % i, [128, 1], F32)) for i in range(4)]
                pmx = ps(st, "pmx", (128, 2048), F32)
                ptr = [ps(st, "cptr%d" % i, (128, 2048), BF16) for i in range(2)]
                for cg in range(4):
                    P.op("pool", lambda e, cg=cg: e.dma_start(out=wo[:, :, cg * 512:(cg + 1) * 512], in_=w_out[l, :, cg * 512:(cg + 1) * 512].rearrange("(k p) n -> p k n", p=128)), w=[wo.b], dsem="Lwo")
                load("sp", G1, G1[:], modt[l, MOD_G1])
                load("sp", A2, A2[:], modt[l, MOD_A2])
                load("sp", B2, B2[:], modt[l, MOD_B2])
                def OP(gt):
                    i = gt % 2
                    rows = slice(gt * 128, (gt + 1) * 128)
                    load("sp", yTt[i], yTt[i][:], yT[:, rows].rearrange("(k p) t -> p k t", p=128))
                    load("sp", xt[i], xt[i][:], xsrc_of[0][rows, :])
                    for ng_ in range(4):
                        for kc in range(16):
                            mm(pmx, pmx[:, ng_ * 512:(ng_ + 1) * 512], yTt[i][:, kc, :], wo[:, kc, ng_ * 512:(ng_ + 1) * 512], kc == 0, kc == 15, [yTt[i].b, wo.b])

                def CH1(gt):
                    i = gt % 2
                    rs = rms_rstd(sst[2 * i], pmx, pmx[:], junk, [pmx.b])
                    stt(tmp, tmp[:], pmx[:], rs[:], G1[:], ALU.mult, ALU.mult, [pmx.b, rs.b, G1.b])

                def CH2(gt):
                    i = gt % 2
                    rows = slice(gt * 128, (gt + 1) * 128)
                    tt(xt[i], xt[i][:], xt[i][:], tmp[:], ALU.add, [xt[i].b, tmp.b])
                    store("sp", xt[i], out[rows, :], xt[i][:])
                    rs2 = rms_rstd(sst[2 * i + 1], xt[i], xt[i][:], junk, [xt[i].b])
                    stt(tmp2, tmp2[:], xt[i][:], rs2[:], A2[:], ALU.mult, ALU.mult, [xt[i].b, rs2.b, A2.b])
                    tt(hb[i], hb[i][:], tmp2[:], B2[:], ALU.add, [tmp2.b, B2.b])
                    p_ = ptr[i]
                    for kc in range(16):
                        P.op("pe", lambda e, kc=kc, p_=p_, i=i: e.transpose(p_[:, kc * 128:(kc + 1) * 128], hb[i][:, kc * 128:(kc + 1) * 128], ident[:]), r=[hb[i].b, ident.b], w=[p_.b])
                    act(h2s[i], h2s[i][:], p_[:].rearrange("p (k t) -> p k t", k=16), AF.Copy, [p_.b])
                    store("sp", h2s[i], h2T[:, rows].rearrange("(k p) t -> p k t", p=128), h2s[i][:])

                OP(0)
                for gt in range(NT):
                    CH1(gt)
                    if gt + 1 < NT:
                        OP(gt + 1)
                    CH2(gt)
                P.barrier()
            if stop == "C1":
                return
            SUB = min(1024, SBT)
            with ExitStack() as st:
                h2 = sb(st, "h2", [128, 16, SBT], BF16)
                wg_ = [sb(st, "wg_%d" % i, [128, 16, 256], BF16) for i in range(2)]
                wv_ = [sb(st, "wv_%d" % i, [128, 16, 256], BF16) for i in range(2)]
                rawg = [sb(st, "rawg%d" % i, [128, 2 + SBT], F32) for i in range(2)]
                rawv = [sb(st, "rawv%d" % i, [128, 2 + SBT], F32) for i in range(2)]
                cg_ = [sb(st, "cg_%d" % i, [128, SUB], F32) for i in range(2)]
                cv_ = [sb(st, "cv_%d" % i, [128, SUB], F32) for i in range(2)]
                ao = [sb(st, "ao%d" % i, [128, SUB], BF16) for i in range(2)]
                halo = sb(st, "halo", [128, 86, 2], F32)
                cvp = sb(st, "cvp", [128, 86, 4], F32)
                pg_ = [ps(st, "fpg%d" % i, (128, SUB), F32) for i in range(2)]
                pv_ = [ps(st, "fpv%d" % i, (128, SUB), F32) for i in range(2)]
                load("sp", cvp, cvp[:], convp[:, l])
                it = 0
                for sbi in range(NSB):
                    load("sp", h2, h2[:], h2T[:, sbi * SBT:(sbi + 1) * SBT].rearrange("(k p) t -> p k t", p=128))
                    for jg in range(22):
                        ncol = 256 if jg < 21 else 128
                        wgt, wvt = wg_[jg % 2], wv_[jg % 2]
                        load("pool", wgt, wgt[:, :, 0:ncol], w_up[l, :, jg * 256:jg * 256 + ncol].rearrange("(k p) n -> p k n", p=128))
                        load("pool", wvt, wvt[:, :, 0:ncol], w_up[l, :, DFF + jg * 256:DFF + jg * 256 + ncol].rearrange("(k p) n -> p k n", p=128))
                        for jj in range(ncol // 128):
                            j = jg * 2 + jj
                            rg, rv = rawg[j % 2], rawv[j % 2]
                            if sbi == 0:
                                mset(rg, rg[:, 0:2], 0.0)
                                mset(rv, rv[:, 0:2], 0.0)
                            else:
                                cp(rg, rg[:, 0:2], halo[:, j, :], [halo.b])
                                cp(rv, rv[:, 0:2], halo[:, 43 + j, :], [halo.b])
                            for sub in range(SBT // SUB):
                                pg, pvl = pg_[it % 2], pv_[it % 2]
                                cg, cv, a_ = cg_[it % 2], cv_[it % 2], ao[it % 2]
                                it += 1
                                off = sub * SUB
                                for t5 in range(SUB // 512):
                                    for kc in range(16):
                                        mm(pg, pg[:, t5 * 512:(t5 + 1) * 512], wgt[:, kc, jj * 128:(jj + 1) * 128], h2[:, kc, off + t5 * 512:off + (t5 + 1) * 512], kc == 0, kc == 15, [wgt.b, h2.b])
                                for t5 in range(SUB // 512):
                                    for kc in range(16):
                                        mm(pvl, pvl[:, t5 * 512:(t5 + 1) * 512], wvt[:, kc, jj * 128:(jj + 1) * 128], h2[:, kc, off + t5 * 512:off + (t5 + 1) * 512], kc == 0, kc == 15, [wvt.b, h2.b])
                                act(rg, rg[:, 2 + off:2 + off + SUB], pg[:], AF.Copy, [pg.b])
                                act(rv, rv[:, 2 + off:2 + off + SUB], pvl[:], AF.Copy, [pvl.b])
                                act(cg, cg[:], pg[:], AF.Identity, [pg.b, cvp.b], bias=cvp[:, j, 3:4], scale=cvp[:, j, 2:3])
                                act(cv, cv[:], pvl[:], AF.Identity, [pvl.b, cvp.b], bias=cvp[:, 43 + j, 3:4], scale=cvp[:, 43 + j, 2:3])
                                stt(cg, cg[:], rg[:, 1 + off:1 + off + SUB], cvp[:, j, 1:2], cg[:], ALU.mult, ALU.add, [rg.b, cvp.b, cg.b])
                                stt(cg, cg[:], rg[:, off:off + SUB], cvp[:, j, 0:1], cg[:], ALU.mult, ALU.add, [rg.b, cvp.b, cg.b])
                                stt(cv, cv[:], rv[:, 1 + off:1 + off + SUB], cvp[:, 43 + j, 1:2], cv[:], ALU.mult, ALU.add, [rv.b, cvp.b, cv.b])
                                stt(cv, cv[:], rv[:, off:off + SUB], cvp[:, 43 + j, 0:1], cv[:], ALU.mult, ALU.add, [rv.b, cvp.b, cv.b])
                                act(cg, cg[:], cg[:], AF.Gelu_apprx_tanh, [cg.b])
                                tt(a_, a_[:], cg[:], cv[:], ALU.mult, [cg.b, cv.b])
                                c0 = sbi * SBT + off
                                store("sp", a_, aT[c0 // 128:(c0 + SUB) // 128, :, j, :].rearrange("n p t -> p n t"), a_[:].rearrange("p (n t) -> p n t", t=128))
                            if sbi < NSB - 1:
                                cp(halo, halo[:, j, :], rg[:, SBT:SBT + 2], [rg.b])
                                cp(halo, halo[:, 43 + j, :], rv[:, SBT:SBT + 2], [rv.b])
                P.barrier()
            if stop == "C2":
                return
            st0 = ExitStack()
            ssq = sb(st0, "ssq", [128, NT, 4], F32)
            with ExitStack() as st:
                wd = [sb(st, "wd%d" % i, [128, NCH, 512], BF16) for i in range(2)]
                at_ = [sb(st, "at_%d" % i, [128, NCH, 128], BF16) for i in range(3)]
                ys = [sb(st, "ys%d" % i, [128, 512], F32) for i in range(2)]
                junk = sb(st, "djunk", [128, 512], BF16)
                pd = [ps(st, "pd%d" % i) for i in range(2)]
                mg = None
                if l + 1 < L:
                    pmm = [ps(st, "pmm%d" % i) for i in range(2)]
                    mg = mod_gen(l + 1, st, pmm)

                def load_wd(ng_):
                    for k0 in range(0, NCH, 22):
                        k1 = min(NCH, k0 + 22)
                        P.op("pool", lambda e, k0=k0, k1=k1, ng_=ng_: e.dma_start(out=wd[ng_ % 2][:, k0:k1, :], in_=w_down[l, k0 * 128:k1 * 128, ng_ * 512:(ng_ + 1) * 512].rearrange("(k p) n -> p k n", p=128)), w=[wd[ng_ % 2].b], dsem="Lwd%d" % (ng_ % 2))

                load_wd(0)
                it = 0
                NIT = 4 * NT
                for j in range(min(2, NIT)):
                    load("sp", at_[j % 3], at_[j % 3][:], aT[j % NT])
                for ng_ in range(4):
                    if ng_ + 1 < 4:
                        load_wd(ng_ + 1)
                    w_ = wd[ng_ % 2]
                    for gt in range(NT):
                        i = it % 2
                        a3 = at_[it % 3]
                        if it + 2 < NIT:
                            load("sp", at_[(it + 2) % 3], at_[(it + 2) % 3][:], aT[(it + 2) % NT])
                        it += 1
                        rows = slice(gt * 128, (gt + 1) * 128)
                        p_ = pd[i]
                        for kc in range(NCH):
                            mm(p_, p_[:], a3[:, kc, :], w_[:, kc, :], kc == 0, kc == NCH - 1, [a3.b, w_.b])
                        act(ys[i], ys[i][:], p_[:], AF.Copy, [p_.b])
                        act(junk, junk[:], p_[:], AF.Square, [p_.b], accum=ssq[:, gt, ng_:ng_ + 1], extra_w=[ssq.b])
                        store("sp", ys[i], ydn[rows, ng_ * 512:(ng_ + 1) * 512], ys[i][:])
                        if mg is not None and ng_ >= 1:
                            next(mg, None)
                if mg is not None:
                    for _ in mg:
                        pass
                P.barrier()
            if stop == "C3a":
                st0.close()
                return
            with ExitStack() as st:
                G2 = sb(st, "G2", [128, D], F32)
                yt = [sb(st, "dyt%d" % i, [128, D], F32) for i in range(3)]
                xt = [sb(st, "dxt%d" % i, [128, D], F32) for i in range(3)]
                rs = [sb(st, "drs%d" % i, [128, 1], F32) for i in range(3)]
                load("sp", G2, G2[:], modt[l, MOD_G2])

                def ld(gt):
                    i = gt % 3
                    rows = slice(gt * 128, (gt + 1) * 128)
                    load("sp", yt[i], yt[i][:], ydn[rows, :])
                    load("sp", xt[i], xt[i][:], out[rows, :])

                ld(0)
                if NT > 1:
                    ld(1)
                for gt in range(NT):
                    i = gt % 3
                    rows = slice(gt * 128, (gt + 1) * 128)
                    if gt + 2 < NT:
                        ld(gt + 2)
                    P.op("dve", lambda e, i=i, gt=gt: e.reduce_sum(out=rs[i][:], in_=ssq[:, gt, :], axis=AX.X), r=[ssq.b], w=[rs[i].b])
                    act(rs[i], rs[i][:], rs[i][:], AF.Sqrt, [rs[i].b, cstf.b], bias=epsc, scale=1.0 / D)
                    P.op("dve", lambda e, i=i: e.reciprocal(out=rs[i][:], in_=rs[i][:]), r=[rs[i].b], w=[rs[i].b])
                    stt(yt[i], yt[i][:], yt[i][:], rs[i][:], G2[:], ALU.mult, ALU.mult, [yt[i].b, rs[i].b, G2.b])
                    tt(xt[i], xt[i][:], xt[i][:], yt[i][:], ALU.add, [xt[i].b, yt[i].b])
                    store("sp", xt[i], out[rows, :], xt[i][:])
                P.barrier()
            st0.close()

        xsrc_of = [x_in]
        for l in range(L):
            xsrc = x_in if l == 0 else out
            xsrc_of[0] = xsrc
            with ExitStack() as st:
                A1 = sb(st, "A1", [128, D], F32)
                B1 = sb(st, "B1", [128, D], F32)
                hT = sb(st, "hT", [128, 16, SBT], BF16)
                xt = [sb(st, "xt%d" % i, [128, D], F32) for i in range(2)]
                junk = sb(st, "junk", [128, D], BF16)
                hb = [sb(st, "hb%d" % i, [128, D], BF16) for i in range(2)]
                sst = [(sb(st, "ss%d" % i, [128, 1], F32), sb(st, "rs%d" % i, [128, 1], F32)) for i in range(2)]
                wb = [sb(st, "wb%d" % i, [128, 16, 512], BF16) for i in range(2)]
                stg = [sb(st, "stg%d" % i, [128, 512], F32) for i in range(3)]
                vst = [sb(st, "vst%d" % i, [128, 512], BF16) for i in range(2)]
                ptr = [ps(st, "ptr%d" % i, (128, 2048), BF16) for i in range(2)]
                pa = [ps(st, "pa%d" % i) for i in range(4)]
                load("sp", A1, A1[:], modt[l, MOD_A1])
                load("sp", B1, B1[:], modt[l, MOD_B1])
                nst = 0
                npa = 0
                nv = 0
                for sbi in range(NSB):
                    for ti in range(SBT // 128):
                        gt = sbi * (SBT // 128) + ti
                        x_ = xt[gt % 2]
                        load("sp", x_, x_[:], xsrc[gt * 128:(gt + 1) * 128, :])
                        rs = rms_rstd(sst[gt % 2], x_, x_[:], junk, [x_.b])
                        h_ = hb[gt % 2]
                        stt(x_, x_[:], x_[:], rs[:], A1[:], ALU.mult, ALU.mult, [x_.b, rs.b, A1.b])
                        tt(h_, h_[:], x_[:], B1[:], ALU.add, [x_.b, B1.b])
                        p_ = ptr[gt % 2]
                        for kc in range(16):
                            P.op("pe", lambda e, kc=kc: e.transpose(p_[:, kc * 128:(kc + 1) * 128], h_[:, kc * 128:(kc + 1) * 128], ident[:]), r=[h_.b, ident.b], w=[p_.b])
                        act(hT, hT[:, :, ti * 128:(ti + 1) * 128], p_[:].rearrange("p (k t) -> p k t", k=16), AF.Copy, [p_.b])
                    for mg in range(7):
                        wt = wb[mg % 2]
                        load("pool", wt, wt[:], w_in[l, :, mg * 512:(mg + 1) * 512].rearrange("(k p) n -> p k n", p=128))
                        if mg < 6:
                            nm = 3 if mg == 2 else 4
                            for tb in range(SBT // 512):
                                for m in range(nm):
                                    p_ = pa[npa % 4]
                                    npa += 1
                                    for kc in range(16):
                                        mm(p_, p_[:], wt[:, kc, m * 128:(m + 1) * 128], hT[:, kc, tb * 512:(tb + 1) * 512], kc == 0, kc == 15, [wt.b, hT.b])
                                    s_ = stg[nst % 3]
                                    nst += 1
                                    if nst % 2:
                                        act(s_, s_[:], p_[:], AF.Copy, [p_.b])
                                    else:
                                        cp(s_, s_[:], p_[:], [p_.b])
                                    row = mg * 512 + m * 128
                                    c0 = sbi * SBT + tb * 512
                                    store("sp", s_, projT[row:row + 128, c0:c0 + 512], s_[:])
                        else:
                            for ti in range(SBT // 128):
                                p_ = pa[npa % 4]
                                npa += 1
                                for kc in range(16):
                                    mm(p_, p_[:], hT[:, kc, ti * 128:(ti + 1) * 128], wt[:, kc, :], kc == 0, kc == 15, [wt.b, hT.b])
                                v_ = vst[nv % 2]
                                nv += 1
                                act(v_, v_[:], p_[:], AF.Copy, [p_.b])
                                t0 = sbi * SBT + ti * 128
                                store("sp", v_, vtok[t0:t0 + 128, :], v_[:])
                P.barrier()
            if stop == "A":
                continue
            phase_s5(l)
            if stop == "S5":
                continue
            phase_mla(l)
            if stop == "MLA":
                continue
            phase_hg(l)
            if stop == "B":
                continue
            phase_c(l)
        P.barrier(["sp"])
    return nc


def _consts():
    c = np.zeros((128, 1024), np.float32)
    c[:, 0:128] = np.eye(128, dtype=np.float32)
    perm = np.zeros((128, 128), np.float32)
    for i in range(64):
        perm[i, 64 + i] = 1.0
        perm[64 + i, i] = 1.0
    c[:, 128:256] = perm
    s = np.arange(64)
    c[0:64, 256:320] = (s[:, None] <= s[None, :]).astype(np.float32)
    c[0:64, 320] = 1.0
    c[64:128, 320] = -1.0
    invf = 1.0 / (10000.0 ** (np.arange(0, 64, 2, dtype=np.float32) / 64.0))
    c[0:32, 321] = invf
    c[32:64, 321] = invf
    c[0:32, 322] = -1.0
    c[32:64, 322] = 1.0
    c[:, 323] = EPS
    c[:, 384:448] = np.arange(64, dtype=np.float32)[None, :]
    k = np.arange(128)[:, None]
    q = np.arange(512)[None, :]
    mA = np.concatenate([(q >= j * 128 + k).astype(np.float32) for j in range(4)], axis=1)
    return c, mA


def prep_shared(inp, L):
    f = lambda a: np.ascontiguousarray(np.asarray(a, dtype=np.float32))
    w_in = f(inp["w_in"])
    wa = np.zeros((L, D, WIN), np.float32)
    wa[:, :, 0:1280] = w_in[:, :, 0:1280]
    wa[:, :, 1280:1344] = w_in[:, :, 1280:1344]
    wa[:, :, 1344:1376] = w_in[:, :, 1312:1344]
    wa[:, :, 1376:1408] = w_in[:, :, 1280:1312]
    wa[:, :, 1536:2048] = w_in[:, :, 1344:1856]
    wa[:, :, 2048:2560] = w_in[:, :, 1856:2368]
    wa[:, :, 2560:3072] = w_in[:, :, 2880:3392]
    wa[:, :, 3072:3584] = w_in[:, :, 2368:2880]
    wuq = f(inp["mla_w_uq"]).reshape(L, 512, 8, 192)
    wq = np.zeros((L, 512, 8, 256), np.float32)
    wq[..., 0:192] = wuq
    wq[..., 192:224] = wuq[..., 160:192]
    wq[..., 224:256] = wuq[..., 128:160]
    norms = np.stack([f(inp["mix_pre_norm"]), f(inp["mix_post_norm"]), f(inp["ffn_pre_norm"]), f(inp["ffn_post_norm"])], axis=1)
    pvec = np.zeros((128, L, 16), np.float32)
    pvec[:, :, 0:4] = f(inp["mla_q_norm"]).reshape(L, 4, 128).transpose(2, 0, 1)
    pvec[:, :, 4:6] = f(inp["mla_kv_norm"]).reshape(L, 2, 128).transpose(2, 0, 1)
    pvec[:, :, 6] = f(inp["hg_out_norm"]).T
    pvec[:, :, 7:11] = f(inp["s5_d"]).reshape(L, 4, 128).transpose(2, 0, 1)
    convp = np.zeros((128, L, 86, 4), np.float32)
    convp[:, :, :, 0:3] = f(inp["ffn_conv_w"]).reshape(L, 3, 86, 128).transpose(3, 0, 2, 1)
    convp[:, :, :, 3] = f(inp["ffn_conv_b"]).reshape(L, 86, 128).transpose(2, 0, 1)
    lre, lim, ldt = f(inp["s5_lambda_re"]), f(inp["s5_lambda_im"]), f(inp["s5_log_dt"])
    ldt_e = np.broadcast_to(ldt[:, :, None], (L, 32, 64))
    s5lam = np.stack([lre.reshape(L, 2048), lim.reshape(L, 2048), np.ascontiguousarray(ldt_e).reshape(L, 2048)], axis=1)
    s5lamp = np.zeros((128, L, 3, 32), np.float32)
    for i, a in enumerate((lre, lim, ldt_e)):
        t = np.asarray(a).transpose(2, 0, 1)
        s5lamp[0:64, :, i, :] = t
        s5lamp[64:128, :, i, :] = t
    s5b = np.zeros((L, 2, 128, 32, 64), np.float32)
    s5c = np.zeros((L, 2, 128, 32, 128), np.float32)
    bre, bim = f(inp["s5_b_re"]), f(inp["s5_b_im"])
    cre, cim = f(inp["s5_c_re"]), f(inp["s5_c_im"])
    for g in range(32):
        g8 = g % 8
        s5b[:, 0, g8 * 16:(g8 + 1) * 16, g, :] = bre[:, g].transpose(0, 2, 1)
        s5b[:, 1, g8 * 16:(g8 + 1) * 16, g, :] = bim[:, g].transpose(0, 2, 1)
        s5c[:, 0, 0:64, g, g8 * 16:(g8 + 1) * 16] = cre[:, g].transpose(0, 2, 1)
        s5c[:, 0, 64:128, g, g8 * 16:(g8 + 1) * 16] = cim[:, g].transpose(0, 2, 1)
        s5c[:, 1, 0:64, g, g8 * 16:(g8 + 1) * 16] = cim[:, g].transpose(0, 2, 1)
        s5c[:, 1, 64:128, g, g8 * 16:(g8 + 1) * 16] = cre[:, g].transpose(0, 2, 1)
    lbl = f(inp["hg_lb_logits"]).reshape(L, 4, 128).transpose(2, 1, 0)
    cst, mA = _consts()
    return {
        "w_in": wa, "w_out": f(inp["w_out"]), "w_up": f(inp["ffn_w_up"]), "w_down": f(inp["ffn_w_down"]),
        "w_ada": f(inp["w_ada"]), "b_ada": f(inp["b_ada"]), "w_uq": np.ascontiguousarray(wq.reshape(L, 512, 2048)),
        "w_ukv": f(inp["mla_w_ukv"]), "w_glu": f(inp["s5_w_glu"]), "norms": np.ascontiguousarray(norms),
        "pvec": pvec, "convp": convp, "s5lam": np.ascontiguousarray(s5lam), "s5lamp": s5lamp,
        "s5b": np.ascontiguousarray(s5b.reshape(L, 2, 128, 2048)), "s5c": np.ascontiguousarray(s5c.reshape(L, 2, 128, 4096)),
        "lbl": np.ascontiguousarray(lbl), "cst": cst, "maskA": mA,
    }


def prep_core(inp, b):
    x = np.ascontiguousarray(np.asarray(inp["x"][b], dtype=np.float32))
    pos = np.ascontiguousarray(np.asarray(inp["positions"][b], dtype=np.int32)).reshape(1, -1)
    cT = np.ascontiguousarray(np.asarray(inp["c"][b], dtype=np.float32).reshape(16, 128).T)
    return {"x": x, "pos": pos, "cT": cT}


def kernel(**inputs):
    B, T, _ = inputs["x"].shape
    L = inputs["w_in"].shape[0]
    nc = build(T, L)
    shared = prep_shared(inputs, L)
    in_maps = []
    for b in range(B):
        m = dict(shared)
        m.update(prep_core(inputs, b))
        in_maps.append(m)
    res = run_bass_kernel_spmd(nc, in_maps, core_ids=list(range(B)))
    return np.stack([np.asarray(r["out"], dtype=np.float32) for r in res.results], axis=0)
```

```python
import math
from contextlib import ExitStack
import numpy as np
import concourse.bass as bass
import concourse.mybir as mybir
from concourse.bass_utils import run_bass_kernel_spmd

F32 = mybir.dt.float32
BF16 = mybir.dt.bfloat16
I32 = mybir.dt.int32
AF = mybir.ActivationFunctionType
ALU = mybir.AluOpType
AX = mybir.AxisListType

D = 2048
DFF = 5504
NCH = 43
WIN = 3584
EPS = 1e-6
TWO_PI = 2.0 * math.pi
C1_2PI = 6.28125
C2_2PI = TWO_PI - 6.28125

COMPUTE = ("pe", "act", "dve", "pool")


class Buf:
    __slots__ = ("w", "r")

    def __init__(self):
        self.w = {}
        self.r = {}


class Tile:
    def __init__(self, t, name):
        self.t = t
        self.b = Buf()
        self.name = name

    def __getitem__(self, k):
        return self.t[k]


class Prog:
    def __init__(self, nc, es):
        self.nc = nc
        self.es = es
        self.eh = {"pe": nc.tensor, "act": nc.scalar, "dve": nc.vector, "pool": nc.gpsimd, "sp": nc.sync}
        self.clock = {e: {} for e in self.eh}
        self.val = {}
        self.sems = {}
        self.n_ins = 0

    def sem(self, key):
        s = self.sems.get(key)
        if s is None:
            s = self.es.enter_context(self.nc.semaphore("s_" + key.replace(":", "_")))
            self.sems[key] = s
        return s

    def op(self, eng, fn, r=(), w=(), dsem=None):
        deps = {}
        war = {}
        for b in r:
            for k, v in b.w.items():
                if deps.get(k, 0) < v:
                    deps[k] = v
        for b in w:
            for k, v in b.w.items():
                if deps.get(k, 0) < v:
                    deps[k] = v
            for k, v in b.r.items():
                if war.get(k, 0) < v:
                    war[k] = v
        if dsem is None and eng == "pe":
            war.pop("pe", None)
            deps.pop("pe", None)
        for k, v in war.items():
            if deps.get(k, 0) < v:
                deps[k] = v
        clk = self.clock[eng]
        e = self.eh[eng]
        for k, v in deps.items():
            if clk.get(k, 0) < v:
                e.wait_ge(self.sem(k), v)
                clk[k] = v
        ins = fn(e)
        self.n_ins += 1
        if dsem is None:
            key = eng
            val = self.val.get(key, 0) + 1
            ins.then_inc(self.sem(key), 1)
        else:
            key = "d:" + dsem
            val = self.val.get(key, 0) + 16
            ins.then_inc(self.sem(key), 16)
        self.val[key] = val
        for b in w:
            b.w = {key: val}
            b.r = {}
        for b in r:
            if b.r.get(key, 0) < val:
                b.r[key] = val
        return (key, val)

    def barrier(self, engines=None):
        for eng in (engines or self.eh):
            clk = self.clock[eng]
            e = self.eh[eng]
            for k, v in self.val.items():
                if clk.get(k, 0) < v:
                    e.wait_ge(self.sem(k), v)
                    clk[k] = v


def build(T, L, dbg=False, stop=None):
    NB = T // 512
    NT = T // 128
    SBT = min(T, 2048)
    NSB = T // SBT
    NCK = T // 64
    nc = bass.Bass("TRN2", target_bir_lowering=False)

    def din(name, shape, dt=F32):
        return nc.dram_tensor(name, list(shape), dt, kind="ExternalInput").ap()

    def dscr(name, shape, dt=F32):
        return nc.dram_tensor(name, list(shape), dt, kind=("ExternalOutput" if dbg else "Internal")).ap()

    x_in = din("x", [T, D])
    pos_in = din("pos", [1, T], I32)
    cT_in = din("cT", [128, 16])
    w_in = din("w_in", [L, D, WIN])
    w_out = din("w_out", [L, D, D])
    w_up = din("w_up", [L, D, 2 * DFF])
    w_down = din("w_down", [L, DFF, D])
    w_ada = din("w_ada", [L, D, 6 * D])
    b_ada = din("b_ada", [L, 6 * D])
    w_uq = din("w_uq", [L, 512, 2048])
    w_ukv = din("w_ukv", [L, 256, 2048])
    w_glu = din("w_glu", [L, 512, 512])
    norms = din("norms", [L, 4, D])
    pvec = din("pvec", [128, L, 16])
    convp = din("convp", [128, L, 86, 4])
    s5lam = din("s5lam", [L, 3, 2048])
    s5lamp = din("s5lamp", [128, L, 3, 32])
    s5b = din("s5b", [L, 2, 128, 2048])
    s5c = din("s5c", [L, 2, 128, 4096])
    lbl = din("lbl", [128, 4, L])
    cst = din("cst", [128, 1024])
    maskA_in = din("maskA", [128, 2048])
    out = nc.dram_tensor("out", [T, D], F32, kind="ExternalOutput").ap()

    modt = dscr("modt", [L, 6, 128, D])
    projT = dscr("projT", [3072, T])
    vtok = dscr("vtok", [T, 512], BF16)
    yT = dscr("yT", [D, T], BF16)
    h2T = dscr("h2T", [D, T], BF16)
    aT = dscr("aT", [T // 128, 128, NCH, 128], BF16)
    ydn = dscr("ydn", [T, D])
    ropeT = dscr("ropeT", [2, 64, T])

    with ExitStack() as es:
        P = Prog(nc, es)

        uid = [0]

        def sb(stack, name, shape, dt):
            uid[0] += 1
            return Tile(stack.enter_context(nc.sbuf_tensor("%s_u%d" % (name, uid[0]), list(shape), dt)), name)

        def ps(stack, name, shape=(128, 512), dt=F32):
            uid[0] += 1
            return Tile(stack.enter_context(nc.psum_tensor("%s_u%d" % (name, uid[0]), list(shape), dt)), name)

        def load(eng, dst, dst_ap, src_ap):
            return P.op(eng, lambda e: e.dma_start(out=dst_ap, in_=src_ap), w=[dst.b], dsem="L" + dst.name)

        def store(eng, src, dst_ap, src_ap):
            return P.op(eng, lambda e: e.dma_start(out=dst_ap, in_=src_ap), r=[src.b], dsem="S" + src.name)

        def mm(o, o_ap, lhsT_ap, rhs_ap, start, stop, r):
            return P.op("pe", lambda e: e.matmul(o_ap, lhsT_ap, rhs_ap, start=start, stop=stop), r=r, w=[o.b])

        def act(o, o_ap, i_ap, func, r, bias=None, scale=None, accum=None, extra_w=()):
            kw = {}
            if bias is not None:
                kw["bias"] = bias
            if scale is not None:
                kw["scale"] = scale
            if accum is not None:
                kw["accum_out"] = accum
            return P.op("act", lambda e: e.activation(out=o_ap, in_=i_ap, func=func, **kw), r=r, w=[o.b] + list(extra_w))

        def tt(o, o_ap, a_ap, b_ap, op, r, eng="dve"):
            return P.op(eng, lambda e: e.tensor_tensor(out=o_ap, in0=a_ap, in1=b_ap, op=op), r=r, w=[o.b])

        def ts(o, o_ap, a_ap, s1, s2, op0, op1, r, eng="dve"):
            if op1 is None:
                return P.op(eng, lambda e: e.tensor_scalar(out=o_ap, in0=a_ap, scalar1=s1, scalar2=None, op0=op0), r=r, w=[o.b])
            return P.op(eng, lambda e: e.tensor_scalar(out=o_ap, in0=a_ap, scalar1=s1, scalar2=s2, op0=op0, op1=op1), r=r, w=[o.b])

        def stt(o, o_ap, a_ap, s, b_ap, op0, op1, r):
            return P.op("dve", lambda e: e.scalar_tensor_tensor(out=o_ap, in0=a_ap, scalar=s, in1=b_ap, op0=op0, op1=op1), r=r, w=[o.b])

        def cp(o, o_ap, i_ap, r, eng="dve"):
            return P.op(eng, lambda e: e.tensor_copy(out=o_ap, in_=i_ap), r=r, w=[o.b])

        def mset(o, o_ap, v, eng="dve"):
            return P.op(eng, lambda e: e.memset(o_ap, v), w=[o.b])

        cstf = sb(es, "cstf", [128, 1024], F32)
        ident = sb(es, "ident", [128, 128], BF16)
        ones = sb(es, "ones", [128, 128], BF16)
        mask64 = sb(es, "mask64", [128, 64], BF16)
        rst = sb(es, "rst", [128, 512], F32)
        lbt = sb(es, "lbt", [128, 4, L], F32)
        omlt = sb(es, "omlt", [128, 4, L], F32)
        nomlt = sb(es, "nomlt", [128, 4, L], F32)
        pv = sb(es, "pv", [128, L, 16], F32)
        load("sp", cstf, cstf[:], cst)
        load("sp", pv, pv[:], pvec)
        cp(ident, ident[:], cstf[:, 0:128], [cstf.b])
        cp(mask64, mask64[:], cstf[:, 256:320], [cstf.b])
        mset(ones, ones[:], 1.0)
        mset(rst, rst[:], 1.0)
        mset(rst, rst[:, 0:512:64], 0.0)
        rst32 = sb(es, "rst32", [128, 512], F32)
        mset(rst32, rst32[:], 1.0)
        mset(rst32, rst32[:, 0:512:32], 0.0)
        permsw = cstf[:, 128:256]
        sgnB = cstf[:, 320:321]
        invf = cstf[0:64, 321:322]
        sgnr = cstf[0:64, 322:323]
        epsc = cstf[:, 323:324]

        def sincos(stack, nm, ang, shape, o_sin, o_sin_ap, o_cos, o_cos_ap, npart=128):
            k_i = sb(stack, nm + "_ki", shape, I32)
            k_f = sb(stack, nm + "_kf", shape, F32)
            r_t = sb(stack, nm + "_r", shape, F32)
            sl = tuple([slice(0, npart)] + [slice(None)] * (len(shape) - 1))
            for which in (0, 1):
                if which == 1:
                    ts(r_t, r_t[sl], ang[sl], math.pi / 2, None, ALU.add, None, [ang.b])
                    src = r_t
                else:
                    src = ang
                ts(k_f, k_f[sl], src[sl], 1.0 / TWO_PI, None, ALU.mult, None, [src.b])
                cp(k_i, k_i[sl], k_f[sl], [k_f.b])
                cp(k_f, k_f[sl], k_i[sl], [k_i.b])
                stt(r_t, r_t[sl], k_f[sl], -C1_2PI, src[sl], ALU.mult, ALU.add, [k_f.b, src.b])
                stt(r_t, r_t[sl], k_f[sl], -C2_2PI, r_t[sl], ALU.mult, ALU.add, [k_f.b, r_t.b])
                ts(r_t, r_t[sl], r_t[sl], -3.1415925, 3.1415925, ALU.max, ALU.min, [r_t.b])
                if which == 0:
                    act(o_sin, o_sin_ap, r_t[sl], AF.Sin, [r_t.b])
                else:
                    act(o_cos, o_cos_ap, r_t[sl], AF.Sin, [r_t.b])

        with ExitStack() as st:
            lg = sb(st, "lg", [128, 4, L], F32)
            ssum = sb(st, "ssum", [128, 4], F32)
            load("sp", lg, lg[:], lbl)
            act(lg, lg[:], lg[:], AF.Exp, [lg.b])
            P.op("dve", lambda e: e.reduce_sum(out=ssum[:], in_=lg[:], axis=AX.X), r=[lg.b], w=[ssum.b])
            P.op("dve", lambda e: e.reciprocal(out=ssum[:], in_=ssum[:]), r=[ssum.b], w=[ssum.b])
            tt(lg, lg[:], lg[:], ssum[:].unsqueeze(2).broadcast_to([128, 4, L]), ALU.mult, [lg.b, ssum.b])
            mset(lbt, lbt[:, :, 0:1], 0.0)
            for l in range(1, L):
                tt(lbt, lbt[:, :, l:l + 1], lbt[:, :, l - 1:l], lg[:, :, l:l + 1], ALU.add, [lbt.b, lg.b])
            ts(omlt, omlt[:], lbt[:], -1.0, 1.0, ALU.mult, ALU.add, [lbt.b])
            ts(nomlt, nomlt[:], omlt[:], -1.0, None, ALU.mult, None, [omlt.b])
            posi = sb(st, "posi", [64, T], I32)
            ang = sb(st, "ang", [64, T], F32)
            sint = sb(st, "sint", [64, T], F32)
            cost = sb(st, "cost", [64, T], F32)
            load("sp", posi, posi[:], pos_in[0, :].partition_broadcast(64))
            cp(ang, ang[:], posi[:], [posi.b])
            ts(ang, ang[:], ang[:], invf, None, ALU.mult, None, [ang.b, cstf.b])
            sincos(st, "rp", ang, [64, T], sint, sint[:], cost, cost[:], npart=64)
            ts(sint, sint[:], sint[:], sgnr, None, ALU.mult, None, [sint.b, cstf.b])
            store("sp", cost, ropeT[0], cost[:])
            store("sp", sint, ropeT[1], sint[:])
            P.barrier()

        cbc = sb(es, "cbc", [128, 16, 128], BF16)
        with ExitStack() as st:
            cTt = sb(st, "cTt", [128, 16], F32)
            onesf = sb(st, "onesf", [128, 128], F32)
            load("sp", cTt, cTt[:], cT_in)
            act(cTt, cTt[:], cTt[:], AF.Silu, [cTt.b])
            mset(onesf, onesf[:], 1.0)
            for kc in range(16):
                ts(cbc, cbc[:, kc, :], onesf[:], cTt[:, kc:kc + 1], None, ALU.mult, None, [onesf.b, cTt.b])
            P.barrier()

        def mod_gen(l, st, pm, nb=2):
            wa = [sb(st, "wa%d" % i, [128, 16, 512], BF16) for i in range(nb)]
            bb = [sb(st, "mbb%d" % i, [128, 512], F32) for i in range(nb)]
            nn = [sb(st, "mnn%d" % i, [128, 512], F32) for i in range(nb)]
            rr = [sb(st, "mrr%d" % i, [128, 512], F32) for i in range(nb)]
            for cg in range(24):
                i = cg % nb
                sec = cg // 4
                c0 = (cg % 4) * 512
                wt = wa[i]
                load("pool", wt, wt[:], w_ada[l, :, cg * 512:(cg + 1) * 512].rearrange("(k p) n -> p k n", p=128))
                load("pool", bb[i], bb[i][:], b_ada[l, cg * 512:(cg + 1) * 512].partition_broadcast(128))
                kind = sec % 3
                if kind != 0:
                    nidx = (0 if kind == 1 else 1) + 2 * (sec // 3)
                    load("pool", nn[i], nn[i][:], norms[l, nidx, c0:c0 + 512].partition_broadcast(128))
                p_ = pm[cg % len(pm)]
                for kc in range(16):
                    mm(p_, p_[:], cbc[:, kc, :], wt[:, kc, :], kc == 0, kc == 15, [cbc.b, wt.b])
                tt(rr[i], rr[i][:], p_[:], bb[i][:], ALU.add, [p_.b, bb[i].b])
                if kind == 1:
                    stt(rr[i], rr[i][:], rr[i][:], 1.0, nn[i][:], ALU.add, ALU.mult, [rr[i].b, nn[i].b])
                elif kind == 2:
                    tt(rr[i], rr[i][:], rr[i][:], nn[i][:], ALU.mult, [rr[i].b, nn[i].b])
                store("pool", rr[i], modt[l, sec, :, c0:c0 + 512], rr[i][:])
                yield cg

        with ExitStack() as st:
            pm0 = [ps(st, "pm%d" % i) for i in range(2)]
            for _ in mod_gen(0, st, pm0):
                pass
            P.barrier()
        MOD_B1, MOD_A1, MOD_G1, MOD_B2, MOD_A2, MOD_G2 = 0, 1, 2, 3, 4, 5

        def rms_rstd(stack_tiles, src, src_ap, junk, r):
            ss, rs = stack_tiles
            act(junk, junk[:], src_ap, AF.Square, r, accum=ss[:], extra_w=[ss.b])
            act(rs, rs[:], ss[:], AF.Sqrt, [ss.b, cstf.b], bias=epsc, scale=1.0 / D)
            P.op("dve", lambda e: e.reciprocal(out=rs[:], in_=rs[:]), r=[rs.b], w=[rs.b])
            return rs

        JT = cstf[:, 384:448]

        def phase_s5(l):
            with ExitStack() as st:
                TA = sb(st, "TA", [128, 32, 64], F32)
                TB = sb(st, "TB", [128, 32, 64], F32)
                TC = sb(st, "TC", [128, 32, 64], F32)
                TD = sb(st, "TD", [128, 32, 64], F32)
                Bb1 = sb(st, "Bb1", [128, 32, 128], BF16)
                Bb2 = sb(st, "Bb2", [128, 32, 128], BF16)
                Cp1 = sb(st, "Cp1", [128, 32, 128], BF16)
                Cp2 = sb(st, "Cp2", [128, 32, 128], BF16)
                K1 = sb(st, "K1", [128, 32], F32)
                K2 = sb(st, "K2", [128, 32], F32)
                with ExitStack() as s2:
                    lamf = sb(s2, "lamf", [128, 3, 2048], F32)
                    load("sp", lamf, lamf[:].rearrange("p a n -> p (a n)"), s5lam[l].rearrange("a n -> (a n)").partition_broadcast(128))
                    lr, li, ld = lamf[:, 0, :], lamf[:, 1, :], lamf[:, 2, :]
                    tl = [sb(s2, "tl%d" % i, [128, 2048], F32) for i in range(6)]
                    dtf, af, ef, sf, cf, phf = tl
                    act(dtf, dtf[:], ld, AF.Exp, [lamf.b])
                    tt(af, af[:], lr, dtf[:], ALU.mult, [lamf.b, dtf.b])
                    tt(phf, phf[:], li, dtf[:], ALU.mult, [lamf.b, dtf.b])
                    act(ef, ef[:], af[:], AF.Exp, [af.b])
                    sincos(s2, "sf", phf, [128, 2048], sf, sf[:], cf, cf[:])
                    tt(cf, cf[:], cf[:], ef[:], ALU.mult, [cf.b, ef.b])
                    ts(cf, cf[:], cf[:], -1.0, None, ALU.add, None, [cf.b])
                    tt(sf, sf[:], sf[:], ef[:], ALU.mult, [sf.b, ef.b])
                    tt(dtf, dtf[:], lr, lr, ALU.mult, [lamf.b])
                    tt(af, af[:], li, li, ALU.mult, [lamf.b])
                    tt(dtf, dtf[:], dtf[:], af[:], ALU.add, [dtf.b, af.b])
                    P.op("dve", lambda e: e.reciprocal(out=dtf[:], in_=dtf[:]), r=[dtf.b], w=[dtf.b])
                    tt(ef, ef[:], cf[:], lr, ALU.mult, [cf.b, lamf.b])
                    tt(af, af[:], sf[:], li, ALU.mult, [sf.b, lamf.b])
                    tt(ef, ef[:], ef[:], af[:], ALU.add, [ef.b, af.b])
                    tt(ef, ef[:], ef[:], dtf[:], ALU.mult, [ef.b, dtf.b])
                    tt(phf, phf[:], sf[:], lr, ALU.mult, [sf.b, lamf.b])
                    tt(af, af[:], cf[:], li, ALU.mult, [cf.b, lamf.b])
                    tt(phf, phf[:], phf[:], af[:], ALU.subtract, [phf.b, af.b])
                    tt(phf, phf[:], phf[:], dtf[:], ALU.mult, [phf.b, dtf.b])
                    cr3 = ef[:].rearrange("p (g q) -> p g q", q=64)
                    ci3 = phf[:].rearrange("p (g q) -> p g q", q=64)
                    braw = sb(s2, "braw", [128, 2, 2048], F32)
                    load("sp", braw, braw[:], s5b[l].rearrange("a k n -> k a n"))
                    bre3 = braw[:, 0, :].rearrange("p (g q) -> p g q", q=64)
                    bim3 = braw[:, 1, :].rearrange("p (g q) -> p g q", q=64)
                    t1 = af[:].rearrange("p (g q) -> p g q", q=64)
                    t2 = sf[:].rearrange("p (g q) -> p g q", q=64)
                    tt(af, t1, cr3, bre3, ALU.mult, [ef.b, braw.b])
                    tt(sf, t2, ci3, bim3, ALU.mult, [phf.b, braw.b])
                    tt(Bb1, Bb1[:, :, 0:64], t1, t2, ALU.subtract, [af.b, sf.b])
                    tt(af, t1, cr3, bim3, ALU.mult, [ef.b, braw.b])
                    tt(sf, t2, ci3, bre3, ALU.mult, [phf.b, braw.b])
                    tt(Bb1, Bb1[:, :, 64:128], t1, t2, ALU.add, [af.b, sf.b])
                    cp(Bb2, Bb2[:, :, 0:64], Bb1[:, :, 64:128], [Bb1.b])
                    cp(Bb2, Bb2[:, :, 64:128], Bb1[:, :, 0:64], [Bb1.b])
                    P.barrier()
                with ExitStack() as s2:
                    craw = sb(s2, "craw", [128, 2, 4096], F32)
                    load("sp", craw, craw[:], s5c[l].rearrange("a k n -> k a n"))
                    ts(Cp1, Cp1[:].rearrange("p g m -> p (g m)"), craw[:, 0, :], sgnB, None, ALU.mult, None, [craw.b, cstf.b])
                    ts(Cp2, Cp2[:].rearrange("p g m -> p (g m)"), craw[:, 1, :], -1.0, None, ALU.mult, None, [craw.b])
                    lamp = sb(s2, "lamp", [128, 3, 32], F32)
                    load("sp", lamp, lamp[:], s5lamp[:, l])
                    dtp = sb(s2, "dtp", [128, 32], F32)
                    app = sb(s2, "app", [128, 32], F32)
                    php = sb(s2, "php", [128, 32], F32)
                    act(dtp, dtp[:], lamp[:, 2, :], AF.Exp, [lamp.b])
                    tt(app, app[:], lamp[:, 0, :], dtp[:], ALU.mult, [lamp.b, dtp.b])
                    tt(php, php[:], lamp[:, 1, :], dtp[:], ALU.mult, [lamp.b, dtp.b])
                    ang3 = sb(s2, "ang3", [128, 32, 64], F32)
                    ma3 = sb(s2, "ma3", [128, 32, 64], F32)
                    sn3 = sb(s2, "sn3", [128, 32, 64], F32)
                    cs3 = sb(s2, "cs3", [128, 32, 64], F32)
                    jb = JT.unsqueeze(1).broadcast_to([128, 32, 64])
                    tt(ang3, ang3[:], php[:].unsqueeze(2).broadcast_to([128, 32, 64]), jb, ALU.mult, [php.b, cstf.b])
                    tt(ma3, ma3[:], app[:].unsqueeze(2).broadcast_to([128, 32, 64]), jb, ALU.mult, [app.b, cstf.b])
                    sincos(s2, "s3", ang3, [128, 32, 64], sn3, sn3[:], cs3, cs3[:])
                    act(ang3, ang3[:], ma3[:], AF.Exp, [ma3.b], scale=-1.0)
                    tt(TA, TA[:], ang3[:], cs3[:], ALU.mult, [ang3.b, cs3.b])
                    tt(TB, TB[:], ang3[:], sn3[:], ALU.mult, [ang3.b, sn3.b])
                    ts(TB, TB[:], TB[:], sgnB, None, ALU.mult, None, [TB.b, cstf.b])
                    act(ang3, ang3[:], ma3[:], AF.Exp, [ma3.b])
                    tt(TC, TC[:], ang3[:], cs3[:], ALU.mult, [ang3.b, cs3.b])
                    tt(TD, TD[:], ang3[:], sn3[:], ALU.mult, [ang3.b, sn3.b])
                    a64 = sb(s2, "a64", [128, 32], F32)
                    p64 = sb(s2, "p64", [128, 32], F32)
                    s64 = sb(s2, "s64", [128, 32], F32)
                    c64 = sb(s2, "c64", [128, 32], F32)
                    ts(a64, a64[:], app[:], 64.0, None, ALU.mult, None, [app.b])
                    ts(p64, p64[:], php[:], 64.0, None, ALU.mult, None, [php.b])
                    sincos(s2, "s6", p64, [128, 32], s64, s64[:], c64, c64[:])
                    act(a64, a64[:], a64[:], AF.Exp, [a64.b])
                    tt(K1, K1[:], a64[:], c64[:], ALU.mult, [a64.b, c64.b])
                    tt(K2, K2[:], a64[:], s64[:], ALU.mult, [a64.b, s64.b])
                    ts(K2, K2[:], K2[:], sgnB, -1.0, ALU.mult, ALU.mult, [K2.b, cstf.b])
                    P.barrier()
                sy = ExitStack()
                ygb = sb(sy, "ygb", [128, 4, T], BF16)
                with ExitStack() as s2:
                    uTb = [sb(s2, "uTb%d" % i, [128, 2, 512], BF16) for i in range(2)]
                    Sall2 = [sb(s2, "Sall%d" % i, [128, 16, 512], F32) for i in range(2)]
                    Hs1 = sb(s2, "Hs1", [128, 16, NCK + 1], F32)
                    H2 = [sb(s2, "H2%d" % i, [128, 16], F32) for i in range(2)]
                    S63 = sb(s2, "S63", [128, 16, 8], F32)
                    S63s = sb(s2, "S63s", [128, 16, 8], F32)
                    w1 = [sb(s2, "w1%d" % i, [128, 512], F32) for i in range(3)]
                    w2 = [sb(s2, "w2%d" % i, [128, 512], F32) for i in range(3)]
                    spt = [sb(s2, "spt%d" % i, [128, 512], F32) for i in range(2)]
                    V1 = [sb(s2, "V1%d" % i, [128, 512], BF16) for i in range(2)]
                    V2 = [sb(s2, "V2%d" % i, [128, 512], BF16) for i in range(2)]
                    zt = [sb(s2, "zt%d" % i, [128, 16], F32) for i in range(6)]
                    yv = sb(s2, "yv", [128, 512], F32)
                    pbu = [ps(s2, "pbu%d" % i) for i in range(4)]
                    psw = ps(s2, "psw")
                    py = [ps(s2, "py%d" % i) for i in range(2)]
                    nwl = [0]
                    for half in range(2):
                        g0 = half * 16
                        mset(Hs1, Hs1[:, :, 0:1], 0.0)
                        mset(H2[0], H2[0][:], 0.0)
                        hcl = [0]
                        K1h = K1[:, g0:g0 + 16]
                        K2h = K2[:, g0:g0 + 16]
                        def P1(tb):
                            tsl = slice(tb * 512, (tb + 1) * 512)
                            Sall = Sall2[tb % 2]
                            uT = uTb[tb % 2]
                            for c in range(2):
                                load("pool", uT, uT[:, c, :], projT[(2 * half + c) * 128:(2 * half + c + 1) * 128, tsl])
                            pend = None
                            for gi in range(17):
                                if gi < 16:
                                    g = g0 + gi
                                    c = gi // 8
                                    nw = nwl[0]
                                    p1 = pbu[(2 * nw) % 4]
                                    p2 = pbu[(2 * nw + 1) % 4]
                                    a_, b_ = w1[nw % 3], w2[nw % 3]
                                    nwl[0] += 1
                                    mm(p1, p1[:], Bb1[:, g, :], uT[:, c, :], True, True, [Bb1.b, uT.b])
                                    mm(p2, p2[:], Bb2[:, g, :], uT[:, c, :], True, True, [Bb2.b, uT.b])
                                    tab = TA[:, g, :].unsqueeze(1).broadcast_to([128, 8, 64])
                                    tbb = TB[:, g, :].unsqueeze(1).broadcast_to([128, 8, 64])
                                    tt(a_, a_[:].rearrange("p (n j) -> p n j", j=64), p1[:].rearrange("p (n j) -> p n j", j=64), tab, ALU.mult, [p1.b, TA.b])
                                    tt(b_, b_[:].rearrange("p (n j) -> p n j", j=64), p2[:].rearrange("p (n j) -> p n j", j=64), tbb, ALU.mult, [p2.b, TB.b])
                                    tt(a_, a_[:], a_[:], b_[:], ALU.add, [a_.b, b_.b])
                                if pend is not None:
                                    pgi, pa = pend
                                    P.op("dve", lambda e, pgi=pgi, pa=pa: e.tensor_tensor_scan(out=Sall[:, pgi, :], data0=rst[:], data1=pa[:], initial=0.0, op0=ALU.mult, op1=ALU.add), r=[rst.b, pa.b], w=[Sall.b])
                                pend = (gi, a_) if gi < 16 else None

                        def BD(tb):
                            Sall = Sall2[tb % 2]
                            cp(S63, S63[:], Sall[:, :, 63:512:64], [Sall.b])
                            mm(psw, psw[:, 0:128], permsw, S63[:].rearrange("p g n -> p (g n)"), True, True, [cstf.b, S63.b])
                            cp(S63s, S63s[:].rearrange("p g n -> p (g n)"), psw[:, 0:128], [psw.b])
                            for n8 in range(8):
                                n = tb * 8 + n8
                                z1, z2, t1_, t2_, t3_, t4_ = zt
                                hcur = hcl[0]
                                hc, hn = H2[hcur], H2[1 - hcur]
                                tt(z1, z1[:], S63[:, :, n8], Hs1[:, :, n], ALU.add, [S63.b, Hs1.b])
                                tt(z2, z2[:], S63s[:, :, n8], hc[:], ALU.add, [S63s.b, hc.b])
                                tt(t1_, t1_[:], z1[:], K1h, ALU.mult, [z1.b, K1.b])
                                tt(t2_, t2_[:], z2[:], K2h, ALU.mult, [z2.b, K2.b])
                                tt(Hs1, Hs1[:, :, n + 1], t1_[:], t2_[:], ALU.add, [t1_.b, t2_.b])
                                tt(t3_, t3_[:], z2[:], K1h, ALU.mult, [z2.b, K1.b])
                                tt(t4_, t4_[:], z1[:], K2h, ALU.mult, [z1.b, K2.b])
                                tt(hn, hn[:], t3_[:], t4_[:], ALU.subtract, [t3_.b, t4_.b])
                                hcl[0] = 1 - hcur

                        def P2(tb):
                            tsl = slice(tb * 512, (tb + 1) * 512)
                            Sall = Sall2[tb % 2]
                            uT = uTb[tb % 2]
                            for c in range(2):
                                ch = 2 * half + c
                                py_ = py[(tb * 2 + c) % 2]
                                for g8 in range(8):
                                    gi = c * 8 + g8
                                    g = g0 + gi
                                    s_ = spt[g8 % 2]
                                    v1_, v2_ = V1[g8 % 2], V2[g8 % 2]
                                    hb_ = Hs1[:, gi, tb * 8:tb * 8 + 8].unsqueeze(2).broadcast_to([128, 8, 64])
                                    s3 = s_[:].rearrange("p (n j) -> p n j", j=64)
                                    for n8 in range(8):
                                        act(s_, s_[:, n8 * 64:(n8 + 1) * 64], Sall[:, gi, n8 * 64:(n8 + 1) * 64], AF.Identity, [Sall.b, Hs1.b], bias=Hs1[:, gi, tb * 8 + n8:tb * 8 + n8 + 1])
                                    tt(v1_, v1_[:].rearrange("p (n j) -> p n j", j=64), s3, TC[:, g, :].unsqueeze(1).broadcast_to([128, 8, 64]), ALU.mult, [s_.b, TC.b], eng="pool")
                                    tt(v2_, v2_[:].rearrange("p (n j) -> p n j", j=64), s3, TD[:, g, :].unsqueeze(1).broadcast_to([128, 8, 64]), ALU.mult, [s_.b, TD.b], eng="pool")
                                    mm(py_, py_[:], Cp1[:, g, :], v1_[:], g8 == 0, False, [Cp1.b, v1_.b])
                                    mm(py_, py_[:], Cp2[:, g, :], v2_[:], False, g8 == 7, [Cp2.b, v2_.b])
                                stt(yv, yv[:], uT[:, c, :], pv[:, l, 7 + ch:8 + ch], py_[:], ALU.mult, ALU.add, [uT.b, pv.b, py_.b])
                                act(ygb, ygb[:, ch, tsl], yv[:], AF.Gelu_apprx_tanh, [yv.b])

                        P1(0)
                        for tb in range(NB):
                            BD(tb)
                            if tb + 1 < NB:
                                P1(tb + 1)
                            P2(tb)
                    P.barrier()
                with ExitStack() as s2:
                    wg = sb(s2, "wg", [128, 4, 512], BF16)
                    sg = [sb(s2, "sg%d" % i, [128, 512], F32) for i in range(2)]
                    yo = [sb(s2, "yo%d" % i, [128, 512], BF16) for i in range(2)]
                    pg = [ps(s2, "pg%d" % i) for i in range(2)]
                    load("pool", wg, wg[:], w_glu[l].rearrange("(k p) n -> p k n", p=128))
                    i = 0
                    for tb in range(NB):
                        tsl = slice(tb * 512, (tb + 1) * 512)
                        for mch in range(4):
                            p_ = pg[i % 2]
                            for kc in range(4):
                                mm(p_, p_[:], wg[:, kc, mch * 128:(mch + 1) * 128], ygb[:, kc, tsl], kc == 0, kc == 3, [wg.b, ygb.b])
                            act(sg[i % 2], sg[i % 2][:], p_[:], AF.Sigmoid, [p_.b])
                            tt(yo[i % 2], yo[i % 2][:], ygb[:, mch, tsl], sg[i % 2][:], ALU.mult, [ygb.b, sg[i % 2].b])
                            store("sp", yo[i % 2], yT[mch * 128:(mch + 1) * 128, tsl], yo[i % 2][:])
                            i += 1
                    P.barrier()
                sy.close()

        def phase_mla(l):
            SCALE = 192.0 ** -0.5
            with ExitStack() as st:
                cqn = sb(st, "cqn", [128, 4, T], BF16)
                ckn = sb(st, "ckn", [128, 2, T], BF16)
                krr = sb(st, "krr", [64, T], BF16)
                wq = sb(st, "wq", [128, 4, 2048], BF16)
                wkv = sb(st, "wkv", [128, 2, 2048], BF16)
                maskA = sb(st, "maskA", [128, 2048], BF16)
                load("pool", maskA, maskA[:], maskA_in)
                cosT = sb(st, "cosT", [64, T], F32)
                sinT = sb(st, "sinT", [64, T], F32)
                load("pool", wq, wq[:], w_uq[l].rearrange("(k p) n -> p k n", p=128))
                load("pool", wkv, wkv[:], w_ukv[l].rearrange("(k p) n -> p k n", p=128))
                load("sp", cosT, cosT[:], ropeT[0])
                load("sp", sinT, sinT[:], ropeT[1])
                with ExitStack() as s2:
                    raw = [sb(s2, "raw%d" % i, [128, 6, 512], F32) for i in range(2)]
                    sq = [sb(s2, "sq%d" % i, [128, 6, 512], BF16) for i in range(2)]
                    rq = [sb(s2, "rq%d" % i, [128, 512], F32) for i in range(2)]
                    rk = [sb(s2, "rk%d" % i, [128, 512], F32) for i in range(2)]
                    kra = [sb(s2, "kra%d" % i, [64, 512], F32) for i in range(2)]
                    krb = [sb(s2, "krb%d" % i, [64, 512], F32) for i in range(2)]
                    pq_ = [ps(s2, "pssq%d" % i) for i in range(2)]
                    pk_ = [ps(s2, "pssk%d" % i) for i in range(2)]
                    for tb in range(NB):
                        tsl = slice(tb * 512, (tb + 1) * 512)
                        i = tb % 2
                        load("sp", raw[i], raw[i][:], projT[512:1280, tsl].rearrange("(k p) t -> p k t", p=128))
                        load("sp", kra[i], kra[i][:], projT[1280:1344, tsl])
                        load("sp", krb[i], krb[i][:], projT[1344:1408, tsl])
                        act(sq[i], sq[i][:], raw[i][:], AF.Square, [raw[i].b])
                        for kc in range(4):
                            mm(pq_[i], pq_[i][:], ones[:], sq[i][:, kc, :], kc == 0, kc == 3, [ones.b, sq[i].b])
                        for kc in range(2):
                            mm(pk_[i], pk_[i][:], ones[:], sq[i][:, 4 + kc, :], kc == 0, kc == 1, [ones.b, sq[i].b])
                        act(rq[i], rq[i][:], pq_[i][:], AF.Sqrt, [pq_[i].b, cstf.b], bias=epsc, scale=1.0 / 512)
                        P.op("dve", lambda e, i=i: e.reciprocal(out=rq[i][:], in_=rq[i][:]), r=[rq[i].b], w=[rq[i].b])
                        act(rk[i], rk[i][:], pk_[i][:], AF.Sqrt, [pk_[i].b, cstf.b], bias=epsc, scale=1.0 / 256)
                        P.op("dve", lambda e, i=i: e.reciprocal(out=rk[i][:], in_=rk[i][:]), r=[rk[i].b], w=[rk[i].b])
                        for kc in range(4):
                            stt(cqn, cqn[:, kc, tsl], raw[i][:, kc, :], pv[:, l, kc:kc + 1], rq[i][:], ALU.mult, ALU.mult, [raw[i].b, pv.b, rq[i].b])
                        for kc in range(2):
                            stt(ckn, ckn[:, kc, tsl], raw[i][:, 4 + kc, :], pv[:, l, 4 + kc:5 + kc], rk[i][:], ALU.mult, ALU.mult, [raw[i].b, pv.b, rk[i].b])
                        tt(kra[i], kra[i][:], kra[i][:], cosT[:, tsl], ALU.mult, [kra[i].b, cosT.b])
                        tt(krb[i], krb[i][:], krb[i][:], sinT[:, tsl], ALU.mult, [krb[i].b, sinT.b])
                        tt(krr, krr[:, tsl], kra[i][:], krb[i][:], ALU.add, [kra[i].b, krb[i].b])
                    P.barrier()
                with ExitStack() as s2:
                    qn_ = sb(s2, "qn_", [128, T], BF16)
                    qr_ = sb(s2, "qr_", [64, T], BF16)
                    kn_ = sb(s2, "kn_", [128, T], BF16)
                    v_ = sb(s2, "v_", [128, NT, 128], BF16)
                    r1 = [sb(s2, "r1%d" % i, [64, 512], F32) for i in range(2)]
                    r2 = [sb(s2, "r2%d" % i, [64, 512], F32) for i in range(2)]
                    pT = [sb(s2, "pT%d" % i, [128, 512], BF16) for i in range(3)]
                    rsm = [sb(s2, "rsm%d" % i, [128, 512], F32) for i in range(2)]
                    yh = [sb(s2, "yh%d" % i, [128, 512], BF16) for i in range(2)]
                    pgen = [ps(s2, "pgen%d" % i) for i in range(2)]
                    pS = [ps(s2, "pS%d" % i) for i in range(2)]
                    po = [ps(s2, "po%d" % i) for i in range(2)]
                    psm = [ps(s2, "psm%d" % i) for i in range(2)]
                    ng = 0
                    npt = 0
                    nq = 0
                    mgen = mod_gen(l + 1, s2, pgen, nb=1) if l + 1 < L else None
                    nkbt = 0
                    for h in range(8):
                        c0 = h * 256
                        for tb in range(NB):
                            tsl = slice(tb * 512, (tb + 1) * 512)
                            p_ = pgen[ng % 2]; ng += 1
                            for kc in range(4):
                                mm(p_, p_[:], wq[:, kc, c0:c0 + 128], cqn[:, kc, tsl], kc == 0, kc == 3, [wq.b, cqn.b])
                            act(qn_, qn_[:, tsl], p_[:], AF.Copy, [p_.b])
                            pr = pgen[ng % 2]; ng += 1
                            for kc in range(4):
                                mm(pr, pr[0:64, :], wq[:, kc, c0 + 128:c0 + 192], cqn[:, kc, tsl], kc == 0, kc == 3, [wq.b, cqn.b])
                            a_ = r1[tb % 2]
                            tt(a_, a_[:], pr[0:64, :], cosT[:, tsl], ALU.mult, [pr.b, cosT.b])
                            pw = pgen[ng % 2]; ng += 1
                            for kc in range(4):
                                mm(pw, pw[0:64, :], wq[:, kc, c0 + 192:c0 + 256], cqn[:, kc, tsl], kc == 0, kc == 3, [wq.b, cqn.b])
                            b_ = r2[tb % 2]
                            tt(b_, b_[:], pw[0:64, :], sinT[:, tsl], ALU.mult, [pw.b, sinT.b])
                            tt(qr_, qr_[:, tsl], a_[:], b_[:], ALU.add, [a_.b, b_.b])
                            pk = pgen[ng % 2]; ng += 1
                            for kc in range(2):
                                mm(pk, pk[:], wkv[:, kc, c0:c0 + 128], ckn[:, kc, tsl], kc == 0, kc == 1, [wkv.b, ckn.b])
                            act(kn_, kn_[:, tsl], pk[:], AF.Copy, [pk.b])
                            pvv = pgen[ng % 2]; ng += 1
                            for ti in range(4):
                                t0 = tb * 512 + ti * 128
                                for kc in range(2):
                                    mm(pvv, pvv[:, ti * 128:(ti + 1) * 128], ckn[:, kc, t0:t0 + 128], wkv[:, kc, c0 + 128:c0 + 256], kc == 0, kc == 1, [wkv.b, ckn.b])
                            cp(v_, v_[:, tb * 4:tb * 4 + 4, :], pvv[:].rearrange("p (a d) -> p a d", d=128), [pvv.b])
                        for qc in range(NB):
                            qsl = slice(qc * 512, (qc + 1) * 512)
                            po_ = po[nq % 2]
                            pm_ = psm[nq % 2]
                            nkb = 4 * (qc + 1)
                            def s_stage(kb, idx):
                                ksl = slice(kb * 128, (kb + 1) * 128)
                                s_ = pS[idx % 2]
                                mm(s_, s_[:], kn_[:, ksl], qn_[:, qsl], True, False, [kn_.b, qn_.b])
                                mm(s_, s_[:], krr[0:64, ksl], qr_[0:64, qsl], False, True, [krr.b, qr_.b])
                                return s_
                            nxt = s_stage(0, npt)
                            for kb in range(nkb):
                                s_ = nxt
                                t_ = pT[npt % 3]
                                npt += 1
                                if kb + 1 < nkb:
                                    nxt = s_stage(kb + 1, npt)
                                act(t_, t_[:], s_[:], AF.Exp, [s_.b], scale=SCALE)
                                if kb >= 4 * qc:
                                    j = kb - 4 * qc
                                    tt(t_, t_[:], t_[:], maskA[:, j * 512:(j + 1) * 512], ALU.mult, [t_.b, maskA.b])
                                mm(po_, po_[:], v_[:, kb, :], t_[:], kb == 0, kb == nkb - 1, [v_.b, t_.b])
                                mm(pm_, pm_[:], ones[:], t_[:], kb == 0, kb == nkb - 1, [ones.b, t_.b])
                                nkbt += 1
                                if mgen is not None and nkbt % 40 == 0:
                                    next(mgen, None)
                            rs_ = rsm[nq % 2]
                            y_ = yh[nq % 2]
                            nq += 1
                            P.op("dve", lambda e, rs_=rs_, pm_=pm_: e.reciprocal(out=rs_[:], in_=pm_[:]), r=[pm_.b], w=[rs_.b])
                            tt(y_, y_[:], po_[:], rs_[:], ALU.mult, [po_.b, rs_.b])
                            store("sp", y_, yT[512 + h * 128:512 + (h + 1) * 128, qsl], y_[:])
                    if mgen is not None:
                        for _ in mgen:
                            pass
                    P.barrier()

        def phase_hg(l):
            HB = min(T, 1024)
            NHB = T // HB
            with ExitStack() as st:
                NC2 = T // 32
                nck = HB // 32
                qt_ = sb(st, "qt_", [128, T], BF16)
                kt_ = sb(st, "kt_", [128, T], BF16)
                qh_ = sb(st, "qh_", [128, T], F32)
                khT = sb(st, "khT", [32, NC2, 128], BF16)
                vt = sb(st, "vt", [32, NC2, 128], BF16)
                ebl = sb(st, "ebl", [128, NC2], F32)
                oT = sb(st, "oT", [128, T], F32)
                qtq = [Tile(qt_.t, "qt_") for _ in range(NHB)]
                ktq = [Tile(kt_.t, "kt_") for _ in range(NHB)]
                qhq = [Tile(qh_.t, "qh_") for _ in range(NHB)]
                khq = [Tile(khT.t, "khT") for _ in range(NHB)]
                ebq = [Tile(ebl.t, "ebl") for _ in range(NHB)]
                vtq = [Tile(vt.t, "vt") for _ in range(NHB)]
                oT2 = [oT, sb(st, "oTb", [128, T], F32)]
                nsq = sb(st, "nsq", [128, HB], BF16)
                nE = sb(st, "nE", [128, HB], F32)
                nG = sb(st, "nG", [128, HB], F32)
                S32 = [sb(st, "S32%d" % i, [128, 128], F32) for i in range(4)]
                Zt = sb(st, "hZ", [128, HB], F32)
                Qt = sb(st, "hQ", [128, HB], F32)
                SGt = sb(st, "hSG", [128, HB], F32)
                Bt = sb(st, "hB", [128, HB], F32)
                Et = sb(st, "hE", [128, HB], F32)
                khb = sb(st, "khb", [128, HB], BF16)
                aTt = [sb(st, "aTt%d" % i, [32, 32], BF16) for i in range(3)]
                yo_ = sb(st, "hyo", [128, HB], BF16)
                ptk = ps(st, "ptk", (128, 1024), BF16)
                pnn = ps(st, "hpnn")
                pa_ = [ps(st, "hpa%d" % i) for i in range(2)]
                pob = [ps(st, "hpo%d" % i) for i in range(2)]
                pst = [ps(st, "hpst%d" % i) for i in range(2)]

                def prep_gen(h, hb):
                    lb_ap = lbt[:, h, l:l + 1]
                    oml_ap = omlt[:, h, l:l + 1]
                    noml_ap = nomlt[:, h, l:l + 1]
                    hsl = slice(hb * HB, (hb + 1) * HB)
                    P.op("sp", lambda e: e.dma_start(out=vt[:, hb * nck:(hb + 1) * nck, :], in_=vtok[hb * HB:(hb + 1) * HB, h * 128:(h + 1) * 128].rearrange("(n s) d -> s n d", s=32)), w=[vtq[hb].b], dsem="Lvt%d" % (hb % 2))
                    load("sp", Zt, Zt[:], projT[2048 + h * 128:2048 + (h + 1) * 128, hsl])
                    load("sp", Qt, Qt[:], projT[1536 + h * 128:1536 + (h + 1) * 128, hsl])
                    act(SGt, SGt[:], Zt[:], AF.Sigmoid, [Zt.b])
                    yield
                    ts(Zt, Zt[:], SGt[:], oml_ap, lb_ap, ALU.mult, ALU.add, [SGt.b, omlt.b, lbt.b])
                    yield
                    ts(SGt, SGt[:], SGt[:], noml_ap, oml_ap, ALU.mult, ALU.add, [SGt.b, omlt.b, nomlt.b])
                    act(Zt, Zt[:], Zt[:], AF.Ln, [Zt.b])
                    yield
                    for tb in range(HB // 512):
                        bsl = slice(tb * 512, (tb + 1) * 512)
                        P.op("dve", lambda e, bsl=bsl: e.tensor_tensor_scan(out=Bt[:, bsl], data0=rst32[:], data1=Zt[:, bsl], initial=0.0, op0=ALU.mult, op1=ALU.add), r=[rst32.b, Zt.b], w=[Bt.b])
                        yield
                    act(Qt, Qt[:], Qt[:], AF.Silu, [Qt.b])
                    b3 = Bt[:].rearrange("p (n j) -> p n j", j=32)
                    c3 = Zt[:].rearrange("p (n j) -> p n j", j=32)
                    tt(Zt, c3, b3, b3[:, :, 15:16].broadcast_to([128, nck, 32]), ALU.subtract, [Bt.b], eng="pool")
                    act(Et, Et[:], Zt[:], AF.Exp, [Zt.b])
                    yield
                    tt(qtq[hb], qt_[:, hsl], Qt[:], Et[:], ALU.mult, [Qt.b, Et.b])
                    act(Et, Et[:], Zt[:], AF.Exp, [Zt.b], scale=-1.0)
                    yield
                    tt(ktq[hb], kt_[:, hsl], SGt[:], Et[:], ALU.mult, [SGt.b, Et.b])
                    act(Et, Et[:], Bt[:], AF.Exp, [Bt.b])
                    yield
                    tt(qhq[hb], qh_[:, hsl], Qt[:], Et[:], ALU.mult, [Qt.b, Et.b])
                    tt(Zt, c3, b3[:, :, 31:32].broadcast_to([128, nck, 32]), b3, ALU.subtract, [Bt.b], eng="pool")
                    act(Et, Et[:], Zt[:], AF.Exp, [Zt.b])
                    yield
                    tt(khb, khb[:], SGt[:], Et[:], ALU.mult, [SGt.b, Et.b])
                    act(ebq[hb], ebl[:, hb * nck:(hb + 1) * nck], Bt[:, 31:HB:32], AF.Exp, [Bt.b])
                    yield
                    for n16 in range(nck // 8):
                        for n8 in range(8):
                            n = n16 * 8 + n8
                            P.op("pe", lambda e, n8=n8, n=n: e.transpose(ptk[0:32, n8 * 128:(n8 + 1) * 128], khb[:, n * 32:(n + 1) * 32], ident[:]), r=[khb.b, ident.b], w=[ptk.b])
                        n0 = hb * nck + n16 * 8
                        cp(khq[hb], khT[:, n0:n0 + 8, :], ptk[0:32, :].rearrange("p (n d) -> p n d", d=128), [ptk.b])
                        yield

                def norm_gen(h):
                    oTh = oT2[h % 2]
                    for hb in range(NHB):
                        hsl = slice(hb * HB, (hb + 1) * HB)
                        load("sp", nG, nG[:], projT[2560 + h * 128:2560 + (h + 1) * 128, hsl])
                        act(nsq, nsq[:], oTh[:, hsl], AF.Square, [oTh.b])
                        yield
                        for tb in range(HB // 512):
                            bsl = slice(tb * 512, (tb + 1) * 512)
                            mm(pnn, pnn[:], ones[:], nsq[:, bsl], True, True, [ones.b, nsq.b])
                            act(nE, nE[:, bsl], pnn[:], AF.Sqrt, [pnn.b, cstf.b], bias=epsc, scale=1.0 / 128)
                            yield
                        P.op("dve", lambda e: e.reciprocal(out=nE[:], in_=nE[:]), r=[nE.b], w=[nE.b])
                        yield
                        tt(nE, nE[:], nE[:], oTh[:, hsl], ALU.mult, [nE.b, oTh.b])
                        act(nG, nG[:], nG[:], AF.Silu, [nG.b])
                        yield
                        stt(yo_, yo_[:], nE[:], pv[:, l, 6:7], nG[:], ALU.mult, ALU.mult, [nE.b, pv.b, nG.b])
                        store("sp", yo_, yT[1536 + h * 128:1536 + (h + 1) * 128, hsl], yo_[:])
                        yield

                chain = NHB > 1
                ngen = None
                for h in range(4):
                    oT = oT2[h % 2]
                    if h == 0 or not chain:
                        for _ in prep_gen(h, 0):
                            pass
                    mset(S32[0], S32[0][:], 0.0)
                    R = len(S32)

                    def o_stage(n):
                        q = n // nck
                        nsl = slice(n * 32, (n + 1) * 32)
                        pb = pob[(n // 16) % 2]
                        osl = slice((n % 16) * 32, (n % 16 + 1) * 32)
                        at = aTt[n % 3]
                        mm(pb, pb[:, osl], vt[:, n, :], at[:], True, False, [vtq[q].b, at.b])
                        mm(pb, pb[:, osl], S32[n % R][:], qh_[:, nsl], False, True, [S32[n % R].b, qhq[q].b])
                        if n % 16 == 15:
                            act(oT, oT[:, (n - 15) * 32:(n + 1) * 32], pb[:], AF.Copy, [pb.b])

                    for q in range(NHB):
                        if q + 1 < NHB:
                            pg = prep_gen(h, q + 1)
                        elif chain and h + 1 < 4:
                            pg = prep_gen(h + 1, 0)
                        else:
                            pg = None
                        for n in range(q * nck, (q + 1) * nck):
                            nsl = slice(n * 32, (n + 1) * 32)
                            s_ = pst[n % 2]
                            mm(s_, s_[:, 0:128], khT[:, n, :], vt[:, n, :], True, True, [khq[q].b, vtq[q].b])
                            stt(S32[(n + 1) % R], S32[(n + 1) % R][:], S32[n % R][:], ebl[:, n:n + 1], s_[:, 0:128], ALU.mult, ALU.add, [S32[n % R].b, ebq[q].b, s_.b])
                            a_ = pa_[n % 2]
                            at = aTt[n % 3]
                            mm(a_, a_[0:32, 0:32], kt_[:, nsl], qt_[:, nsl], True, True, [ktq[q].b, qtq[q].b])
                            tt(at, at[:], a_[0:32, 0:32], mask64[0:32, 0:32], ALU.mult, [a_.b, mask64.b])
                            if n >= 1:
                                o_stage(n - 1)
                            if pg is not None and n % 2 == 1:
                                next(pg, None)
                            if ngen is not None and n % 2 == 0:
                                next(ngen, None)
                        if pg is not None:
                            for _ in pg:
                                pass
                    o_stage(NC2 - 1)
                    if ngen is not None:
                        for _ in ngen:
                            pass
                    ngen = norm_gen(h)
                    if not chain or h == 3:
                        for _ in ngen:
                            pass
                        ngen = None
                P.barrier()

        def phase_c(l):
            with ExitStack() as st:
                wo = sb(st, "wo", [128, 16, 2048], BF16)
                G1 = sb(st, "G1", [128, D], F32)
                A2 = sb(st, "A2", [128, D], F32)
                B2 = sb(st, "B2", [128, D], F32)
                yTt = [sb(st, "yTt%d" % i, [128, 16, 128], BF16) for i in range(2)]
                xt = [sb(st, "cxt%d" % i, [128, D], F32) for i in range(2)]
                tmp = sb(st, "ctmp", [128, D], F32)
                tmp2 = sb(st, "ctmp2", [128, D], F32)
                hb = [sb(st, "chb%d" % i, [128, D], BF16) for i in range(2)]
                junk = sb(st, "cjunk", [128, D], BF16)
                h2s = [sb(st, "h2s%d" % i, [128, 16, 128], BF16) for i in range(2)]
                sst = [(sb(st, "css%d" % i, [128, 1], F32), sb(st, "crs%d" % i, [128, 1], F32)) for i in range(4)]
                pmx = ps(st, "pmx", (128, 2048), F32)
                ptr = [ps(st, "cptr%d" % i, (128, 2048), BF16) for i in range(2)]
                for cg in range(4):
                    P.op("pool", lambda e, cg=cg: e.dma_start(out=wo[:, :, cg * 512:(cg + 1) * 512], in_=w_out[l, :, cg * 512:(cg + 1) * 512].rearrange("(k p) n -> p k n", p=128)), w=[wo.b], dsem="Lwo")
                load("sp", G1, G1[:], modt[l, MOD_G1])
                load("sp", A2, A2[:], modt[l, MOD_A2])
                load("sp", B2, B2[:], modt[l, MOD_B2])
                def OP(gt):
                    i = gt % 2
                    rows = slice(gt * 128, (gt + 1) * 128)
                    load("sp", yTt[i], yTt[i][:], yT[:, rows].rearrange("(k p) t -> p k t", p=128))
                    load("sp", xt[i], xt[i][:], xsrc_of[0][rows, :])
                    for ng_ in range(4):
                        for kc in range(16):
                            mm(pmx, pmx[:, ng_ * 512:(ng_ + 1) * 512], yTt[i][:, kc, :], wo[:, kc, ng_ * 512:(ng_ + 1) * 512], kc == 0, kc == 15, [yTt[i].b, wo.b])

                def CH1(gt):
                    i = gt % 2
                    rs = rms_rstd(sst[2 * i], pmx, pmx[:], junk, [pmx.b])
                    stt(tmp, tmp[:], pmx[:], rs[:], G1[:], ALU.mult, ALU.mult, [pmx.b, rs.b, G1.b])

                def CH2(gt):
                    i = gt % 2
                    rows = slice(gt * 128, (gt + 1) * 128)
                    tt(xt[i], xt[i][:], xt[i][:], tmp[:], ALU.add, [xt[i].b, tmp.b])
                    store("sp", xt[i], out[rows, :], xt[i][:])
                    rs2 = rms_rstd(sst[2 * i + 1], xt[i], xt[i][:], junk, [xt[i].b])
                    stt(tmp2, tmp2[:], xt[i][:], rs2[:], A2[:], ALU.mult, ALU.mult, [xt[i].b, rs2.b, A2.b])
                    tt(hb[i], hb[i][:], tmp2[:], B2[:], ALU.add, [tmp2.b, B2.b])
                    p_ = ptr[i]
                    for kc in range(16):
                        P.op("pe", lambda e, kc=kc, p_=p_, i=i: e.transpose(p_[:, kc * 128:(kc + 1) * 128], hb[i][:, kc * 128:(kc + 1) * 128], ident[:]), r=[hb[i].b, ident.b], w=[p_.b])
                    act(h2s[i], h2s[i][:], p_[:].rearrange("p (k t) -> p k t", k=16), AF.Copy, [p_.b])
                    store("sp", h2s[i], h2T[:, rows].rearrange("(k p) t -> p k t", p=128), h2s[i][:])

                OP(0)
                for gt in range(NT):
                    CH1(gt)
                    if gt + 1 < NT:
                        OP(gt + 1)
                    CH2(gt)
                P.barrier()
            if stop == "C1":
                return
            SUB = min(1024, SBT)
            with ExitStack() as st:
                h2 = sb(st, "h2", [128, 16, SBT], BF16)
                wg_ = [sb(st, "wg_%d" % i, [128, 16, 256], BF16) for i in range(2)]
                wv_ = [sb(st, "wv_%d" % i, [128, 16, 256], BF16) for i in range(2)]
                rawg = [sb(st, "rawg%d" % i, [128, 2 + SBT], F32) for i in range(2)]
                rawv = [sb(st, "rawv%d" % i, [128, 2 + SBT], F32) for i in range(2)]
                cg_ = [sb(st, "cg_%d" % i, [128, SUB], F32) for i in range(2)]
                cv_ = [sb(st, "cv_%d" % i, [128, SUB], F32) for i in range(2)]
                ao = [sb(st, "ao%d" % i, [128, SUB], BF16) for i in range(2)]
                halo = sb(st, "halo", [128, 86, 2], F32)
                cvp = sb(st, "cvp", [128, 86, 4], F32)
                pg_ = [ps(st, "fpg%d" % i, (128, SUB), F32) for i in range(2)]
                pv_ = [ps(st, "fpv%d" % i, (128, SUB), F32) for i in range(2)]
                load("sp", cvp, cvp[:], convp[:, l])
                it = 0
                for sbi in range(NSB):
                    load("sp", h2, h2[:], h2T[:, sbi * SBT:(sbi + 1) * SBT].rearrange("(k p) t -> p k t", p=128))
                    for jg in range(22):
                        ncol = 256 if jg < 21 else 128
                        wgt, wvt = wg_[jg % 2], wv_[jg % 2]
                        load("pool", wgt, wgt[:, :, 0:ncol], w_up[l, :, jg * 256:jg * 256 + ncol].rearrange("(k p) n -> p k n", p=128))
                        load("pool", wvt, wvt[:, :, 0:ncol], w_up[l, :, DFF + jg * 256:DFF + jg * 256 + ncol].rearrange("(k p) n -> p k n", p=128))
                        for jj in range(ncol // 128):
                            j = jg * 2 + jj
                            rg, rv = rawg[j % 2], rawv[j % 2]
                            if sbi == 0:
                                mset(rg, rg[:, 0:2], 0.0)
                                mset(rv, rv[:, 0:2], 0.0)
                            else:
                                cp(rg, rg[:, 0:2], halo[:, j, :], [halo.b])
                                cp(rv, rv[:, 0:2], halo[:, 43 + j, :], [halo.b])
                            for sub in range(SBT // SUB):
                                pg, pvl = pg_[it % 2], pv_[it % 2]
                                cg, cv, a_ = cg_[it % 2], cv_[it % 2], ao[it % 2]
                                it += 1
                                off = sub * SUB
                                for t5 in range(SUB // 512):
                                    for kc in range(16):
                                        mm(pg, pg[:, t5 * 512:(t5 + 1) * 512], wgt[:, kc, jj * 128:(jj + 1) * 128], h2[:, kc, off + t5 * 512:off + (t5 + 1) * 512], kc == 0, kc == 15, [wgt.b, h2.b])
                                for t5 in range(SUB // 512):
                                    for kc in range(16):
                                        mm(pvl, pvl[:, t5 * 512:(t5 + 1) * 512], wvt[:, kc, jj * 128:(jj + 1) * 128], h2[:, kc, off + t5 * 512:off + (t5 + 1) * 512], kc == 0, kc == 15, [wvt.b, h2.b])
                                act(rg, rg[:, 2 + off:2 + off + SUB], pg[:], AF.Copy, [pg.b])
                                act(rv, rv[:, 2 + off:2 + off + SUB], pvl[:], AF.Copy, [pvl.b])
                                act(cg, cg[:], pg[:], AF.Identity, [pg.b, cvp.b], bias=cvp[:, j, 3:4], scale=cvp[:, j, 2:3])
                                act(cv, cv[:], pvl[:], AF.Identity, [pvl.b, cvp.b], bias=cvp[:, 43 + j, 3:4], scale=cvp[:, 43 + j, 2:3])
                                stt(cg, cg[:], rg[:, 1 + off:1 + off + SUB], cvp[:, j, 1:2], cg[:], ALU.mult, ALU.add, [rg.b, cvp.b, cg.b])
                                stt(cg, cg[:], rg[:, off:off + SUB], cvp[:, j, 0:1], cg[:], ALU.mult, ALU.add, [rg.b, cvp.b, cg.b])
                                stt(cv, cv[:], rv[:, 1 + off:1 + off + SUB], cvp[:, 43 + j, 1:2], cv[:], ALU.mult, ALU.add, [rv.b, cvp.b, cv.b])
                                stt(cv, cv[:], rv[:, off:off + SUB], cvp[:, 43 + j, 0:1], cv[:], ALU.mult, ALU.add, [rv.b, cvp.b, cv.b])
                                act(cg, cg[:], cg[:], AF.Gelu_apprx_tanh, [cg.b])
                                tt(a_, a_[:], cg[:], cv[:], ALU.mult, [cg.b, cv.b])
                                c0 = sbi * SBT + off
                                store("sp", a_, aT[c0 // 128:(c0 + SUB) // 128, :, j, :].rearrange("n p t -> p n t"), a_[:].rearrange("p (n t) -> p n t", t=128))
                            if sbi < NSB - 1:
                                cp(halo, halo[:, j, :], rg[:, SBT:SBT + 2], [rg.b])
                                cp(halo, halo[:, 43 + j, :], rv[:, SBT:SBT + 2], [rv.b])
                P.barrier()
            if stop == "C2":
                return
            st0 = ExitStack()
            ssq = sb(st0, "ssq", [128, NT, 4], F32)
            with ExitStack() as st:
                wd = [sb(st, "wd%d" % i, [128, NCH, 512], BF16) for i in range(2)]
                at_ = [sb(st, "at_%d" % i, [128, NCH, 128], BF16) for i in range(3)]
                ys = [sb(st, "ys%d" % i, [128, 512], F32) for i in range(2)]
                junk = sb(st, "djunk", [128, 512], BF16)
                pd = [ps(st, "pd%d" % i) for i in range(2)]
                mg = None

                def load_wd(ng_):
                    for k0 in range(0, NCH, 22):
                        k1 = min(NCH, k0 + 22)
                        P.op("pool", lambda e, k0=k0, k1=k1, ng_=ng_: e.dma_start(out=wd[ng_ % 2][:, k0:k1, :], in_=w_down[l, k0 * 128:k1 * 128, ng_ * 512:(ng_ + 1) * 512].rearrange("(k p) n -> p k n", p=128)), w=[wd[ng_ % 2].b], dsem="Lwd%d" % (ng_ % 2))

                load_wd(0)
                it = 0
                NIT = 4 * NT
                for j in range(min(2, NIT)):
                    load("sp", at_[j % 3], at_[j % 3][:], aT[j % NT])
                for ng_ in range(4):
                    if ng_ + 1 < 4:
                        load_wd(ng_ + 1)
                    w_ = wd[ng_ % 2]
                    for gt in range(NT):
                        i = it % 2
                        a3 = at_[it % 3]
                        if it + 2 < NIT:
                            load("sp", at_[(it + 2) % 3], at_[(it + 2) % 3][:], aT[(it + 2) % NT])
                        it += 1
                        rows = slice(gt * 128, (gt + 1) * 128)
                        p_ = pd[i]
                        for kc in range(NCH):
                            mm(p_, p_[:], a3[:, kc, :], w_[:, kc, :], kc == 0, kc == NCH - 1, [a3.b, w_.b])
                        act(ys[i], ys[i][:], p_[:], AF.Copy, [p_.b])
                        act(junk, junk[:], p_[:], AF.Square, [p_.b], accum=ssq[:, gt, ng_:ng_ + 1], extra_w=[ssq.b])
                        store("sp", ys[i], ydn[rows, ng_ * 512:(ng_ + 1) * 512], ys[i][:])
                        if mg is not None and ng_ >= 1:
                            next(mg, None)
                if mg is not None:
                    for _ in mg:
                        pass
                P.barrier()
            if stop == "C3a":
                st0.close()
                return
            with ExitStack() as st:
                G2 = sb(st, "G2", [128, D], F32)
                yt = [sb(st, "dyt%d" % i, [128, D], F32) for i in range(3)]
                xt = [sb(st, "dxt%d" % i, [128, D], F32) for i in range(3)]
                rs = [sb(st, "drs%d" % i, [128, 1], F32) for i in range(3)]
                load("sp", G2, G2[:], modt[l, MOD_G2])

                def ld(gt):
                    i = gt % 3
                    rows = slice(gt * 128, (gt + 1) * 128)
                    load("sp", yt[i], yt[i][:], ydn[rows, :])
                    load("sp", xt[i], xt[i][:], out[rows, :])

                ld(0)
                if NT > 1:
                    ld(1)
                for gt in range(NT):
                    i = gt % 3
                    rows = slice(gt * 128, (gt + 1) * 128)
                    if gt + 2 < NT:
                        ld(gt + 2)
                    P.op("dve", lambda e, i=i, gt=gt: e.reduce_sum(out=rs[i][:], in_=ssq[:, gt, :], axis=AX.X), r=[ssq.b], w=[rs[i].b])
                    act(rs[i], rs[i][:], rs[i][:], AF.Sqrt, [rs[i].b, cstf.b], bias=epsc, scale=1.0 / D)
                    P.op("dve", lambda e, i=i: e.reciprocal(out=rs[i][:], in_=rs[i][:]), r=[rs[i].b], w=[rs[i].b])
                    stt(yt[i], yt[i][:], yt[i][:], rs[i][:], G2[:], ALU.mult, ALU.mult, [yt[i].b, rs[i].b, G2.b])
                    tt(xt[i], xt[i][:], xt[i][:], yt[i][:], ALU.add, [xt[i].b, yt[i].b])
                    store("sp", xt[i], out[rows, :], xt[i][:])
                P.barrier()
            st0.close()

        xsrc_of = [x_in]
        for l in range(L):
            xsrc = x_in if l == 0 else out
            xsrc_of[0] = xsrc
            with ExitStack() as st:
                A1 = sb(st, "A1", [128, D], F32)
                B1 = sb(st, "B1", [128, D], F32)
                hT = sb(st, "hT", [128, 16, SBT], BF16)
                xt = [sb(st, "xt%d" % i, [128, D], F32) for i in range(2)]
                junk = sb(st, "junk", [128, D], BF16)
                hb = [sb(st, "hb%d" % i, [128, D], BF16) for i in range(2)]
                sst = [(sb(st, "ss%d" % i, [128, 1], F32), sb(st, "rs%d" % i, [128, 1], F32)) for i in range(2)]
                wb = [sb(st, "wb%d" % i, [128, 16, 512], BF16) for i in range(2)]
                stg = [sb(st, "stg%d" % i, [128, 512], F32) for i in range(3)]
                vst = [sb(st, "vst%d" % i, [128, 512], BF16) for i in range(2)]
                ptr = [ps(st, "ptr%d" % i, (128, 2048), BF16) for i in range(2)]
                pa = [ps(st, "pa%d" % i) for i in range(4)]
                load("sp", A1, A1[:], modt[l, MOD_A1])
                load("sp", B1, B1[:], modt[l, MOD_B1])
                nst = 0
                npa = 0
                nv = 0
                for sbi in range(NSB):
                    for ti in range(SBT // 128):
                        gt = sbi * (SBT // 128) + ti
                        x_ = xt[gt % 2]
                        load("sp", x_, x_[:], xsrc[gt * 128:(gt + 1) * 128, :])
                        rs = rms_rstd(sst[gt % 2], x_, x_[:], junk, [x_.b])
                        h_ = hb[gt % 2]
                        stt(x_, x_[:], x_[:], rs[:], A1[:], ALU.mult, ALU.mult, [x_.b, rs.b, A1.b])
                        tt(h_, h_[:], x_[:], B1[:], ALU.add, [x_.b, B1.b])
                        p_ = ptr[gt % 2]
                        for kc in range(16):
                            P.op("pe", lambda e, kc=kc: e.transpose(p_[:, kc * 128:(kc + 1) * 128], h_[:, kc * 128:(kc + 1) * 128], ident[:]), r=[h_.b, ident.b], w=[p_.b])
                        act(hT, hT[:, :, ti * 128:(ti + 1) * 128], p_[:].rearrange("p (k t) -> p k t", k=16), AF.Copy, [p_.b])
                    for mg in range(7):
                        wt = wb[mg % 2]
                        load("pool", wt, wt[:], w_in[l, :, mg * 512:(mg + 1) * 512].rearrange("(k p) n -> p k n", p=128))
                        if mg < 6:
                            nm = 3 if mg == 2 else 4
                            for tb in range(SBT // 512):
                                for m in range(nm):
                                    p_ = pa[npa % 4]
                                    npa += 1
                                    for kc in range(16):
                                        mm(p_, p_[:], wt[:, kc, m * 128:(m + 1) * 128], hT[:, kc, tb * 512:(tb + 1) * 512], kc == 0, kc == 15, [wt.b, hT.b])
                                    s_ = stg[nst % 3]
                                    nst += 1
                                    if nst % 2:
                                        act(s_, s_[:], p_[:], AF.Copy, [p_.b])
                                    else:
                                        cp(s_, s_[:], p_[:], [p_.b])
                                    row = mg * 512 + m * 128
                                    c0 = sbi * SBT + tb * 512
                                    store("sp", s_, projT[row:row + 128, c0:c0 + 512], s_[:])
                        else:
                            for ti in range(SBT // 128):
                                p_ = pa[npa % 4]
                                npa += 1
                                for kc in range(16):
                                    mm(p_, p_[:], hT[:, kc, ti * 128:(ti + 1) * 128], wt[:, kc, :], kc == 0, kc == 15, [wt.b, hT.b])
                                v_ = vst[nv % 2]
                                nv += 1
                                act(v_, v_[:], p_[:], AF.Copy, [p_.b])
                                t0 = sbi * SBT + ti * 128
                                store("sp", v_, vtok[t0:t0 + 128, :], v_[:])
                P.barrier()
            if stop == "A":
                continue
            phase_s5(l)
            if stop == "S5":
                continue
            phase_mla(l)
            if stop == "MLA":
                continue
            phase_hg(l)
            if stop == "B":
                continue
            phase_c(l)
        P.barrier(["sp"])
    return nc


def _consts():
    c = np.zeros((128, 1024), np.float32)
    c[:, 0:128] = np.eye(128, dtype=np.float32)
    perm = np.zeros((128, 128), np.float32)
    for i in range(64):
        perm[i, 64 + i] = 1.0
        perm[64 + i, i] = 1.0
    c[:, 128:256] = perm
    s = np.arange(64)
    c[0:64, 256:320] = (s[:, None] <= s[None, :]).astype(np.float32)
    c[0:64, 320] = 1.0
    c[64:128, 320] = -1.0
    invf = 1.0 / (10000.0 ** (np.arange(0, 64, 2, dtype=np.float32) / 64.0))
    c[0:32, 321] = invf
    c[32:64, 321] = invf
    c[0:32, 322] = -1.0
    c[32:64, 322] = 1.0
    c[:, 323] = EPS
    c[:, 384:448] = np.arange(64, dtype=np.float32)[None, :]
    k = np.arange(128)[:, None]
    q = np.arange(512)[None, :]
    mA = np.concatenate([(q >= j * 128 + k).astype(np.float32) for j in range(4)], axis=1)
    return c, mA


def prep_shared(inp, L):
    f = lambda a: np.ascontiguousarray(np.asarray(a, dtype=np.float32))
    w_in = f(inp["w_in"])
    wa = np.zeros((L, D, WIN), np.float32)
    wa[:, :, 0:1280] = w_in[:, :, 0:1280]
    wa[:, :, 1280:1344] = w_in[:, :, 1280:1344]
    wa[:, :, 1344:1376] = w_in[:, :, 1312:1344]
    wa[:, :, 1376:1408] = w_in[:, :, 1280:1312]
    wa[:, :, 1536:2048] = w_in[:, :, 1344:1856]
    wa[:, :, 2048:2560] = w_in[:, :, 1856:2368]
    wa[:, :, 2560:3072] = w_in[:, :, 2880:3392]
    wa[:, :, 3072:3584] = w_in[:, :, 2368:2880]
    wuq = f(inp["mla_w_uq"]).reshape(L, 512, 8, 192)
    wq = np.zeros((L, 512, 8, 256), np.float32)
    wq[..., 0:192] = wuq
    wq[..., 192:224] = wuq[..., 160:192]
    wq[..., 224:256] = wuq[..., 128:160]
    norms = np.stack([f(inp["mix_pre_norm"]), f(inp["mix_post_norm"]), f(inp["ffn_pre_norm"]), f(inp["ffn_post_norm"])], axis=1)
    pvec = np.zeros((128, L, 16), np.float32)
    pvec[:, :, 0:4] = f(inp["mla_q_norm"]).reshape(L, 4, 128).transpose(2, 0, 1)
    pvec[:, :, 4:6] = f(inp["mla_kv_norm"]).reshape(L, 2, 128).transpose(2, 0, 1)
    pvec[:, :, 6] = f(inp["hg_out_norm"]).T
    pvec[:, :, 7:11] = f(inp["s5_d"]).reshape(L, 4, 128).transpose(2, 0, 1)
    convp = np.zeros((128, L, 86, 4), np.float32)
    convp[:, :, :, 0:3] = f(inp["ffn_conv_w"]).reshape(L, 3, 86, 128).transpose(3, 0, 2, 1)
    convp[:, :, :, 3] = f(inp["ffn_conv_b"]).reshape(L, 86, 128).transpose(2, 0, 1)
    lre, lim, ldt = f(inp["s5_lambda_re"]), f(inp["s5_lambda_im"]), f(inp["s5_log_dt"])
    ldt_e = np.broadcast_to(ldt[:, :, None], (L, 32, 64))
    s5lam = np.stack([lre.reshape(L, 2048), lim.reshape(L, 2048), np.ascontiguousarray(ldt_e).reshape(L, 2048)], axis=1)
    s5lamp = np.zeros((128, L, 3, 32), np.float32)
    for i, a in enumerate((lre, lim, ldt_e)):
        t = np.asarray(a).transpose(2, 0, 1)
        s5lamp[0:64, :, i, :] = t
        s5lamp[64:128, :, i, :] = t
    s5b = np.zeros((L, 2, 128, 32, 64), np.float32)
    s5c = np.zeros((L, 2, 128, 32, 128), np.float32)
    bre, bim = f(inp["s5_b_re"]), f(inp["s5_b_im"])
    cre, cim = f(inp["s5_c_re"]), f(inp["s5_c_im"])
    for g in range(32):
        g8 = g % 8
        s5b[:, 0, g8 * 16:(g8 + 1) * 16, g, :] = bre[:, g].transpose(0, 2, 1)
        s5b[:, 1, g8 * 16:(g8 + 1) * 16, g, :] = bim[:, g].transpose(0, 2, 1)
        s5c[:, 0, 0:64, g, g8 * 16:(g8 + 1) * 16] = cre[:, g].transpose(0, 2, 1)
        s5c[:, 0, 64:128, g, g8 * 16:(g8 + 1) * 16] = cim[:, g].transpose(0, 2, 1)
        s5c[:, 1, 0:64, g, g8 * 16:(g8 + 1) * 16] = cim[:, g].transpose(0, 2, 1)
        s5c[:, 1, 64:128, g, g8 * 16:(g8 + 1) * 16] = cre[:, g].transpose(0, 2, 1)
    lbl = f(inp["hg_lb_logits"]).reshape(L, 4, 128).transpose(2, 1, 0)
    cst, mA = _consts()
    return {
        "w_in": wa, "w_out": f(inp["w_out"]), "w_up": f(inp["ffn_w_up"]), "w_down": f(inp["ffn_w_down"]),
        "w_ada": f(inp["w_ada"]), "b_ada": f(inp["b_ada"]), "w_uq": np.ascontiguousarray(wq.reshape(L, 512, 2048)),
        "w_ukv": f(inp["mla_w_ukv"]), "w_glu": f(inp["s5_w_glu"]), "norms": np.ascontiguousarray(norms),
        "pvec": pvec, "convp": convp, "s5lam": np.ascontiguousarray(s5lam), "s5lamp": s5lamp,
        "s5b": np.ascontiguousarray(s5b.reshape(L, 2, 128, 2048)), "s5c": np.ascontiguousarray(s5c.reshape(L, 2, 128, 4096)),
        "lbl": np.ascontiguousarray(lbl), "cst": cst, "maskA": mA,
    }


def prep_core(inp, b):
    x = np.ascontiguousarray(np.asarray(inp["x"][b], dtype=np.float32))
    pos = np.ascontiguousarray(np.asarray(inp["positions"][b], dtype=np.int32)).reshape(1, -1)
    cT = np.ascontiguousarray(np.asarray(inp["c"][b], dtype=np.float32).reshape(16, 128).T)
    return {"x": x, "pos": pos, "cT": cT}


def kernel(**inputs):
    B, T, _ = inputs["x"].shape
    L = inputs["w_in"].shape[0]
    nc = build(T, L)
    shared = prep_shared(inputs, L)
    in_maps = []
    for b in range(B):
        m = dict(shared)
        m.update(prep_core(inputs, b))
        in_maps.append(m)
    res = run_bass_kernel_spmd(nc, in_maps, core_ids=list(range(B)))
    return np.stack([np.asarray(r["out"], dtype=np.float32) for r in res.results], axis=0)
```

```python
import math
from contextlib import ExitStack
import numpy as np
import concourse.bass as bass
import concourse.mybir as mybir
from concourse.bass_utils import run_bass_kernel_spmd

F32 = mybir.dt.float32
BF16 = mybir.dt.bfloat16
I32 = mybir.dt.int32
AF = mybir.ActivationFunctionType
ALU = mybir.AluOpType
AX = mybir.AxisListType

D = 2048
DFF = 5504
NCH = 43
WIN = 3584
EPS = 1e-6
TWO_PI = 2.0 * math.pi
C1_2PI = 6.28125
C2_2PI = TWO_PI - 6.28125

COMPUTE = ("pe", "act", "dve", "pool")


class Buf:
    __slots__ = ("w", "r")

    def __init__(self):
        self.w = {}
        self.r = {}


class Tile:
    def __init__(self, t, name):
        self.t = t
        self.b = Buf()
        self.name = name

    def __getitem__(self, k):
        return self.t[k]


class Prog:
    def __init__(self, nc, es):
        self.nc = nc
        self.es = es
        self.eh = {"pe": nc.tensor, "act": nc.scalar, "dve": nc.vector, "pool": nc.gpsimd, "sp": nc.sync}
        self.clock = {e: {} for e in self.eh}
        self.val = {}
        self.sems = {}
        self.n_ins = 0

    def sem(self, key):
        s = self.sems.get(key)
        if s is None:
            s = self.es.enter_context(self.nc.semaphore("s_" + key.replace(":", "_")))
            self.sems[key] = s
        return s

    def op(self, eng, fn, r=(), w=(), dsem=None):
        deps = {}
        war = {}
        for b in r:
            for k, v in b.w.items():
                if deps.get(k, 0) < v:
                    deps[k] = v
        for b in w:
            for k, v in b.w.items():
                if deps.get(k, 0) < v:
                    deps[k] = v
            for k, v in b.r.items():
                if war.get(k, 0) < v:
                    war[k] = v
        if dsem is None and eng == "pe":
            war.pop("pe", None)
            deps.pop("pe", None)
        for k, v in war.items():
            if deps.get(k, 0) < v:
                deps[k] = v
        clk = self.clock[eng]
        e = self.eh[eng]
        for k, v in deps.items():
            if clk.get(k, 0) < v:
                e.wait_ge(self.sem(k), v)
                clk[k] = v
        ins = fn(e)
        self.n_ins += 1
        if dsem is None:
            key = eng
            val = self.val.get(key, 0) + 1
            ins.then_inc(self.sem(key), 1)
        else:
            key = "d:" + dsem
            val = self.val.get(key, 0) + 16
            ins.then_inc(self.sem(key), 16)
        self.val[key] = val
        for b in w:
            b.w = {key: val}
            b.r = {}
        for b in r:
            if b.r.get(key, 0) < val:
                b.r[key] = val
        return (key, val)

    def barrier(self, engines=None):
        for eng in (engines or self.eh):
            clk = self.clock[eng]
            e = self.eh[eng]
            for k, v in self.val.items():
                if clk.get(k, 0) < v:
                    e.wait_ge(self.sem(k), v)
                    clk[k] = v


def build(T, L, dbg=False, stop=None):
    NB = T // 512
    NT = T // 128
    SBT = min(T, 2048)
    NSB = T // SBT
    NCK = T // 64
    nc = bass.Bass("TRN2", target_bir_lowering=False)

    def din(name, shape, dt=F32):
        return nc.dram_tensor(name, list(shape), dt, kind="ExternalInput").ap()

    def dscr(name, shape, dt=F32):
        return nc.dram_tensor(name, list(shape), dt, kind=("ExternalOutput" if dbg else "Internal")).ap()

    x_in = din("x", [T, D])
    pos_in = din("pos", [1, T], I32)
    cT_in = din("cT", [128, 16])
    w_in = din("w_in", [L, D, WIN])
    w_out = din("w_out", [L, D, D])
    w_up = din("w_up", [L, D, 2 * DFF])
    w_down = din("w_down", [L, DFF, D])
    w_ada = din("w_ada", [L, D, 6 * D])
    b_ada = din("b_ada", [L, 6 * D])
    w_uq = din("w_uq", [L, 512, 2048])
    w_ukv = din("w_ukv", [L, 256, 2048])
    w_glu = din("w_glu", [L, 512, 512])
    norms = din("norms", [L, 4, D])
    pvec = din("pvec", [128, L, 16])
    convp = din("convp", [128, L, 86, 4])
    s5lam = din("s5lam", [L, 3, 2048])
    s5lamp = din("s5lamp", [128, L, 3, 32])
    s5b = din("s5b", [L, 2, 128, 2048])
    s5c = din("s5c", [L, 2, 128, 4096])
    lbl = din("lbl", [128, 4, L])
    cst = din("cst", [128, 1024])
    maskA_in = din("maskA", [128, 2048])
    out = nc.dram_tensor("out", [T, D], F32, kind="ExternalOutput").ap()

    modt = dscr("modt", [L, 6, 128, D])
    projT = dscr("projT", [3072, T])
    vtok = dscr("vtok", [T, 512], BF16)
    yT = dscr("yT", [D, T], BF16)
    h2T = dscr("h2T", [D, T], BF16)
    aT = dscr("aT", [T // 128, 128, NCH, 128], BF16)
    ydn = dscr("ydn", [T, D])
    ropeT = dscr("ropeT", [2, 64, T])

    with ExitStack() as es:
        P = Prog(nc, es)

        uid = [0]

        def sb(stack, name, shape, dt):
            uid[0] += 1
            return Tile(stack.enter_context(nc.sbuf_tensor("%s_u%d" % (name, uid[0]), list(shape), dt)), name)

        def ps(stack, name, shape=(128, 512), dt=F32):
            uid[0] += 1
            return Tile(stack.enter_context(nc.psum_tensor("%s_u%d" % (name, uid[0]), list(shape), dt)), name)

        def load(eng, dst, dst_ap, src_ap):
            return P.op(eng, lambda e: e.dma_start(out=dst_ap, in_=src_ap), w=[dst.b], dsem="L" + dst.name)

        def store(eng, src, dst_ap, src_ap):
            return P.op(eng, lambda e: e.dma_start(out=dst_ap, in_=src_ap), r=[src.b], dsem="S" + src.name)

        def mm(o, o_ap, lhsT_ap, rhs_ap, start, stop, r):
            return P.op("pe", lambda e: e.matmul(o_ap, lhsT_ap, rhs_ap, start=start, stop=stop), r=r, w=[o.b])

        def act(o, o_ap, i_ap, func, r, bias=None, scale=None, accum=None, extra_w=()):
            kw = {}
            if bias is not None:
                kw["bias"] = bias
            if scale is not None:
                kw["scale"] = scale
            if accum is not None:
                kw["accum_out"] = accum
            return P.op("act", lambda e: e.activation(out=o_ap, in_=i_ap, func=func, **kw), r=r, w=[o.b] + list(extra_w))

        def tt(o, o_ap, a_ap, b_ap, op, r, eng="dve"):
            return P.op(eng, lambda e: e.tensor_tensor(out=o_ap, in0=a_ap, in1=b_ap, op=op), r=r, w=[o.b])

        def ts(o, o_ap, a_ap, s1, s2, op0, op1, r, eng="dve"):
            if op1 is None:
                return P.op(eng, lambda e: e.tensor_scalar(out=o_ap, in0=a_ap, scalar1=s1, scalar2=None, op0=op0), r=r, w=[o.b])
            return P.op(eng, lambda e: e.tensor_scalar(out=o_ap, in0=a_ap, scalar1=s1, scalar2=s2, op0=op0, op1=op1), r=r, w=[o.b])

        def stt(o, o_ap, a_ap, s, b_ap, op0, op1, r):
            return P.op("dve", lambda e: e.scalar_tensor_tensor(out=o_ap, in0=a_ap, scalar=s, in1=b_ap, op0=op0, op1=op1), r=r, w=[o.b])

        def cp(o, o_ap, i_ap, r, eng="dve"):
            return P.op(eng, lambda e: e.tensor_copy(out=o_ap, in_=i_ap), r=r, w=[o.b])

        def mset(o, o_ap, v, eng="dve"):
            return P.op(eng, lambda e: e.memset(o_ap, v), w=[o.b])

        cstf = sb(es, "cstf", [128, 1024], F32)
        ident = sb(es, "ident", [128, 128], BF16)
        ones = sb(es, "ones", [128, 128], BF16)
        mask64 = sb(es, "mask64", [128, 64], BF16)
        rst = sb(es, "rst", [128, 512], F32)
        lbt = sb(es, "lbt", [128, 4, L], F32)
        omlt = sb(es, "omlt", [128, 4, L], F32)
        nomlt = sb(es, "nomlt", [128, 4, L], F32)
        pv = sb(es, "pv", [128, L, 16], F32)
        ssqP = sb(es, "ssqP", [128, NT, 4], F32)
        load("sp", cstf, cstf[:], cst)
        load("sp", pv, pv[:], pvec)
        cp(ident, ident[:], cstf[:, 0:128], [cstf.b])
        cp(mask64, mask64[:], cstf[:, 256:320], [cstf.b])
        mset(ones, ones[:], 1.0)
        mset(rst, rst[:], 1.0)
        mset(rst, rst[:, 0:512:64], 0.0)
        rst32 = sb(es, "rst32", [128, 512], F32)
        mset(rst32, rst32[:], 1.0)
        mset(rst32, rst32[:, 0:512:32], 0.0)
        permsw = cstf[:, 128:256]
        sgnB = cstf[:, 320:321]
        invf = cstf[0:64, 321:322]
        sgnr = cstf[0:64, 322:323]
        epsc = cstf[:, 323:324]

        def sincos(stack, nm, ang, shape, o_sin, o_sin_ap, o_cos, o_cos_ap, npart=128):
            k_i = sb(stack, nm + "_ki", shape, I32)
            k_f = sb(stack, nm + "_kf", shape, F32)
            r_t = sb(stack, nm + "_r", shape, F32)
            sl = tuple([slice(0, npart)] + [slice(None)] * (len(shape) - 1))
            for which in (0, 1):
                if which == 1:
                    ts(r_t, r_t[sl], ang[sl], math.pi / 2, None, ALU.add, None, [ang.b])
                    src = r_t
                else:
                    src = ang
                ts(k_f, k_f[sl], src[sl], 1.0 / TWO_PI, None, ALU.mult, None, [src.b])
                cp(k_i, k_i[sl], k_f[sl], [k_f.b])
                cp(k_f, k_f[sl], k_i[sl], [k_i.b])
                stt(r_t, r_t[sl], k_f[sl], -C1_2PI, src[sl], ALU.mult, ALU.add, [k_f.b, src.b])
                stt(r_t, r_t[sl], k_f[sl], -C2_2PI, r_t[sl], ALU.mult, ALU.add, [k_f.b, r_t.b])
                ts(r_t, r_t[sl], r_t[sl], -3.1415925, 3.1415925, ALU.max, ALU.min, [r_t.b])
                if which == 0:
                    act(o_sin, o_sin_ap, r_t[sl], AF.Sin, [r_t.b])
                else:
                    act(o_cos, o_cos_ap, r_t[sl], AF.Sin, [r_t.b])

        with ExitStack() as st:
            lg = sb(st, "lg", [128, 4, L], F32)
            ssum = sb(st, "ssum", [128, 4], F32)
            load("sp", lg, lg[:], lbl)
            act(lg, lg[:], lg[:], AF.Exp, [lg.b])
            P.op("dve", lambda e: e.reduce_sum(out=ssum[:], in_=lg[:], axis=AX.X), r=[lg.b], w=[ssum.b])
            P.op("dve", lambda e: e.reciprocal(out=ssum[:], in_=ssum[:]), r=[ssum.b], w=[ssum.b])
            tt(lg, lg[:], lg[:], ssum[:].unsqueeze(2).broadcast_to([128, 4, L]), ALU.mult, [lg.b, ssum.b])
            mset(lbt, lbt[:, :, 0:1], 0.0)
            for l in range(1, L):
                tt(lbt, lbt[:, :, l:l + 1], lbt[:, :, l - 1:l], lg[:, :, l:l + 1], ALU.add, [lbt.b, lg.b])
            ts(omlt, omlt[:], lbt[:], -1.0, 1.0, ALU.mult, ALU.add, [lbt.b])
            ts(nomlt, nomlt[:], omlt[:], -1.0, None, ALU.mult, None, [omlt.b])
            posi = sb(st, "posi", [64, T], I32)
            ang = sb(st, "ang", [64, T], F32)
            sint = sb(st, "sint", [64, T], F32)
            cost = sb(st, "cost", [64, T], F32)
            load("sp", posi, posi[:], pos_in[0, :].partition_broadcast(64))
            cp(ang, ang[:], posi[:], [posi.b])
            ts(ang, ang[:], ang[:], invf, None, ALU.mult, None, [ang.b, cstf.b])
            sincos(st, "rp", ang, [64, T], sint, sint[:], cost, cost[:], npart=64)
            ts(sint, sint[:], sint[:], sgnr, None, ALU.mult, None, [sint.b, cstf.b])
            store("sp", cost, ropeT[0], cost[:])
            store("sp", sint, ropeT[1], sint[:])
            P.barrier()

        cbc = sb(es, "cbc", [128, 16, 128], BF16)
        with ExitStack() as st:
            cTt = sb(st, "cTt", [128, 16], F32)
            onesf = sb(st, "onesf", [128, 128], F32)
            load("sp", cTt, cTt[:], cT_in)
            act(cTt, cTt[:], cTt[:], AF.Silu, [cTt.b])
            mset(onesf, onesf[:], 1.0)
            for kc in range(16):
                ts(cbc, cbc[:, kc, :], onesf[:], cTt[:, kc:kc + 1], None, ALU.mult, None, [onesf.b, cTt.b])
            P.barrier()

        def mod_gen(l, st, pm, nb=2):
            wa = [sb(st, "wa%d" % i, [128, 16, 512], BF16) for i in range(nb)]
            bb = [sb(st, "mbb%d" % i, [128, 512], F32) for i in range(nb)]
            nn = [sb(st, "mnn%d" % i, [128, 512], F32) for i in range(nb)]
            rr = [sb(st, "mrr%d" % i, [128, 512], F32) for i in range(nb)]
            for cg in range(24):
                i = cg % nb
                sec = cg // 4
                c0 = (cg % 4) * 512
                wt = wa[i]
                load("pool", wt, wt[:], w_ada[l, :, cg * 512:(cg + 1) * 512].rearrange("(k p) n -> p k n", p=128))
                load("pool", bb[i], bb[i][:], b_ada[l, cg * 512:(cg + 1) * 512].partition_broadcast(128))
                kind = sec % 3
                if kind != 0:
                    nidx = (0 if kind == 1 else 1) + 2 * (sec // 3)
                    load("pool", nn[i], nn[i][:], norms[l, nidx, c0:c0 + 512].partition_broadcast(128))
                p_ = pm[cg % len(pm)]
                for kc in range(16):
                    mm(p_, p_[:], cbc[:, kc, :], wt[:, kc, :], kc == 0, kc == 15, [cbc.b, wt.b])
                tt(rr[i], rr[i][:], p_[:], bb[i][:], ALU.add, [p_.b, bb[i].b])
                if kind == 1:
                    stt(rr[i], rr[i][:], rr[i][:], 1.0, nn[i][:], ALU.add, ALU.mult, [rr[i].b, nn[i].b])
                elif kind == 2:
                    tt(rr[i], rr[i][:], rr[i][:], nn[i][:], ALU.mult, [rr[i].b, nn[i].b])
                store("pool", rr[i], modt[l, sec, :, c0:c0 + 512], rr[i][:])
                yield cg

        with ExitStack() as st:
            pm0 = [ps(st, "pm%d" % i) for i in range(2)]
            for _ in mod_gen(0, st, pm0):
                pass
            P.barrier()
        MOD_B1, MOD_A1, MOD_G1, MOD_B2, MOD_A2, MOD_G2 = 0, 1, 2, 3, 4, 5

        def rms_rstd(stack_tiles, src, src_ap, junk, r):
            ss, rs = stack_tiles
            act(junk, junk[:], src_ap, AF.Square, r, accum=ss[:], extra_w=[ss.b])
            act(rs, rs[:], ss[:], AF.Sqrt, [ss.b, cstf.b], bias=epsc, scale=1.0 / D)
            P.op("dve", lambda e: e.reciprocal(out=rs[:], in_=rs[:]), r=[rs.b], w=[rs.b])
            return rs

        JT = cstf[:, 384:448]

        def phase_s5(l):
            with ExitStack() as st:
                TA = sb(st, "TA", [128, 32, 64], F32)
                TB = sb(st, "TB", [128, 32, 64], F32)
                TC = sb(st, "TC", [128, 32, 64], F32)
                TD = sb(st, "TD", [128, 32, 64], F32)
                Bb1 = sb(st, "Bb1", [128, 32, 128], BF16)
                Bb2 = sb(st, "Bb2", [128, 32, 128], BF16)
                Cp1 = sb(st, "Cp1", [128, 32, 128], BF16)
                Cp2 = sb(st, "Cp2", [128, 32, 128], BF16)
                K1 = sb(st, "K1", [128, 32], F32)
                K2 = sb(st, "K2", [128, 32], F32)
                with ExitStack() as s2:
                    lamf = sb(s2, "lamf", [128, 3, 2048], F32)
                    load("sp", lamf, lamf[:].rearrange("p a n -> p (a n)"), s5lam[l].rearrange("a n -> (a n)").partition_broadcast(128))
                    lr, li, ld = lamf[:, 0, :], lamf[:, 1, :], lamf[:, 2, :]
                    tl = [sb(s2, "tl%d" % i, [128, 2048], F32) for i in range(6)]
                    dtf, af, ef, sf, cf, phf = tl
                    act(dtf, dtf[:], ld, AF.Exp, [lamf.b])
                    tt(af, af[:], lr, dtf[:], ALU.mult, [lamf.b, dtf.b])
                    tt(phf, phf[:], li, dtf[:], ALU.mult, [lamf.b, dtf.b])
                    act(ef, ef[:], af[:], AF.Exp, [af.b])
                    sincos(s2, "sf", phf, [128, 2048], sf, sf[:], cf, cf[:])
                    tt(cf, cf[:], cf[:], ef[:], ALU.mult, [cf.b, ef.b])
                    ts(cf, cf[:], cf[:], -1.0, None, ALU.add, None, [cf.b])
                    tt(sf, sf[:], sf[:], ef[:], ALU.mult, [sf.b, ef.b])
                    tt(dtf, dtf[:], lr, lr, ALU.mult, [lamf.b])
                    tt(af, af[:], li, li, ALU.mult, [lamf.b])
                    tt(dtf, dtf[:], dtf[:], af[:], ALU.add, [dtf.b, af.b])
                    P.op("dve", lambda e: e.reciprocal(out=dtf[:], in_=dtf[:]), r=[dtf.b], w=[dtf.b])
                    tt(ef, ef[:], cf[:], lr, ALU.mult, [cf.b, lamf.b])
                    tt(af, af[:], sf[:], li, ALU.mult, [sf.b, lamf.b])
                    tt(ef, ef[:], ef[:], af[:], ALU.add, [ef.b, af.b])
                    tt(ef, ef[:], ef[:], dtf[:], ALU.mult, [ef.b, dtf.b])
                    tt(phf, phf[:], sf[:], lr, ALU.mult, [sf.b, lamf.b])
                    tt(af, af[:], cf[:], li, ALU.mult, [cf.b, lamf.b])
                    tt(phf, phf[:], phf[:], af[:], ALU.subtract, [phf.b, af.b])
                    tt(phf, phf[:], phf[:], dtf[:], ALU.mult, [phf.b, dtf.b])
                    cr3 = ef[:].rearrange("p (g q) -> p g q", q=64)
                    ci3 = phf[:].rearrange("p (g q) -> p g q", q=64)
                    braw = sb(s2, "braw", [128, 2, 2048], F32)
                    load("sp", braw, braw[:], s5b[l].rearrange("a k n -> k a n"))
                    bre3 = braw[:, 0, :].rearrange("p (g q) -> p g q", q=64)
                    bim3 = braw[:, 1, :].rearrange("p (g q) -> p g q", q=64)
                    t1 = af[:].rearrange("p (g q) -> p g q", q=64)
                    t2 = sf[:].rearrange("p (g q) -> p g q", q=64)
                    tt(af, t1, cr3, bre3, ALU.mult, [ef.b, braw.b])
                    tt(sf, t2, ci3, bim3, ALU.mult, [phf.b, braw.b])
                    tt(Bb1, Bb1[:, :, 0:64], t1, t2, ALU.subtract, [af.b, sf.b])
                    tt(af, t1, cr3, bim3, ALU.mult, [ef.b, braw.b])
                    tt(sf, t2, ci3, bre3, ALU.mult, [phf.b, braw.b])
                    tt(Bb1, Bb1[:, :, 64:128], t1, t2, ALU.add, [af.b, sf.b])
                    cp(Bb2, Bb2[:, :, 0:64], Bb1[:, :, 64:128], [Bb1.b])
                    cp(Bb2, Bb2[:, :, 64:128], Bb1[:, :, 0:64], [Bb1.b])
                    P.barrier()
                with ExitStack() as s2:
                    craw = sb(s2, "craw", [128, 2, 4096], F32)
                    load("sp", craw, craw[:], s5c[l].rearrange("a k n -> k a n"))
                    ts(Cp1, Cp1[:].rearrange("p g m -> p (g m)"), craw[:, 0, :], sgnB, None, ALU.mult, None, [craw.b, cstf.b])
                    ts(Cp2, Cp2[:].rearrange("p g m -> p (g m)"), craw[:, 1, :], -1.0, None, ALU.mult, None, [craw.b])
                    lamp = sb(s2, "lamp", [128, 3, 32], F32)
                    load("sp", lamp, lamp[:], s5lamp[:, l])
                    dtp = sb(s2, "dtp", [128, 32], F32)
                    app = sb(s2, "app", [128, 32], F32)
                    php = sb(s2, "php", [128, 32], F32)
                    act(dtp, dtp[:], lamp[:, 2, :], AF.Exp, [lamp.b])
                    tt(app, app[:], lamp[:, 0, :], dtp[:], ALU.mult, [lamp.b, dtp.b])
                    tt(php, php[:], lamp[:, 1, :], dtp[:], ALU.mult, [lamp.b, dtp.b])
                    ang3 = sb(s2, "ang3", [128, 32, 64], F32)
                    ma3 = sb(s2, "ma3", [128, 32, 64], F32)
                    sn3 = sb(s2, "sn3", [128, 32, 64], F32)
                    cs3 = sb(s2, "cs3", [128, 32, 64], F32)
                    jb = JT.unsqueeze(1).broadcast_to([128, 32, 64])
                    tt(ang3, ang3[:], php[:].unsqueeze(2).broadcast_to([128, 32, 64]), jb, ALU.mult, [php.b, cstf.b])
                    tt(ma3, ma3[:], app[:].unsqueeze(2).broadcast_to([128, 32, 64]), jb, ALU.mult, [app.b, cstf.b])
                    sincos(s2, "s3", ang3, [128, 32, 64], sn3, sn3[:], cs3, cs3[:])
                    act(ang3, ang3[:], ma3[:], AF.Exp, [ma3.b], scale=-1.0)
                    tt(TA, TA[:], ang3[:], cs3[:], ALU.mult, [ang3.b, cs3.b])
                    tt(TB, TB[:], ang3[:], sn3[:], ALU.mult, [ang3.b, sn3.b])
                    ts(TB, TB[:], TB[:], sgnB, None, ALU.mult, None, [TB.b, cstf.b])
                    act(ang3, ang3[:], ma3[:], AF.Exp, [ma3.b])
                    tt(TC, TC[:], ang3[:], cs3[:], ALU.mult, [ang3.b, cs3.b])
                    tt(TD, TD[:], ang3[:], sn3[:], ALU.mult, [ang3.b, sn3.b])
                    a64 = sb(s2, "a64", [128, 32], F32)
                    p64 = sb(s2, "p64", [128, 32], F32)
                    s64 = sb(s2, "s64", [128, 32], F32)
                    c64 = sb(s2, "c64", [128, 32], F32)
                    ts(a64, a64[:], app[:], 64.0, None, ALU.mult, None, [app.b])
                    ts(p64, p64[:], php[:], 64.0, None, ALU.mult, None, [php.b])
                    sincos(s2, "s6", p64, [128, 32], s64, s64[:], c64, c64[:])
                    act(a64, a64[:], a64[:], AF.Exp, [a64.b])
                    tt(K1, K1[:], a64[:], c64[:], ALU.mult, [a64.b, c64.b])
                    tt(K2, K2[:], a64[:], s64[:], ALU.mult, [a64.b, s64.b])
                    ts(K2, K2[:], K2[:], sgnB, -1.0, ALU.mult, ALU.mult, [K2.b, cstf.b])
                    P.barrier()
                sy = ExitStack()
                ygb = sb(sy, "ygb", [128, 4, T], BF16)
                with ExitStack() as s2:
                    uTb = [sb(s2, "uTb%d" % i, [128, 2, 512], BF16) for i in range(2)]
                    Sall2 = [sb(s2, "Sall%d" % i, [128, 16, 512], F32) for i in range(2)]
                    Hs1 = sb(s2, "Hs1", [128, 16, NCK + 1], F32)
                    H2 = [sb(s2, "H2%d" % i, [128, 16], F32) for i in range(2)]
                    S63 = sb(s2, "S63", [128, 16, 8], F32)
                    S63s = sb(s2, "S63s", [128, 16, 8], F32)
                    w1 = [sb(s2, "w1%d" % i, [128, 512], F32) for i in range(3)]
                    w2 = [sb(s2, "w2%d" % i, [128, 512], F32) for i in range(3)]
                    spt = [sb(s2, "spt%d" % i, [128, 512], F32) for i in range(2)]
                    V1 = [sb(s2, "V1%d" % i, [128, 512], BF16) for i in range(2)]
                    V2 = [sb(s2, "V2%d" % i, [128, 512], BF16) for i in range(2)]
                    zt = [sb(s2, "zt%d" % i, [128, 16], F32) for i in range(6)]
                    yv = sb(s2, "yv", [128, 512], F32)
                    pbu = [ps(s2, "pbu%d" % i) for i in range(4)]
                    psw = ps(s2, "psw")
                    py = [ps(s2, "py%d" % i) for i in range(2)]
                    nwl = [0]
                    for half in range(2):
                        g0 = half * 16
                        mset(Hs1, Hs1[:, :, 0:1], 0.0)
                        mset(H2[0], H2[0][:], 0.0)
                        hcl = [0]
                        K1h = K1[:, g0:g0 + 16]
                        K2h = K2[:, g0:g0 + 16]
                        def P1(tb):
                            tsl = slice(tb * 512, (tb + 1) * 512)
                            Sall = Sall2[tb % 2]
                            uT = uTb[tb % 2]
                            for c in range(2):
                                load("pool", uT, uT[:, c, :], projT[(2 * half + c) * 128:(2 * half + c + 1) * 128, tsl])
                            pend = None
                            for gi in range(17):
                                if gi < 16:
                                    g = g0 + gi
                                    c = gi // 8
                                    nw = nwl[0]
                                    p1 = pbu[(2 * nw) % 4]
                                    p2 = pbu[(2 * nw + 1) % 4]
                                    a_, b_ = w1[nw % 3], w2[nw % 3]
                                    nwl[0] += 1
                                    mm(p1, p1[:], Bb1[:, g, :], uT[:, c, :], True, True, [Bb1.b, uT.b])
                                    mm(p2, p2[:], Bb2[:, g, :], uT[:, c, :], True, True, [Bb2.b, uT.b])
                                    tab = TA[:, g, :].unsqueeze(1).broadcast_to([128, 8, 64])
                                    tbb = TB[:, g, :].unsqueeze(1).broadcast_to([128, 8, 64])
                                    tt(a_, a_[:].rearrange("p (n j) -> p n j", j=64), p1[:].rearrange("p (n j) -> p n j", j=64), tab, ALU.mult, [p1.b, TA.b])
                                    tt(b_, b_[:].rearrange("p (n j) -> p n j", j=64), p2[:].rearrange("p (n j) -> p n j", j=64), tbb, ALU.mult, [p2.b, TB.b])
                                    tt(a_, a_[:], a_[:], b_[:], ALU.add, [a_.b, b_.b])
                                if pend is not None:
                                    pgi, pa = pend
                                    P.op("dve", lambda e, pgi=pgi, pa=pa: e.tensor_tensor_scan(out=Sall[:, pgi, :], data0=rst[:], data1=pa[:], initial=0.0, op0=ALU.mult, op1=ALU.add), r=[rst.b, pa.b], w=[Sall.b])
                                pend = (gi, a_) if gi < 16 else None

                        def BD(tb):
                            Sall = Sall2[tb % 2]
                            cp(S63, S63[:], Sall[:, :, 63:512:64], [Sall.b])
                            mm(psw, psw[:, 0:128], permsw, S63[:].rearrange("p g n -> p (g n)"), True, True, [cstf.b, S63.b])
                            cp(S63s, S63s[:].rearrange("p g n -> p (g n)"), psw[:, 0:128], [psw.b])
                            for n8 in range(8):
                                n = tb * 8 + n8
                                z1, z2, t1_, t2_, t3_, t4_ = zt
                                hcur = hcl[0]
                                hc, hn = H2[hcur], H2[1 - hcur]
                                tt(z1, z1[:], S63[:, :, n8], Hs1[:, :, n], ALU.add, [S63.b, Hs1.b])
                                tt(z2, z2[:], S63s[:, :, n8], hc[:], ALU.add, [S63s.b, hc.b])
                                tt(t1_, t1_[:], z1[:], K1h, ALU.mult, [z1.b, K1.b])
                                tt(t2_, t2_[:], z2[:], K2h, ALU.mult, [z2.b, K2.b])
                                tt(Hs1, Hs1[:, :, n + 1], t1_[:], t2_[:], ALU.add, [t1_.b, t2_.b])
                                tt(t3_, t3_[:], z2[:], K1h, ALU.mult, [z2.b, K1.b])
                                tt(t4_, t4_[:], z1[:], K2h, ALU.mult, [z1.b, K2.b])
                                tt(hn, hn[:], t3_[:], t4_[:], ALU.subtract, [t3_.b, t4_.b])
                                hcl[0] = 1 - hcur

                        def P2(tb):
                            tsl = slice(tb * 512, (tb + 1) * 512)
                            Sall = Sall2[tb % 2]
                            uT = uTb[tb % 2]
                            for c in range(2):
                                ch = 2 * half + c
                                py_ = py[(tb * 2 + c) % 2]
                                for g8 in range(8):
                                    gi = c * 8 + g8
                                    g = g0 + gi
                                    s_ = spt[g8 % 2]
                                    v1_, v2_ = V1[g8 % 2], V2[g8 % 2]
                                    hb_ = Hs1[:, gi, tb * 8:tb * 8 + 8].unsqueeze(2).broadcast_to([128, 8, 64])
                                    s3 = s_[:].rearrange("p (n j) -> p n j", j=64)
                                    for n8 in range(8):
                                        act(s_, s_[:, n8 * 64:(n8 + 1) * 64], Sall[:, gi, n8 * 64:(n8 + 1) * 64], AF.Identity, [Sall.b, Hs1.b], bias=Hs1[:, gi, tb * 8 + n8:tb * 8 + n8 + 1])
                                    tt(v1_, v1_[:].rearrange("p (n j) -> p n j", j=64), s3, TC[:, g, :].unsqueeze(1).broadcast_to([128, 8, 64]), ALU.mult, [s_.b, TC.b], eng="pool")
                                    tt(v2_, v2_[:].rearrange("p (n j) -> p n j", j=64), s3, TD[:, g, :].unsqueeze(1).broadcast_to([128, 8, 64]), ALU.mult, [s_.b, TD.b], eng="pool")
                                    mm(py_, py_[:], Cp1[:, g, :], v1_[:], g8 == 0, False, [Cp1.b, v1_.b])
                                    mm(py_, py_[:], Cp2[:, g, :], v2_[:], False, g8 == 7, [Cp2.b, v2_.b])
                                stt(yv, yv[:], uT[:, c, :], pv[:, l, 7 + ch:8 + ch], py_[:], ALU.mult, ALU.add, [uT.b, pv.b, py_.b])
                                act(ygb, ygb[:, ch, tsl], yv[:], AF.Gelu_apprx_tanh, [yv.b])

                        P1(0)
                        for tb in range(NB):
                            BD(tb)
                            if tb + 1 < NB:
                                P1(tb + 1)
                            P2(tb)
                    P.barrier()
                with ExitStack() as s2:
                    wg = sb(s2, "wg", [128, 4, 512], BF16)
                    sg = [sb(s2, "sg%d" % i, [128, 512], F32) for i in range(2)]
                    yo = [sb(s2, "yo%d" % i, [128, 512], BF16) for i in range(2)]
                    pg = [ps(s2, "pg%d" % i) for i in range(2)]
                    load("pool", wg, wg[:], w_glu[l].rearrange("(k p) n -> p k n", p=128))
                    i = 0
                    for tb in range(NB):
                        tsl = slice(tb * 512, (tb + 1) * 512)
                        for mch in range(4):
                            p_ = pg[i % 2]
                            for kc in range(4):
                                mm(p_, p_[:], wg[:, kc, mch * 128:(mch + 1) * 128], ygb[:, kc, tsl], kc == 0, kc == 3, [wg.b, ygb.b])
                            act(sg[i % 2], sg[i % 2][:], p_[:], AF.Sigmoid, [p_.b])
                            tt(yo[i % 2], yo[i % 2][:], ygb[:, mch, tsl], sg[i % 2][:], ALU.mult, [ygb.b, sg[i % 2].b])
                            store("sp", yo[i % 2], yT[mch * 128:(mch + 1) * 128, tsl], yo[i % 2][:])
                            i += 1
                    P.barrier()
                sy.close()

        def phase_mla(l):
            SCALE = 192.0 ** -0.5
            with ExitStack() as st:
                cqn = sb(st, "cqn", [128, 4, T], BF16)
                ckn = sb(st, "ckn", [128, 2, T], BF16)
                krr = sb(st, "krr", [64, T], BF16)
                wq = sb(st, "wq", [128, 4, 2048], BF16)
                wkv = sb(st, "wkv", [128, 2, 2048], BF16)
                maskA = sb(st, "maskA", [128, 2048], BF16)
                load("pool", maskA, maskA[:], maskA_in)
                cosT = sb(st, "cosT", [64, T], F32)
                sinT = sb(st, "sinT", [64, T], F32)
                load("pool", wq, wq[:], w_uq[l].rearrange("(k p) n -> p k n", p=128))
                load("pool", wkv, wkv[:], w_ukv[l].rearrange("(k p) n -> p k n", p=128))
                load("sp", cosT, cosT[:], ropeT[0])
                load("sp", sinT, sinT[:], ropeT[1])
                with ExitStack() as s2:
                    raw = [sb(s2, "raw%d" % i, [128, 6, 512], F32) for i in range(2)]
                    sq = [sb(s2, "sq%d" % i, [128, 6, 512], BF16) for i in range(2)]
                    rq = [sb(s2, "rq%d" % i, [128, 512], F32) for i in range(2)]
                    rk = [sb(s2, "rk%d" % i, [128, 512], F32) for i in range(2)]
                    kra = [sb(s2, "kra%d" % i, [64, 512], F32) for i in range(2)]
                    krb = [sb(s2, "krb%d" % i, [64, 512], F32) for i in range(2)]
                    pq_ = [ps(s2, "pssq%d" % i) for i in range(2)]
                    pk_ = [ps(s2, "pssk%d" % i) for i in range(2)]
                    for tb in range(NB):
                        tsl = slice(tb * 512, (tb + 1) * 512)
                        i = tb % 2
                        load("sp", raw[i], raw[i][:], projT[512:1280, tsl].rearrange("(k p) t -> p k t", p=128))
                        load("sp", kra[i], kra[i][:], projT[1280:1344, tsl])
                        load("sp", krb[i], krb[i][:], projT[1344:1408, tsl])
                        act(sq[i], sq[i][:], raw[i][:], AF.Square, [raw[i].b])
                        for kc in range(4):
                            mm(pq_[i], pq_[i][:], ones[:], sq[i][:, kc, :], kc == 0, kc == 3, [ones.b, sq[i].b])
                        for kc in range(2):
                            mm(pk_[i], pk_[i][:], ones[:], sq[i][:, 4 + kc, :], kc == 0, kc == 1, [ones.b, sq[i].b])
                        act(rq[i], rq[i][:], pq_[i][:], AF.Sqrt, [pq_[i].b, cstf.b], bias=epsc, scale=1.0 / 512)
                        P.op("dve", lambda e, i=i: e.reciprocal(out=rq[i][:], in_=rq[i][:]), r=[rq[i].b], w=[rq[i].b])
                        act(rk[i], rk[i][:], pk_[i][:], AF.Sqrt, [pk_[i].b, cstf.b], bias=epsc, scale=1.0 / 256)
                        P.op("dve", lambda e, i=i: e.reciprocal(out=rk[i][:], in_=rk[i][:]), r=[rk[i].b], w=[rk[i].b])
                        for kc in range(4):
                            stt(cqn, cqn[:, kc, tsl], raw[i][:, kc, :], pv[:, l, kc:kc + 1], rq[i][:], ALU.mult, ALU.mult, [raw[i].b, pv.b, rq[i].b])
                        for kc in range(2):
                            stt(ckn, ckn[:, kc, tsl], raw[i][:, 4 + kc, :], pv[:, l, 4 + kc:5 + kc], rk[i][:], ALU.mult, ALU.mult, [raw[i].b, pv.b, rk[i].b])
                        tt(kra[i], kra[i][:], kra[i][:], cosT[:, tsl], ALU.mult, [kra[i].b, cosT.b])
                        tt(krb[i], krb[i][:], krb[i][:], sinT[:, tsl], ALU.mult, [krb[i].b, sinT.b])
                        tt(krr, krr[:, tsl], kra[i][:], krb[i][:], ALU.add, [kra[i].b, krb[i].b])
                    P.barrier()
                with ExitStack() as s2:
                    qn_ = sb(s2, "qn_", [128, T], BF16)
                    qr_ = sb(s2, "qr_", [64, T], BF16)
                    kn_ = sb(s2, "kn_", [128, T], BF16)
                    v_ = sb(s2, "v_", [128, NT, 128], BF16)
                    r1 = [sb(s2, "r1%d" % i, [64, 512], F32) for i in range(2)]
                    r2 = [sb(s2, "r2%d" % i, [64, 512], F32) for i in range(2)]
                    pT = [sb(s2, "pT%d" % i, [128, 512], BF16) for i in range(3)]
                    rsm = [sb(s2, "rsm%d" % i, [128, 512], F32) for i in range(2)]
                    yh = [sb(s2, "yh%d" % i, [128, 512], BF16) for i in range(2)]
                    pgen = [ps(s2, "pgen%d" % i) for i in range(2)]
                    pS = [ps(s2, "pS%d" % i) for i in range(2)]
                    po = [ps(s2, "po%d" % i) for i in range(2)]
                    psm = [ps(s2, "psm%d" % i) for i in range(2)]
                    ng = 0
                    npt = 0
                    nq = 0
                    mgen = mod_gen(l + 1, s2, pgen, nb=1) if l + 1 < L else None
                    nkbt = 0
                    for h in range(8):
                        c0 = h * 256
                        for tb in range(NB):
                            tsl = slice(tb * 512, (tb + 1) * 512)
                            p_ = pgen[ng % 2]; ng += 1
                            for kc in range(4):
                                mm(p_, p_[:], wq[:, kc, c0:c0 + 128], cqn[:, kc, tsl], kc == 0, kc == 3, [wq.b, cqn.b])
                            act(qn_, qn_[:, tsl], p_[:], AF.Copy, [p_.b])
                            pr = pgen[ng % 2]; ng += 1
                            for kc in range(4):
                                mm(pr, pr[0:64, :], wq[:, kc, c0 + 128:c0 + 192], cqn[:, kc, tsl], kc == 0, kc == 3, [wq.b, cqn.b])
                            a_ = r1[tb % 2]
                            tt(a_, a_[:], pr[0:64, :], cosT[:, tsl], ALU.mult, [pr.b, cosT.b])
                            pw = pgen[ng % 2]; ng += 1
                            for kc in range(4):
                                mm(pw, pw[0:64, :], wq[:, kc, c0 + 192:c0 + 256], cqn[:, kc, tsl], kc == 0, kc == 3, [wq.b, cqn.b])
                            b_ = r2[tb % 2]
                            tt(b_, b_[:], pw[0:64, :], sinT[:, tsl], ALU.mult, [pw.b, sinT.b])
                            tt(qr_, qr_[:, tsl], a_[:], b_[:], ALU.add, [a_.b, b_.b])
                            pk = pgen[ng % 2]; ng += 1
                            for kc in range(2):
                                mm(pk, pk[:], wkv[:, kc, c0:c0 + 128], ckn[:, kc, tsl], kc == 0, kc == 1, [wkv.b, ckn.b])
                            act(kn_, kn_[:, tsl], pk[:], AF.Copy, [pk.b])
                            pvv = pgen[ng % 2]; ng += 1
                            for ti in range(4):
                                t0 = tb * 512 + ti * 128
                                for kc in range(2):
                                    mm(pvv, pvv[:, ti * 128:(ti + 1) * 128], ckn[:, kc, t0:t0 + 128], wkv[:, kc, c0 + 128:c0 + 256], kc == 0, kc == 1, [wkv.b, ckn.b])
                            cp(v_, v_[:, tb * 4:tb * 4 + 4, :], pvv[:].rearrange("p (a d) -> p a d", d=128), [pvv.b])
                        for qc in range(NB):
                            qsl = slice(qc * 512, (qc + 1) * 512)
                            po_ = po[nq % 2]
                            pm_ = psm[nq % 2]
                            nkb = 4 * (qc + 1)
                            def s_stage(kb, idx):
                                ksl = slice(kb * 128, (kb + 1) * 128)
                                s_ = pS[idx % 2]
                                mm(s_, s_[:], kn_[:, ksl], qn_[:, qsl], True, False, [kn_.b, qn_.b])
                                mm(s_, s_[:], krr[0:64, ksl], qr_[0:64, qsl], False, True, [krr.b, qr_.b])
                                return s_
                            nxt = s_stage(0, npt)
                            for kb in range(nkb):
                                s_ = nxt
                                t_ = pT[npt % 3]
                                npt += 1
                                if kb + 1 < nkb:
                                    nxt = s_stage(kb + 1, npt)
                                act(t_, t_[:], s_[:], AF.Exp, [s_.b], scale=SCALE)
                                if kb >= 4 * qc:
                                    j = kb - 4 * qc
                                    tt(t_, t_[:], t_[:], maskA[:, j * 512:(j + 1) * 512], ALU.mult, [t_.b, maskA.b])
                                mm(po_, po_[:], v_[:, kb, :], t_[:], kb == 0, kb == nkb - 1, [v_.b, t_.b])
                                mm(pm_, pm_[:], ones[:], t_[:], kb == 0, kb == nkb - 1, [ones.b, t_.b])
                                nkbt += 1
                                if mgen is not None and nkbt % 40 == 0:
                                    next(mgen, None)
                            rs_ = rsm[nq % 2]
                            y_ = yh[nq % 2]
                            nq += 1
                            P.op("dve", lambda e, rs_=rs_, pm_=pm_: e.reciprocal(out=rs_[:], in_=pm_[:]), r=[pm_.b], w=[rs_.b])
                            tt(y_, y_[:], po_[:], rs_[:], ALU.mult, [po_.b, rs_.b])
                            store("sp", y_, yT[512 + h * 128:512 + (h + 1) * 128, qsl], y_[:])
                    if mgen is not None:
                        for _ in mgen:
                            pass
                    P.barrier()

        def phase_hg(l):
            HB = min(T, 1024)
            NHB = T // HB
            with ExitStack() as st:
                NC2 = T // 32
                nck = HB // 32
                qt_ = sb(st, "qt_", [128, T], BF16)
                kt_ = sb(st, "kt_", [128, T], BF16)
                qh_ = sb(st, "qh_", [128, T], F32)
                khT = sb(st, "khT", [32, NC2, 128], BF16)
                vt = sb(st, "vt", [32, NC2, 128], BF16)
                ebl = sb(st, "ebl", [128, NC2], F32)
                oT = sb(st, "oT", [128, T], F32)
                qtq = [Tile(qt_.t, "qt_") for _ in range(NHB)]
                ktq = [Tile(kt_.t, "kt_") for _ in range(NHB)]
                qhq = [Tile(qh_.t, "qh_") for _ in range(NHB)]
                khq = [Tile(khT.t, "khT") for _ in range(NHB)]
                ebq = [Tile(ebl.t, "ebl") for _ in range(NHB)]
                vtq = [Tile(vt.t, "vt") for _ in range(NHB)]
                oT2 = [oT, sb(st, "oTb", [128, T], F32)]
                nsq = sb(st, "nsq", [128, HB], BF16)
                nE = sb(st, "nE", [128, HB], F32)
                nG = sb(st, "nG", [128, HB], F32)
                S32 = [sb(st, "S32%d" % i, [128, 128], F32) for i in range(4)]
                Zt = sb(st, "hZ", [128, HB], F32)
                Qt = sb(st, "hQ", [128, HB], F32)
                SGt = sb(st, "hSG", [128, HB], F32)
                Bt = sb(st, "hB", [128, HB], F32)
                Et = sb(st, "hE", [128, HB], F32)
                khb = sb(st, "khb", [128, HB], BF16)
                aTt = [sb(st, "aTt%d" % i, [32, 32], BF16) for i in range(3)]
                yo_ = sb(st, "hyo", [128, HB], BF16)
                ptk = ps(st, "ptk", (128, 1024), BF16)
                pnn = ps(st, "hpnn")
                pa_ = [ps(st, "hpa%d" % i) for i in range(2)]
                pob = [ps(st, "hpo%d" % i) for i in range(2)]
                pst = [ps(st, "hpst%d" % i) for i in range(2)]

                def prep_gen(h, hb):
                    lb_ap = lbt[:, h, l:l + 1]
                    oml_ap = omlt[:, h, l:l + 1]
                    noml_ap = nomlt[:, h, l:l + 1]
                    hsl = slice(hb * HB, (hb + 1) * HB)
                    P.op("sp", lambda e: e.dma_start(out=vt[:, hb * nck:(hb + 1) * nck, :], in_=vtok[hb * HB:(hb + 1) * HB, h * 128:(h + 1) * 128].rearrange("(n s) d -> s n d", s=32)), w=[vtq[hb].b], dsem="Lvt%d" % (hb % 2))
                    load("sp", Zt, Zt[:], projT[2048 + h * 128:2048 + (h + 1) * 128, hsl])
                    load("sp", Qt, Qt[:], projT[1536 + h * 128:1536 + (h + 1) * 128, hsl])
                    act(SGt, SGt[:], Zt[:], AF.Sigmoid, [Zt.b])
                    yield
                    ts(Zt, Zt[:], SGt[:], oml_ap, lb_ap, ALU.mult, ALU.add, [SGt.b, omlt.b, lbt.b])
                    yield
                    ts(SGt, SGt[:], SGt[:], noml_ap, oml_ap, ALU.mult, ALU.add, [SGt.b, omlt.b, nomlt.b])
                    act(Zt, Zt[:], Zt[:], AF.Ln, [Zt.b])
                    yield
                    for tb in range(HB // 512):
                        bsl = slice(tb * 512, (tb + 1) * 512)
                        P.op("dve", lambda e, bsl=bsl: e.tensor_tensor_scan(out=Bt[:, bsl], data0=rst32[:], data1=Zt[:, bsl], initial=0.0, op0=ALU.mult, op1=ALU.add), r=[rst32.b, Zt.b], w=[Bt.b])
                        yield
                    act(Qt, Qt[:], Qt[:], AF.Silu, [Qt.b])
                    b3 = Bt[:].rearrange("p (n j) -> p n j", j=32)
                    c3 = Zt[:].rearrange("p (n j) -> p n j", j=32)
                    tt(Zt, c3, b3, b3[:, :, 15:16].broadcast_to([128, nck, 32]), ALU.subtract, [Bt.b], eng="pool")
                    act(Et, Et[:], Zt[:], AF.Exp, [Zt.b])
                    yield
                    tt(qtq[hb], qt_[:, hsl], Qt[:], Et[:], ALU.mult, [Qt.b, Et.b])
                    act(Et, Et[:], Zt[:], AF.Exp, [Zt.b], scale=-1.0)
                    yield
                    tt(ktq[hb], kt_[:, hsl], SGt[:], Et[:], ALU.mult, [SGt.b, Et.b])
                    act(Et, Et[:], Bt[:], AF.Exp, [Bt.b])
                    yield
                    tt(qhq[hb], qh_[:, hsl], Qt[:], Et[:], ALU.mult, [Qt.b, Et.b])
                    tt(Zt, c3, b3[:, :, 31:32].broadcast_to([128, nck, 32]), b3, ALU.subtract, [Bt.b], eng="pool")
                    act(Et, Et[:], Zt[:], AF.Exp, [Zt.b])
                    yield
                    tt(khb, khb[:], SGt[:], Et[:], ALU.mult, [SGt.b, Et.b])
                    act(ebq[hb], ebl[:, hb * nck:(hb + 1) * nck], Bt[:, 31:HB:32], AF.Exp, [Bt.b])
                    yield
                    for n16 in range(nck // 8):
                        for n8 in range(8):
                            n = n16 * 8 + n8
                            P.op("pe", lambda e, n8=n8, n=n: e.transpose(ptk[0:32, n8 * 128:(n8 + 1) * 128], khb[:, n * 32:(n + 1) * 32], ident[:]), r=[khb.b, ident.b], w=[ptk.b])
                        n0 = hb * nck + n16 * 8
                        cp(khq[hb], khT[:, n0:n0 + 8, :], ptk[0:32, :].rearrange("p (n d) -> p n d", d=128), [ptk.b])
                        yield

                def norm_gen(h):
                    oTh = oT2[h % 2]
                    for hb in range(NHB):
                        hsl = slice(hb * HB, (hb + 1) * HB)
                        load("sp", nG, nG[:], projT[2560 + h * 128:2560 + (h + 1) * 128, hsl])
                        act(nsq, nsq[:], oTh[:, hsl], AF.Square, [oTh.b])
                        yield
                        for tb in range(HB // 512):
                            bsl = slice(tb * 512, (tb + 1) * 512)
                            mm(pnn, pnn[:], ones[:], nsq[:, bsl], True, True, [ones.b, nsq.b])
                            act(nE, nE[:, bsl], pnn[:], AF.Sqrt, [pnn.b, cstf.b], bias=epsc, scale=1.0 / 128)
                            yield
                        P.op("dve", lambda e: e.reciprocal(out=nE[:], in_=nE[:]), r=[nE.b], w=[nE.b])
                        yield
                        tt(nE, nE[:], nE[:], oTh[:, hsl], ALU.mult, [nE.b, oTh.b])
                        act(nG, nG[:], nG[:], AF.Silu, [nG.b])
                        yield
                        stt(yo_, yo_[:], nE[:], pv[:, l, 6:7], nG[:], ALU.mult, ALU.mult, [nE.b, pv.b, nG.b])
                        store("sp", yo_, yT[1536 + h * 128:1536 + (h + 1) * 128, hsl], yo_[:])
                        yield

                chain = NHB > 1
                ngen = None
                for h in range(4):
                    oT = oT2[h % 2]
                    if h == 0 or not chain:
                        for _ in prep_gen(h, 0):
                            pass
                    mset(S32[0], S32[0][:], 0.0)
                    R = len(S32)

                    def o_stage(n):
                        q = n // nck
                        nsl = slice(n * 32, (n + 1) * 32)
                        pb = pob[(n // 16) % 2]
                        osl = slice((n % 16) * 32, (n % 16 + 1) * 32)
                        at = aTt[n % 3]
                        mm(pb, pb[:, osl], vt[:, n, :], at[:], True, False, [vtq[q].b, at.b])
                        mm(pb, pb[:, osl], S32[n % R][:], qh_[:, nsl], False, True, [S32[n % R].b, qhq[q].b])
                        if n % 16 == 15:
                            act(oT, oT[:, (n - 15) * 32:(n + 1) * 32], pb[:], AF.Copy, [pb.b])

                    for q in range(NHB):
                        if q + 1 < NHB:
                            pg = prep_gen(h, q + 1)
                        elif chain and h + 1 < 4:
                            pg = prep_gen(h + 1, 0)
                        else:
                            pg = None
                        for n in range(q * nck, (q + 1) * nck):
                            nsl = slice(n * 32, (n + 1) * 32)
                            s_ = pst[n % 2]
                            mm(s_, s_[:, 0:128], khT[:, n, :], vt[:, n, :], True, True, [khq[q].b, vtq[q].b])
                            stt(S32[(n + 1) % R], S32[(n + 1) % R][:], S32[n % R][:], ebl[:, n:n + 1], s_[:, 0:128], ALU.mult, ALU.add, [S32[n % R].b, ebq[q].b, s_.b])
                            a_ = pa_[n % 2]
                            at = aTt[n % 3]
                            mm(a_, a_[0:32, 0:32], kt_[:, nsl], qt_[:, nsl], True, True, [ktq[q].b, qtq[q].b])
                            tt(at, at[:], a_[0:32, 0:32], mask64[0:32, 0:32], ALU.mult, [a_.b, mask64.b])
                            if n >= 1:
                                o_stage(n - 1)
                            if pg is not None and n % 2 == 1:
                                next(pg, None)
                            if ngen is not None and n % 2 == 0:
                                next(ngen, None)
                        if pg is not None:
                            for _ in pg:
                                pass
                    o_stage(NC2 - 1)
                    if ngen is not None:
                        for _ in ngen:
                            pass
                    ngen = norm_gen(h)
                    if not chain or h == 3:
                        for _ in ngen:
                            pass
                        ngen = None
                P.barrier()

        def phase_c(l):
            with ExitStack() as st:
                wo = sb(st, "wo", [128, 16, 2048], BF16)
                G1 = sb(st, "G1", [128, D], F32)
                A2 = sb(st, "A2", [128, D], F32)
                B2 = sb(st, "B2", [128, D], F32)
                yTt = [sb(st, "yTt%d" % i, [128, 16, 128], BF16) for i in range(2)]
                xt = [sb(st, "cxt%d" % i, [128, D], F32) for i in range(2)]
                tmp = sb(st, "ctmp", [128, D], F32)
                tmp2 = sb(st, "ctmp2", [128, D], F32)
                hb = [sb(st, "chb%d" % i, [128, D], BF16) for i in range(2)]
                junk = sb(st, "cjunk", [128, D], BF16)
                h2s = [sb(st, "h2s%d" % i, [128, 16, 128], BF16) for i in range(2)]
                sst = [(sb(st, "css%d" % i, [128, 1], F32), sb(st, "crs%d" % i, [128, 1], F32)) for i in range(4)]
                pmx = ps(st, "pmx", (128, 2048), F32)
                ptr = [ps(st, "cptr%d" % i, (128, 2048), BF16) for i in range(2)]
                for cg in range(4):
                    P.op("pool", lambda e, cg=cg: e.dma_start(out=wo[:, :, cg * 512:(cg + 1) * 512], in_=w_out[l, :, cg * 512:(cg + 1) * 512].rearrange("(k p) n -> p k n", p=128)), w=[wo.b], dsem="Lwo")
                load("sp", G1, G1[:], modt[l, MOD_G1])
                load("sp", A2, A2[:], modt[l, MOD_A2])
                load("sp", B2, B2[:], modt[l, MOD_B2])
                def OP(gt):
                    i = gt % 2
                    rows = slice(gt * 128, (gt + 1) * 128)
                    load("sp", yTt[i], yTt[i][:], yT[:, rows].rearrange("(k p) t -> p k t", p=128))
                    load("sp", xt[i], xt[i][:], xsrc_of[0][rows, :])
                    for ng_ in range(4):
                        for kc in range(16):
                            mm(pmx, pmx[:, ng_ * 512:(ng_ + 1) * 512], yTt[i][:, kc, :], wo[:, kc, ng_ * 512:(ng_ + 1) * 512], kc == 0, kc == 15, [yTt[i].b, wo.b])

                def CH1(gt):
                    i = gt % 2
                    rs = rms_rstd(sst[2 * i], pmx, pmx[:], junk, [pmx.b])
                    stt(tmp, tmp[:], pmx[:], rs[:], G1[:], ALU.mult, ALU.mult, [pmx.b, rs.b, G1.b])

                def CH2(gt):
                    i = gt % 2
                    rows = slice(gt * 128, (gt + 1) * 128)
                    tt(xt[i], xt[i][:], xt[i][:], tmp[:], ALU.add, [xt[i].b, tmp.b])
                    store("sp", xt[i], out[rows, :], xt[i][:])
                    rs2 = rms_rstd(sst[2 * i + 1], xt[i], xt[i][:], junk, [xt[i].b])
                    stt(tmp2, tmp2[:], xt[i][:], rs2[:], A2[:], ALU.mult, ALU.mult, [xt[i].b, rs2.b, A2.b])
                    tt(hb[i], hb[i][:], tmp2[:], B2[:], ALU.add, [tmp2.b, B2.b])
                    p_ = ptr[i]
                    for kc in range(16):
                        P.op("pe", lambda e, kc=kc, p_=p_, i=i: e.transpose(p_[:, kc * 128:(kc + 1) * 128], hb[i][:, kc * 128:(kc + 1) * 128], ident[:]), r=[hb[i].b, ident.b], w=[p_.b])
                    act(h2s[i], h2s[i][:], p_[:].rearrange("p (k t) -> p k t", k=16), AF.Copy, [p_.b])
                    store("sp", h2s[i], h2T[:, rows].rearrange("(k p) t -> p k t", p=128), h2s[i][:])

                OP(0)
                for gt in range(NT):
                    CH1(gt)
                    if gt + 1 < NT:
                        OP(gt + 1)
                    CH2(gt)
                P.barrier()
            if stop == "C1":
                return
            SUB = min(1024, SBT)
            with ExitStack() as st:
                h2 = sb(st, "h2", [128, 16, SBT], BF16)
                wg_ = [sb(st, "wg_%d" % i, [128, 16, 256], BF16) for i in range(2)]
                wv_ = [sb(st, "wv_%d" % i, [128, 16, 256], BF16) for i in range(2)]
                rawg = [sb(st, "rawg%d" % i, [128, 2 + SBT], F32) for i in range(2)]
                rawv = [sb(st, "rawv%d" % i, [128, 2 + SBT], F32) for i in range(2)]
                cg_ = [sb(st, "cg_%d" % i, [128, SUB], F32) for i in range(2)]
                cv_ = [sb(st, "cv_%d" % i, [128, SUB], F32) for i in range(2)]
                ao = [sb(st, "ao%d" % i, [128, SUB], BF16) for i in range(2)]
                halo = sb(st, "halo", [128, 86, 2], F32)
                cvp = sb(st, "cvp", [128, 86, 4], F32)
                pg_ = [ps(st, "fpg%d" % i, (128, SUB), F32) for i in range(2)]
                pv_ = [ps(st, "fpv%d" % i, (128, SUB), F32) for i in range(2)]
                load("sp", cvp, cvp[:], convp[:, l])
                it = 0
                for sbi in range(NSB):
                    load("sp", h2, h2[:], h2T[:, sbi * SBT:(sbi + 1) * SBT].rearrange("(k p) t -> p k t", p=128))
                    for jg in range(22):
                        ncol = 256 if jg < 21 else 128
                        wgt, wvt = wg_[jg % 2], wv_[jg % 2]
                        load("pool", wgt, wgt[:, :, 0:ncol], w_up[l, :, jg * 256:jg * 256 + ncol].rearrange("(k p) n -> p k n", p=128))
                        load("pool", wvt, wvt[:, :, 0:ncol], w_up[l, :, DFF + jg * 256:DFF + jg * 256 + ncol].rearrange("(k p) n -> p k n", p=128))
                        for jj in range(ncol // 128):
                            j = jg * 2 + jj
                            rg, rv = rawg[j % 2], rawv[j % 2]
                            if sbi == 0:
                                mset(rg, rg[:, 0:2], 0.0)
                                mset(rv, rv[:, 0:2], 0.0)
                            else:
                                cp(rg, rg[:, 0:2], halo[:, j, :], [halo.b])
                                cp(rv, rv[:, 0:2], halo[:, 43 + j, :], [halo.b])
                            for sub in range(SBT // SUB):
                                pg, pvl = pg_[it % 2], pv_[it % 2]
                                cg, cv, a_ = cg_[it % 2], cv_[it % 2], ao[it % 2]
                                it += 1
                                off = sub * SUB
                                for t5 in range(SUB // 512):
                                    for kc in range(16):
                                        mm(pg, pg[:, t5 * 512:(t5 + 1) * 512], wgt[:, kc, jj * 128:(jj + 1) * 128], h2[:, kc, off + t5 * 512:off + (t5 + 1) * 512], kc == 0, kc == 15, [wgt.b, h2.b])
                                for t5 in range(SUB // 512):
                                    for kc in range(16):
                                        mm(pvl, pvl[:, t5 * 512:(t5 + 1) * 512], wvt[:, kc, jj * 128:(jj + 1) * 128], h2[:, kc, off + t5 * 512:off + (t5 + 1) * 512], kc == 0, kc == 15, [wvt.b, h2.b])
                                act(rg, rg[:, 2 + off:2 + off + SUB], pg[:], AF.Copy, [pg.b])
                                act(rv, rv[:, 2 + off:2 + off + SUB], pvl[:], AF.Copy, [pvl.b])
                                act(cg, cg[:], pg[:], AF.Identity, [pg.b, cvp.b], bias=cvp[:, j, 3:4], scale=cvp[:, j, 2:3])
                                act(cv, cv[:], pvl[:], AF.Identity, [pvl.b, cvp.b], bias=cvp[:, 43 + j, 3:4], scale=cvp[:, 43 + j, 2:3])
                                stt(cg, cg[:], rg[:, 1 + off:1 + off + SUB], cvp[:, j, 1:2], cg[:], ALU.mult, ALU.add, [rg.b, cvp.b, cg.b])
                                stt(cg, cg[:], rg[:, off:off + SUB], cvp[:, j, 0:1], cg[:], ALU.mult, ALU.add, [rg.b, cvp.b, cg.b])
                                stt(cv, cv[:], rv[:, 1 + off:1 + off + SUB], cvp[:, 43 + j, 1:2], cv[:], ALU.mult, ALU.add, [rv.b, cvp.b, cv.b])
                                stt(cv, cv[:], rv[:, off:off + SUB], cvp[:, 43 + j, 0:1], cv[:], ALU.mult, ALU.add, [rv.b, cvp.b, cv.b])
                                act(cg, cg[:], cg[:], AF.Gelu_apprx_tanh, [cg.b])
                                tt(a_, a_[:], cg[:], cv[:], ALU.mult, [cg.b, cv.b])
                                c0 = sbi * SBT + off
                                store("sp", a_, aT[c0 // 128:(c0 + SUB) // 128, :, j, :].rearrange("n p t -> p n t"), a_[:].rearrange("p (n t) -> p n t", t=128))
                            if sbi < NSB - 1:
                                cp(halo, halo[:, j, :], rg[:, SBT:SBT + 2], [rg.b])
                                cp(halo, halo[:, 43 + j, :], rv[:, SBT:SBT + 2], [rv.b])
                P.barrier()
            if stop == "C2":
                return
            st0 = ExitStack()
            ssq = ssqP
            with ExitStack() as st:
                wd = [sb(st, "wd%d" % i, [128, NCH, 512], BF16) for i in range(2)]
                at_ = [sb(st, "at_%d" % i, [128, NCH, 128], BF16) for i in range(3)]
                ys = [sb(st, "ys%d" % i, [128, 512], F32) for i in range(2)]
                junk = sb(st, "djunk", [128, 512], BF16)
                pd = [ps(st, "pd%d" % i) for i in range(2)]
                mg = None

                def load_wd(ng_):
                    for k0 in range(0, NCH, 22):
                        k1 = min(NCH, k0 + 22)
                        P.op("pool", lambda e, k0=k0, k1=k1, ng_=ng_: e.dma_start(out=wd[ng_ % 2][:, k0:k1, :], in_=w_down[l, k0 * 128:k1 * 128, ng_ * 512:(ng_ + 1) * 512].rearrange("(k p) n -> p k n", p=128)), w=[wd[ng_ % 2].b], dsem="Lwd%d" % (ng_ % 2))

                load_wd(0)
                it = 0
                NIT = 4 * NT
                for j in range(min(2, NIT)):
                    load("sp", at_[j % 3], at_[j % 3][:], aT[j % NT])
                for ng_ in range(4):
                    if ng_ + 1 < 4:
                        load_wd(ng_ + 1)
                    w_ = wd[ng_ % 2]
                    for gt in range(NT):
                        i = it % 2
                        a3 = at_[it % 3]
                        if it + 2 < NIT:
                            load("sp", at_[(it + 2) % 3], at_[(it + 2) % 3][:], aT[(it + 2) % NT])
                        it += 1
                        rows = slice(gt * 128, (gt + 1) * 128)
                        p_ = pd[i]
                        for kc in range(NCH):
                            mm(p_, p_[:], a3[:, kc, :], w_[:, kc, :], kc == 0, kc == NCH - 1, [a3.b, w_.b])
                        act(ys[i], ys[i][:], p_[:], AF.Copy, [p_.b])
                        act(junk, junk[:], p_[:], AF.Square, [p_.b], accum=ssq[:, gt, ng_:ng_ + 1], extra_w=[ssq.b])
                        store("sp", ys[i], ydn[rows, ng_ * 512:(ng_ + 1) * 512], ys[i][:])
                        if mg is not None and ng_ >= 1:
                            next(mg, None)
                if mg is not None:
                    for _ in mg:
                        pass
                P.barrier()
            if stop == "C3a" or l < L - 1:
                st0.close()
                return
            with ExitStack() as st:
                G2 = sb(st, "G2", [128, D], F32)
                yt = [sb(st, "dyt%d" % i, [128, D], F32) for i in range(3)]
                xt = [sb(st, "dxt%d" % i, [128, D], F32) for i in range(3)]
                rs = [sb(st, "drs%d" % i, [128, 1], F32) for i in range(3)]
                load("sp", G2, G2[:], modt[l, MOD_G2])

                def ld(gt):
                    i = gt % 3
                    rows = slice(gt * 128, (gt + 1) * 128)
                    load("sp", yt[i], yt[i][:], ydn[rows, :])
                    load("sp", xt[i], xt[i][:], out[rows, :])

                ld(0)
                if NT > 1:
                    ld(1)
                for gt in range(NT):
                    i = gt % 3
                    rows = slice(gt * 128, (gt + 1) * 128)
                    if gt + 2 < NT:
                        ld(gt + 2)
                    P.op("dve", lambda e, i=i, gt=gt: e.reduce_sum(out=rs[i][:], in_=ssq[:, gt, :], axis=AX.X), r=[ssq.b], w=[rs[i].b])
                    act(rs[i], rs[i][:], rs[i][:], AF.Sqrt, [rs[i].b, cstf.b], bias=epsc, scale=1.0 / D)
                    P.op("dve", lambda e, i=i: e.reciprocal(out=rs[i][:], in_=rs[i][:]), r=[rs[i].b], w=[rs[i].b])
                    stt(yt[i], yt[i][:], yt[i][:], rs[i][:], G2[:], ALU.mult, ALU.mult, [yt[i].b, rs[i].b, G2.b])
                    tt(xt[i], xt[i][:], xt[i][:], yt[i][:], ALU.add, [xt[i].b, yt[i].b])
                    store("sp", xt[i], out[rows, :], xt[i][:])
                P.barrier()
            st0.close()

        xsrc_of = [x_in]
        for l in range(L):
            xsrc = x_in if l == 0 else out
            xsrc_of[0] = xsrc
            with ExitStack() as st:
                A1 = sb(st, "A1", [128, D], F32)
                B1 = sb(st, "B1", [128, D], F32)
                hT = sb(st, "hT", [128, 16, SBT], BF16)
                xt = [sb(st, "xt%d" % i, [128, D], F32) for i in range(2)]
                junk = sb(st, "junk", [128, D], BF16)
                hb = [sb(st, "hb%d" % i, [128, D], BF16) for i in range(2)]
                sst = [(sb(st, "ss%d" % i, [128, 1], F32), sb(st, "rs%d" % i, [128, 1], F32)) for i in range(2)]
                wb = [sb(st, "wb%d" % i, [128, 16, 512], BF16) for i in range(2)]
                stg = [sb(st, "stg%d" % i, [128, 512], F32) for i in range(3)]
                vst = [sb(st, "vst%d" % i, [128, 512], BF16) for i in range(2)]
                ptr = [ps(st, "ptr%d" % i, (128, 2048), BF16) for i in range(2)]
                pa = [ps(st, "pa%d" % i) for i in range(4)]
                load("sp", A1, A1[:], modt[l, MOD_A1])
                load("sp", B1, B1[:], modt[l, MOD_B1])
                ytA = [sb(st, "ytA%d" % i, [128, D], F32) for i in range(2)]
                if l > 0:
                    G2p = sb(st, "G2p", [128, D], F32)
                    rsd = [sb(st, "rsd%d" % i, [128, 1], F32) for i in range(2)]
                    load("sp", G2p, G2p[:], modt[l - 1, MOD_G2])
                nst = 0
                npa = 0
                nv = 0
                for sbi in range(NSB):
                    for ti in range(SBT // 128):
                        gt = sbi * (SBT // 128) + ti
                        x_ = xt[gt % 2]
                        y_ = ytA[gt % 2]
                        rows = slice(gt * 128, (gt + 1) * 128)
                        load("sp", x_, x_[:], xsrc[rows, :])
                        if l > 0:
                            r_ = rsd[gt % 2]
                            load("sp", y_, y_[:], ydn[rows, :])
                            P.op("dve", lambda e, r_=r_, gt=gt: e.reduce_sum(out=r_[:], in_=ssqP[:, gt, :], axis=AX.X), r=[ssqP.b], w=[r_.b])
                            act(r_, r_[:], r_[:], AF.Sqrt, [r_.b, cstf.b], bias=epsc, scale=1.0 / D)
                            P.op("dve", lambda e, r_=r_: e.reciprocal(out=r_[:], in_=r_[:]), r=[r_.b], w=[r_.b])
                            stt(y_, y_[:], y_[:], r_[:], G2p[:], ALU.mult, ALU.mult, [y_.b, r_.b, G2p.b])
                            tt(x_, x_[:], x_[:], y_[:], ALU.add, [x_.b, y_.b])
                            store("sp", x_, out[rows, :], x_[:])
                        rs = rms_rstd(sst[gt % 2], x_, x_[:], junk, [x_.b])
                        h_ = hb[gt % 2]
                        stt(y_, y_[:], x_[:], rs[:], A1[:], ALU.mult, ALU.mult, [x_.b, rs.b, A1.b])
                        tt(h_, h_[:], y_[:], B1[:], ALU.add, [y_.b, B1.b])
                        p_ = ptr[gt % 2]
                        for kc in range(16):
                            P.op("pe", lambda e, kc=kc: e.transpose(p_[:, kc * 128:(kc + 1) * 128], h_[:, kc * 128:(kc + 1) * 128], ident[:]), r=[h_.b, ident.b], w=[p_.b])
                        act(hT, hT[:, :, ti * 128:(ti + 1) * 128], p_[:].rearrange("p (k t) -> p k t", k=16), AF.Copy, [p_.b])
                    for mg in range(7):
                        wt = wb[mg % 2]
                        load("pool", wt, wt[:], w_in[l, :, mg * 512:(mg + 1) * 512].rearrange("(k p) n -> p k n", p=128))
                        if mg < 6:
                            nm = 3 if mg == 2 else 4
                            for tb in range(SBT // 512):
                                for m in range(nm):
                                    p_ = pa[npa % 4]
                                    npa += 1
                                    for kc in range(16):
                                        mm(p_, p_[:], wt[:, kc, m * 128:(m + 1) * 128], hT[:, kc, tb * 512:(tb + 1) * 512], kc == 0, kc == 15, [wt.b, hT.b])
                                    s_ = stg[nst % 3]
                                    nst += 1
                                    if nst % 2:
                                        act(s_, s_[:], p_[:], AF.Copy, [p_.b])
                                    else:
                                        cp(s_, s_[:], p_[:], [p_.b])
                                    row = mg * 512 + m * 128
                                    c0 = sbi * SBT + tb * 512
                                    store("sp", s_, projT[row:row + 128, c0:c0 + 512], s_[:])
                        else:
                            for ti in range(SBT // 128):
                                p_ = pa[npa % 4]
                                npa += 1
                                for kc in range(16):
                                    mm(p_, p_[:], hT[:, kc, ti * 128:(ti + 1) * 128], wt[:, kc, :], kc == 0, kc == 15, [wt.b, hT.b])
                                v_ = vst[nv % 2]
                                nv += 1
                                act(v_, v_[:], p_[:], AF.Copy, [p_.b])
                                t0 = sbi * SBT + ti * 128
                                store("sp", v_, vtok[t0:t0 + 128, :], v_[:])
                P.barrier()
            if stop == "A":
                continue
            phase_s5(l)
            if stop == "S5":
                continue
            phase_mla(l)
            if stop == "MLA":
                continue
            phase_hg(l)
            if stop == "B":
                continue
            phase_c(l)
        P.barrier(["sp"])
    return nc


def _consts():
    c = np.zeros((128, 1024), np.float32)
    c[:, 0:128] = np.eye(128, dtype=np.float32)
    perm = np.zeros((128, 128), np.float32)
    for i in range(64):
        perm[i, 64 + i] = 1.0
        perm[64 + i, i] = 1.0
    c[:, 128:256] = perm
    s = np.arange(64)
    c[0:64, 256:320] = (s[:, None] <= s[None, :]).astype(np.float32)
    c[0:64, 320] = 1.0
    c[64:128, 320] = -1.0
    invf = 1.0 / (10000.0 ** (np.arange(0, 64, 2, dtype=np.float32) / 64.0))
    c[0:32, 321] = invf
    c[32:64, 321] = invf
    c[0:32, 322] = -1.0
    c[32:64, 322] = 1.0
    c[:, 323] = EPS
    c[:, 384:448] = np.arange(64, dtype=np.float32)[None, :]
    k = np.arange(128)[:, None]
    q = np.arange(512)[None, :]
    mA = np.concatenate([(q >= j * 128 + k).astype(np.float32) for j in range(4)], axis=1)
    return c, mA


def prep_shared(inp, L):
    f = lambda a: np.ascontiguousarray(np.asarray(a, dtype=np.float32))
    w_in = f(inp["w_in"])
    wa = np.zeros((L, D, WIN), np.float32)
    wa[:, :, 0:1280] = w_in[:, :, 0:1280]
    wa[:, :, 1280:1344] = w_in[:, :, 1280:1344]
    wa[:, :, 1344:1376] = w_in[:, :, 1312:1344]
    wa[:, :, 1376:1408] = w_in[:, :, 1280:1312]
    wa[:, :, 1536:2048] = w_in[:, :, 1344:1856]
    wa[:, :, 2048:2560] = w_in[:, :, 1856:2368]
    wa[:, :, 2560:3072] = w_in[:, :, 2880:3392]
    wa[:, :, 3072:3584] = w_in[:, :, 2368:2880]
    wuq = f(inp["mla_w_uq"]).reshape(L, 512, 8, 192)
    wq = np.zeros((L, 512, 8, 256), np.float32)
    wq[..., 0:192] = wuq
    wq[..., 192:224] = wuq[..., 160:192]
    wq[..., 224:256] = wuq[..., 128:160]
    norms = np.stack([f(inp["mix_pre_norm"]), f(inp["mix_post_norm"]), f(inp["ffn_pre_norm"]), f(inp["ffn_post_norm"])], axis=1)
    pvec = np.zeros((128, L, 16), np.float32)
    pvec[:, :, 0:4] = f(inp["mla_q_norm"]).reshape(L, 4, 128).transpose(2, 0, 1)
    pvec[:, :, 4:6] = f(inp["mla_kv_norm"]).reshape(L, 2, 128).transpose(2, 0, 1)
    pvec[:, :, 6] = f(inp["hg_out_norm"]).T
    pvec[:, :, 7:11] = f(inp["s5_d"]).reshape(L, 4, 128).transpose(2, 0, 1)
    convp = np.zeros((128, L, 86, 4), np.float32)
    convp[:, :, :, 0:3] = f(inp["ffn_conv_w"]).reshape(L, 3, 86, 128).transpose(3, 0, 2, 1)
    convp[:, :, :, 3] = f(inp["ffn_conv_b"]).reshape(L, 86, 128).transpose(2, 0, 1)
    lre, lim, ldt = f(inp["s5_lambda_re"]), f(inp["s5_lambda_im"]), f(inp["s5_log_dt"])
    ldt_e = np.broadcast_to(ldt[:, :, None], (L, 32, 64))
    s5lam = np.stack([lre.reshape(L, 2048), lim.reshape(L, 2048), np.ascontiguousarray(ldt_e).reshape(L, 2048)], axis=1)
    s5lamp = np.zeros((128, L, 3, 32), np.float32)
    for i, a in enumerate((lre, lim, ldt_e)):
        t = np.asarray(a).transpose(2, 0, 1)
        s5lamp[0:64, :, i, :] = t
        s5lamp[64:128, :, i, :] = t
    s5b = np.zeros((L, 2, 128, 32, 64), np.float32)
    s5c = np.zeros((L, 2, 128, 32, 128), np.float32)
    bre, bim = f(inp["s5_b_re"]), f(inp["s5_b_im"])
    cre, cim = f(inp["s5_c_re"]), f(inp["s5_c_im"])
    for g in range(32):
        g8 = g % 8
        s5b[:, 0, g8 * 16:(g8 + 1) * 16, g, :] = bre[:, g].transpose(0, 2, 1)
        s5b[:, 1, g8 * 16:(g8 + 1) * 16, g, :] = bim[:, g].transpose(0, 2, 1)
        s5c[:, 0, 0:64, g, g8 * 16:(g8 + 1) * 16] = cre[:, g].transpose(0, 2, 1)
        s5c[:, 0, 64:128, g, g8 * 16:(g8 + 1) * 16] = cim[:, g].transpose(0, 2, 1)
        s5c[:, 1, 0:64, g, g8 * 16:(g8 + 1) * 16] = cim[:, g].transpose(0, 2, 1)
        s5c[:, 1, 64:128, g, g8 * 16:(g8 + 1) * 16] = cre[:, g].transpose(0, 2, 1)
    lbl = f(inp["hg_lb_logits"]).reshape(L, 4, 128).transpose(2, 1, 0)
    cst, mA = _consts()
    return {
        "w_in": wa, "w_out": f(inp["w_out"]), "w_up": f(inp["ffn_w_up"]), "w_down": f(inp["ffn_w_down"]),
        "w_ada": f(inp["w_ada"]), "b_ada": f(inp["b_ada"]), "w_uq": np.ascontiguousarray(wq.reshape(L, 512, 2048)),
        "w_ukv": f(inp["mla_w_ukv"]), "w_glu": f(inp["s5_w_glu"]), "norms": np.ascontiguousarray(norms),
        "pvec": pvec, "convp": convp, "s5lam": np.ascontiguousarray(s5lam), "s5lamp": s5lamp,
        "s5b": np.ascontiguousarray(s5b.reshape(L, 2, 128, 2048)), "s5c": np.ascontiguousarray(s5c.reshape(L, 2, 128, 4096)),
        "lbl": np.ascontiguousarray(lbl), "cst": cst, "maskA": mA,
    }


def prep_core(inp, b):
    x = np.ascontiguousarray(np.asarray(inp["x"][b], dtype=np.float32))
    pos = np.ascontiguousarray(np.asarray(inp["positions"][b], dtype=np.int32)).reshape(1, -1)
    cT = np.ascontiguousarray(np.asarray(inp["c"][b], dtype=np.float32).reshape(16, 128).T)
    return {"x": x, "pos": pos, "cT": cT}


def kernel(**inputs):
    B, T, _ = inputs["x"].shape
    L = inputs["w_in"].shape[0]
    nc = build(T, L)
    shared = prep_shared(inputs, L)
    in_maps = []
    for b in range(B):
        m = dict(shared)
        m.update(prep_core(inputs, b))
        in_maps.append(m)
    res = run_bass_kernel_spmd(nc, in_maps, core_ids=list(range(B)))
    return np.stack([np.asarray(r["out"], dtype=np.float32) for r in res.results], axis=0)
```
